# Optimizing a Trainium2 kernel written in Bass

```python
import jax
import jax.numpy as jnp
from jax import lax
import numpy as np

D_MODEL = 1024
BATCH = 16
SEQ = 2048
DEPTH = 2
DEC_BATCH = 128
DEC_SEQ = 4
PAST_LEN = 16384
PAGE_SIZE = 128

N_META = 16
D_MIX = D_MODEL
DK_A = 64
DV_A = 64
W_A = D_MIX // 4
H_A = W_A // DK_A
DH_B = 64
W_B = D_MIX // 4
H_B = W_B // DH_B
CONV_W = 4
HD_C = 64
H_C = (D_MIX // 2) // HD_C
KV_C = 2
G_C = H_C // KV_C
WINDOW = 128
BLOCK = 128
CHUNK = 64
D_FF = ((8 * D_MODEL // 3 + 255) // 256) * 256
D_IN = 4 * W_A + 4 * W_B + 2 * H_B + (H_C + 2 * KV_C) * HD_C
RMS_EPS = 1e-6
NEG_BIG = -1e30
F_FLOOR = 1e-30
N_STATE = 7

kernel_name = 'hymba_hgrn2_mlstm_swa_sink_step'


def _rmsnorm(x, g):
    xf = x.astype(jnp.float32)
    y = xf * lax.rsqrt(jnp.mean(xf * xf, axis=-1, keepdims=True) + RMS_EPS)
    return y * g.astype(jnp.float32)


def _head_rmsnorm(x, g):
    h, d = x.shape[-2:]
    return _rmsnorm(x, g.reshape(h, d))


def _split_points():
    sizes = (W_A, W_A, W_A, W_A, W_B, W_B, W_B, W_B, H_B, H_B,
             H_C * HD_C, KV_C * HD_C, KV_C * HD_C)
    return [int(s) for s in np.cumsum(sizes)[:-1]]


def _pad_front(a, pad, value=0.0):
    widths = [(0, 0), (pad, 0)] + [(0, 0)] * (a.ndim - 2)
    return jnp.pad(a, widths, constant_values=value)


def _to_chunks(a, L):
    b, t = a.shape[:2]
    a = a.reshape((b, t // L, L) + a.shape[2:])
    perm = (1, 0, 3, 2) + tuple(range(4, a.ndim))
    return jnp.transpose(a, perm)


def _from_chunks(o):
    n, b, h, L, d = o.shape
    return jnp.transpose(o, (1, 0, 3, 2, 4)).reshape(b, n * L, h, d)


def _hgrn2_chunked(q, k, v, logf, s0):
    T = q.shape[1]
    L = min(CHUNK, T)
    pad = (-T) % L
    q, k, v, logf = (_pad_front(a.astype(jnp.float32), pad) for a in (q, k, v, logf))
    mask = jnp.tril(jnp.ones((L, L), dtype=bool))[:, :, None]

    def step(S, xs):
        qc, kc, vc, gc = xs
        b = jnp.cumsum(gc, axis=2)
        o_inter = jnp.einsum('bhtd,bhde->bhte', qc * jnp.exp(b), S)
        diff = b[:, :, :, None, :] - b[:, :, None, :, :]
        decay = jnp.where(mask, jnp.exp(jnp.where(mask, diff, 0.0)), 0.0)
        att = jnp.einsum('bhtd,bhsd,bhtsd->bhts', qc, kc, decay)
        o = o_inter + jnp.einsum('bhts,bhse->bhte', att, vc)
        b_end = b[:, :, -1:, :]
        S_new = (jnp.exp(b_end[:, :, 0, :])[..., None] * S
                 + jnp.einsum('bhsd,bhse->bhde', kc * jnp.exp(b_end - b), vc))
        return S_new, o

    xs = tuple(_to_chunks(a, L) for a in (q, k, v, logf))
    s_end, o = lax.scan(step, s0.astype(jnp.float32), xs)
    return _from_chunks(o)[:, pad:], s_end


def _mlstm_chunked(q, k, v, ig, lf, c0, n0, m0):
    T = q.shape[1]
    L = min(CHUNK, T)
    pad = (-T) % L
    q, k, v, lf = (_pad_front(a.astype(jnp.float32), pad) for a in (q, k, v, lf))
    ig = _pad_front(ig.astype(jnp.float32), pad, NEG_BIG)
    mask = jnp.tril(jnp.ones((L, L), dtype=bool))

    def step(carry, xs):
        C, nv, m = carry
        qc, kc, vc, ic, fc = xs
        b = jnp.cumsum(fc, axis=-1)
        mt = b + jnp.maximum(m[..., None], lax.cummax(ic - b, axis=2))
        inter = jnp.exp(b + m[..., None] - mt)
        logd = b[..., :, None] - b[..., None, :] + ic[..., None, :] - mt[..., :, None]
        dmat = jnp.where(mask, jnp.exp(jnp.where(mask, logd, NEG_BIG)), 0.0)
        sc = jnp.einsum('bhtd,bhsd->bhts', qc, kc) * dmat
        num = (inter[..., None] * jnp.einsum('bhed,bhtd->bhte', C, qc)
               + jnp.einsum('bhts,bhse->bhte', sc, vc))
        den = inter * jnp.einsum('bhd,bhtd->bht', nv, qc) + jnp.sum(sc, axis=-1)
        h = num / jnp.maximum(jnp.abs(den), jnp.exp(-mt))[..., None]
        m_end = mt[..., -1]
        w_end = jnp.exp(b[..., -1:] - b + ic - m_end[..., None])
        keep = jnp.exp(b[..., -1] + m - m_end)
        C_new = keep[..., None, None] * C + jnp.einsum('bhs,bhse,bhsd->bhed', w_end, vc, kc)
        n_new = keep[..., None] * nv + jnp.einsum('bhs,bhsd->bhd', w_end, kc)
        return (C_new, n_new, m_end), h

    xs = tuple(_to_chunks(a, L) for a in (q, k, v, ig, lf))
    init = (c0.astype(jnp.float32), n0.astype(jnp.float32), m0.astype(jnp.float32))
    (c_end, n_end, m_end), h = lax.scan(step, init, xs)
    return _from_chunks(h)[:, pad:], c_end, n_end, m_end


def _causal_conv(x, buf, w, bias):
    T = x.shape[1]
    xp = jnp.concatenate([buf.astype(x.dtype), x], axis=1)
    y = bias + sum(xp[:, j:j + T] * w[j] for j in range(CONV_W))
    return y, xp[:, -(CONV_W - 1):]


def _sink_attention(q, k, v, mask, sink):
    s = jnp.einsum('bnqhgd,bnshd->bnhgqs', q.astype(jnp.float32), k.astype(jnp.float32)) * (HD_C ** -0.5)
    s = jnp.where(mask[None, :, None, None], s, NEG_BIG)
    sk = sink.astype(jnp.float32)[None, None, :, :, None, None]
    mx = jnp.maximum(jnp.max(s, axis=-1, keepdims=True), sk)
    p = jnp.exp(s - mx)
    den = jnp.sum(p, axis=-1, keepdims=True) + jnp.exp(sk - mx)
    return jnp.einsum('bnhgqs,bnshd->bnqhgd', p / den, v.astype(jnp.float32))


def _swa_prompt(q, k, v, sink):
    B, T = q.shape[:2]
    nb = -(-T // BLOCK)
    pad = nb * BLOCK - T
    q, k, v = (jnp.pad(a, [(0, 0), (0, pad), (0, 0), (0, 0)]) for a in (q, k, v))
    qb = q.reshape(B, nb, BLOCK, KV_C, G_C, HD_C)
    kb = k.reshape(B, nb, BLOCK, KV_C, HD_C)
    vb = v.reshape(B, nb, BLOCK, KV_C, HD_C)

    def with_prev(a):
        prev = jnp.concatenate([jnp.zeros_like(a[:, :1]), a[:, :-1]], axis=1)
        return jnp.concatenate([prev, a], axis=2)

    qpos = jnp.arange(nb)[:, None] * BLOCK + jnp.arange(BLOCK)[None, :]
    kpos = jnp.arange(nb)[:, None] * BLOCK - BLOCK + jnp.arange(2 * BLOCK)[None, :]
    diff = qpos[:, :, None] - kpos[:, None, :]
    mask = (kpos[:, None, :] >= 0) & (diff >= 0) & (diff < WINDOW)
    o = _sink_attention(qb, with_prev(kb), with_prev(vb), mask, sink)
    return o.reshape(B, nb * BLOCK, H_C * HD_C)[:, :T]


def _swa_sample(q, k, v, k_buf, v_buf, sink):
    B, T = q.shape[:2]
    kc = jnp.concatenate([k_buf.astype(k.dtype), k], axis=1)
    vc = jnp.concatenate([v_buf.astype(v.dtype), v], axis=1)
    qpos = jnp.arange(T) + WINDOW
    kpos = jnp.arange(WINDOW + T)
    diff = qpos[:, None] - kpos[None, :]
    mask = ((diff >= 0) & (diff < WINDOW))[None]
    o = _sink_attention(q.reshape(B, 1, T, KV_C, G_C, HD_C), kc[:, None], vc[:, None], mask, sink)
    return o.reshape(B, T, H_C * HD_C), kc[:, -WINDOW:], vc[:, -WINDOW:]


def _lower_bounds(lb_logits):
    sm = jax.nn.softmax(lb_logits.astype(jnp.float32), axis=0)
    return jnp.cumsum(sm, axis=0) - sm[0:1]


def _mixer(hn, p, st, past_kv):
    w_in, lb, hgrn_norm, conv_w, conv_b, i_bias, f_bias, mlstm_norm, sink, w_out = p
    s_hgrn, c0, n0, m0, conv_buf = st
    B, T, _ = hn.shape
    proj = hn @ w_in
    aq, af, ai, ag, bq, bk, bv, bo, big, bfg, cq, ck, cv = jnp.split(proj, _split_points(), axis=-1)

    af32 = af.astype(jnp.float32)
    f = lb + (1.0 - lb) * jax.nn.sigmoid(af32)
    logf = jnp.log(jnp.maximum(f, F_FLOOR)).reshape(B, T, H_A, DK_A)
    ka = ((1.0 - lb) * jax.nn.sigmoid(-af32)).reshape(B, T, H_A, DK_A)
    qa = jax.nn.silu(aq).reshape(B, T, H_A, DK_A)
    oa, s_hgrn_new = _hgrn2_chunked(qa, ka, ai.reshape(B, T, H_A, DV_A), logf, s_hgrn)
    ya = _head_rmsnorm(oa, hgrn_norm).reshape(B, T, W_A) * jax.nn.silu(ag)

    qk, conv_new = _causal_conv(jnp.concatenate([bq, bk], axis=-1), conv_buf, conv_w, conv_b)
    qk = jax.nn.silu(qk)
    qm = qk[..., :W_B].reshape(B, T, H_B, DH_B)
    km = qk[..., W_B:].reshape(B, T, H_B, DH_B) * (DH_B ** -0.5)
    vm = bv.reshape(B, T, H_B, DH_B)
    igate = (big + i_bias).astype(jnp.float32)
    lfgate = jax.nn.log_sigmoid((bfg + f_bias).astype(jnp.float32))
    hm, c_new, n_new, m_new = _mlstm_chunked(qm, km, vm, igate, lfgate, c0, n0, m0)
    yb = _head_rmsnorm(hm, mlstm_norm).reshape(B, T, W_B) * jax.nn.sigmoid(bo)

    qc = cq.reshape(B, T, H_C, HD_C)
    kc = ck.reshape(B, T, KV_C, HD_C)
    vc = cv.reshape(B, T, KV_C, HD_C)
    sk = sink.reshape(KV_C, G_C)
    if past_kv is None:
        yc = _swa_prompt(qc, kc, vc, sk)
        k_new, v_new = kc[:, -WINDOW:], vc[:, -WINDOW:]
    else:
        yc, k_new, v_new = _swa_sample(qc, kc, vc, past_kv[0], past_kv[1], sk)

    y = jnp.concatenate([ya, yb, yc], axis=-1) @ w_out
    return y, (s_hgrn_new, c_new, n_new, m_new, conv_new, k_new, v_new)


def _swiglu(hn, w_ffn_in, w_ffn_out):
    g, u = jnp.split(hn @ w_ffn_in, 2, axis=-1)
    return (jax.nn.silu(g) * u) @ w_ffn_out


def _trunk(h, init_states, past_kvs, params):
    (norm_mix, norm_ffn, norm_final, w_in, lbs, hgrn_norm, conv_w, conv_b,
     i_bias, f_bias, mlstm_norm, sinks, w_out, w_ffn_in, w_ffn_out) = params
    new_states = []
    for l in range(DEPTH):
        p = (w_in[l], lbs[l], hgrn_norm[l], conv_w[l], conv_b[l], i_bias[l], f_bias[l],
             mlstm_norm[l], sinks[l], w_out[l])
        kv = None if past_kvs is None else past_kvs[l]
        y, st = _mixer(_rmsnorm(h, norm_mix[l]), p, init_states[l], kv)
        h = h + y
        h = h + _swiglu(_rmsnorm(h, norm_ffn[l]), w_ffn_in[l], w_ffn_out[l])
        new_states.append(st)
    stacked = tuple(jnp.stack([s[i] for s in new_states], axis=0) for i in range(N_STATE))
    return _rmsnorm(h, norm_final), stacked


def setup_inputs(seed: int = 0) -> dict:
    key = jax.random.key(seed)
    ks = jax.random.split(key, 26)

    def nrm(k, shape, scale):
        return scale * jax.random.normal(k, shape, jnp.float32)

    return {
        'x_prompt': nrm(ks[0], (BATCH, SEQ, D_MODEL), 1.0),
        'x_sample': nrm(ks[1], (DEC_BATCH, DEC_SEQ, D_MODEL), 1.0),
        'state_hgrn': nrm(ks[2], (DEPTH, DEC_BATCH, H_A, DK_A, DV_A), 0.5),
        'state_mlstm_c': nrm(ks[3], (DEPTH, DEC_BATCH, H_B, DH_B, DH_B), 0.1),
        'state_mlstm_n': nrm(ks[4], (DEPTH, DEC_BATCH, H_B, DH_B), 0.1),
        'state_mlstm_m': jax.random.uniform(ks[5], (DEPTH, DEC_BATCH, H_B), jnp.float32, 0.0, 4.0),
        'state_mlstm_conv': nrm(ks[6], (DEPTH, DEC_BATCH, CONV_W - 1, 2 * W_B), 1.0),
        'cache_swa_k': nrm(ks[7], (DEPTH, DEC_BATCH, WINDOW, KV_C, HD_C), 1.0),
        'cache_swa_v': nrm(ks[8], (DEPTH, DEC_BATCH, WINDOW, KV_C, HD_C), 1.0),
        'meta_tokens': nrm(ks[9], (N_META, D_MODEL), 1.0),
        'norm_mix': 1.0 + nrm(ks[10], (DEPTH, D_MODEL), 0.02),
        'norm_ffn': 1.0 + nrm(ks[11], (DEPTH, D_MODEL), 0.02),
        'norm_final': 1.0 + nrm(ks[12], (D_MODEL,), 0.02),
        'w_in': nrm(ks[13], (DEPTH, D_MODEL, D_IN), D_MODEL ** -0.5),
        'hgrn_lb_logits': nrm(ks[14], (DEPTH, W_A), 1.0),
        'hgrn_norm': 1.0 + nrm(ks[15], (DEPTH, W_A), 0.02),
        'mlstm_conv_w': nrm(ks[16], (DEPTH, CONV_W, 2 * W_B), CONV_W ** -0.5),
        'mlstm_conv_b': nrm(ks[17], (DEPTH, 2 * W_B), 0.02),
        'mlstm_i_bias': nrm(ks[18], (DEPTH, H_B), 0.1),
        'mlstm_f_bias': jnp.linspace(3.0, 6.0, H_B, dtype=jnp.float32)[None, :] + nrm(ks[19], (DEPTH, H_B), 0.1),
        'mlstm_norm': 1.0 + nrm(ks[20], (DEPTH, W_B), 0.02),
        'attn_sinks': nrm(ks[21], (DEPTH, H_C), 0.5),
        'w_out': nrm(ks[22], (DEPTH, D_MIX, D_MODEL), D_MIX ** -0.5),
        'w_ffn_in': nrm(ks[23], (DEPTH, D_MODEL, 2 * D_FF), D_MODEL ** -0.5),
        'w_ffn_out': nrm(ks[24], (DEPTH, D_FF, D_MODEL), D_FF ** -0.5),
    }


def reference(x_prompt, x_sample, state_hgrn, state_mlstm_c, state_mlstm_n, state_mlstm_m,
              state_mlstm_conv, cache_swa_k, cache_swa_v, meta_tokens, norm_mix, norm_ffn,
              norm_final, w_in, hgrn_lb_logits, hgrn_norm, mlstm_conv_w, mlstm_conv_b,
              mlstm_i_bias, mlstm_f_bias, mlstm_norm, attn_sinks, w_out, w_ffn_in, w_ffn_out):
    lbs = _lower_bounds(hgrn_lb_logits)
    params = (norm_mix, norm_ffn, norm_final, w_in, lbs, hgrn_norm, mlstm_conv_w, mlstm_conv_b,
              mlstm_i_bias, mlstm_f_bias, mlstm_norm, attn_sinks, w_out, w_ffn_in, w_ffn_out)

    bp = x_prompt.shape[0]
    meta = jnp.broadcast_to(meta_tokens[None].astype(x_prompt.dtype), (bp, N_META, D_MODEL))
    xp = jnp.concatenate([meta, x_prompt], axis=1)
    init_p = [(jnp.zeros((bp, H_A, DK_A, DV_A), jnp.float32),
               jnp.zeros((bp, H_B, DH_B, DH_B), jnp.float32),
               jnp.zeros((bp, H_B, DH_B), jnp.float32),
               jnp.zeros((bp, H_B), jnp.float32),
               jnp.zeros((bp, CONV_W - 1, 2 * W_B), xp.dtype)) for _ in range(DEPTH)]
    yp_full, st_p = _trunk(xp, init_p, None, params)
    y_prompt = yp_full[:, N_META:]

    init_s = [(state_hgrn[l], state_mlstm_c[l], state_mlstm_n[l], state_mlstm_m[l], state_mlstm_conv[l])
              for l in range(DEPTH)]
    kv_s = [(cache_swa_k[l], cache_swa_v[l]) for l in range(DEPTH)]
    y_sample, st_s = _trunk(x_sample, init_s, kv_s, params)

    hgrn_p, mlstm_c_p, mlstm_n_p, mlstm_m_p, mlstm_conv_p, swa_k_p, swa_v_p = st_p
    hgrn_s, mlstm_c_s, mlstm_n_s, mlstm_m_s, mlstm_conv_s, swa_k_s, swa_v_s = st_s
    return (y_prompt, y_sample,
            hgrn_p, mlstm_c_p, mlstm_n_p, mlstm_m_p, mlstm_conv_p, swa_k_p, swa_v_p,
            hgrn_s, mlstm_c_s, mlstm_n_s, mlstm_m_s, mlstm_conv_s, swa_k_s, swa_v_s)
```

```python
import numpy as np
from contextlib import ExitStack
import concourse.bass as bass
import concourse.mybir as mybir
from concourse.bass_utils import run_bass_kernel_spmd

F32 = mybir.dt.float32
BF16 = mybir.dt.bfloat16
ALU = mybir.AluOpType
AF = mybir.ActivationFunctionType

D = 1024
DIN = 2824
DFF = 2816
L = 2
NCORES = 8
EPS = 1e-6


class Res:
    __slots__ = ("lw", "rd", "name")

    def __init__(self, name):
        self.lw = None
        self.rd = {}
        self.name = name


class A:
    __slots__ = ("ap", "res", "p0", "bank")

    def __init__(self, ap, res, p0=0):
        self.ap = ap
        self.res = res
        self.p0 = p0
        self.bank = None


class Chan:
    def __init__(self, sem, name):
        self.sem = sem
        self.val = 0
        self.name = name


class TT:
    def __init__(self, kb, name, F, dt, psum=False):
        self.kb = kb
        name = name + kb.tag
        self.name = name
        self.F = F
        self.dt = dt
        if psum:
            self.t = kb.es.enter_context(kb.nc.psum_tensor(name, [128, F], dt))
        else:
            self.t = kb.es.enter_context(kb.nc.sbuf_tensor(name, [128, F], dt))
        self.res = {}

    def r(self, key=0):
        if key not in self.res:
            self.res[key] = Res(f"{self.name}:{key}")
        return self.res[key]

    def a(self, off=0, dims=None, p0=0, pn=128, key=0):
        if dims is None:
            dims = [[1, self.F - off]]
        ap = bass.AP(self.t, p0 * self.F + off, [[self.F, pn]] + [list(d) for d in dims])
        if isinstance(key, tuple):
            return A(ap, tuple(self.r(k_) for k_ in key), p0)
        return A(ap, self.r(key), p0)


class Bank:
    def __init__(self, kb, name):
        self.t = kb.es.enter_context(kb.nc.psum_tensor(name, [128, 512], F32))
        self.tb = self.t.bitcast(BF16)
        self.res = Res(name)
        self.res_pe = None

    def a(self, off=0, dims=None, p0=0, pn=128, key=0, bf=False):
        F = 1024 if bf else 512
        if dims is None:
            dims = [[1, F - off]]
        ap = bass.AP(self.tb if bf else self.t, p0 * F + off, [[F, pn]] + [list(d) for d in dims])
        a = A(ap, self.res, p0)
        a.bank = self
        return a


class Arena:
    PAGE = 1024

    def __init__(self, kb, name, nbytes):
        nbytes = -(-nbytes // self.PAGE) * self.PAGE
        self.nbytes = nbytes
        self.t32 = kb.es.enter_context(kb.nc.sbuf_tensor(name + kb.tag, [128, nbytes // 4], F32))
        self.tb = self.t32.bitcast(BF16)
        self.pages = [Res(f"{name}:pg{i}") for i in range(nbytes // self.PAGE)]

    def view(self, boff, F, dt):
        return AV(self, boff, F, dt)


class AV:
    def __init__(self, ar, boff, F, dt):
        self.ar, self.boff, self.F, self.dt = ar, boff, F, dt
        self.es = 4 if dt == F32 else 2
        assert boff % self.es == 0
        self.h = ar.t32 if dt == F32 else ar.tb
        self.ps = ar.nbytes // self.es

    def a(self, off=0, dims=None, p0=0, pn=128, key=0):
        if dims is None:
            dims = [[1, self.F - off]]
        base = self.boff // self.es + off
        ap = bass.AP(self.h, p0 * self.ps + base, [[self.ps, pn]] + [list(d) for d in dims])
        span = sum((c - 1) * abs(st) for st, c in dims)
        b0 = base * self.es
        b1 = (base + span + 1) * self.es - 1
        pg = self.ar.pages[b0 // Arena.PAGE: b1 // Arena.PAGE + 1]
        return A(ap, tuple(pg), p0)


class KB:
    ENG = ("pe", "act", "dve", "pool")

    def __init__(self, needed=None):
        self.nc = bass.Bass("TRN2", target_bir_lowering=False)
        self.es = ExitStack()
        nc = self.nc
        self.eng = {"pe": nc.tensor, "act": nc.scalar, "dve": nc.vector, "pool": nc.gpsimd, "sp": nc.sync}
        self.sem = {e: self.es.enter_context(nc.semaphore("s_" + e)) for e in self.ENG}
        self.idx = {e: 0 for e in self.ENG}
        self.needed = needed
        if needed is not None:
            self.rank = {e: {i: r for r, i in enumerate(sorted(needed[e]))} for e in self.ENG}
        self.rec = {e: set() for e in self.ENG}
        self.waited = {e: {} for e in ("pe", "act", "dve", "pool", "sp")}
        self.chans = []
        self.tag = ""
        self.lab = ""
        self.labels = {e: [] for e in self.ENG}
        self.dbg = []
        self.ninst = 0

    def chan(self, name):
        c = Chan(self.es.enter_context(self.nc.semaphore("c_" + name)), name)
        self.chans.append(c)
        return c

    def _wait(self, eng, src, i):
        if isinstance(src, Chan):
            val, sem = i, src.sem
        else:
            self.rec[src].add(i)
            val = (i + 1) if self.needed is None else self.rank[src][i] + 1
            sem = self.sem[src]
        w = self.waited[eng]
        if w.get(src, 0) >= val:
            return
        w[src] = val
        self.eng[eng].wait_ge(sem, val)

    def _deps(self, eng, reads, writes, skip=None):
        for r in reads:
            if r.lw is not None:
                src, i = r.lw
                if (src == eng and eng == "pe") or src is skip:
                    continue
                self._wait(eng, src, i)
        for w in writes:
            if w.lw is not None:
                src, i = w.lw
                if (src != eng or eng != "pe") and src is not skip:
                    self._wait(eng, src, i)
            for src, i in w.rd.items():
                if (src != eng or eng != "pe") and src is not skip:
                    self._wait(eng, src, i)

    @staticmethod
    def _flat(xs):
        out = []
        for x in xs:
            if isinstance(x, A):
                x = x.res
            if isinstance(x, Res):
                out.append(x)
            elif isinstance(x, (tuple, list)):
                out.extend(x)
        return out

    def emit(self, eng, fn, reads, writes):
        writes = list(writes) + [x for x in reads if isinstance(x, A) and x.bank is not None]
        reads = self._flat(reads)
        writes = self._flat(writes)
        self._deps(eng, reads, writes)
        ins = fn()
        i = self.idx[eng]
        self.idx[eng] = i + 1
        self.labels[eng].append(self.lab)
        self.ninst += 1
        if self.needed is None or i in self.rank[eng]:
            ins.then_inc(self.sem[eng], 1)
        for r in reads:
            r.rd[eng] = i
        for w in writes:
            w.lw = (eng, i)
            w.rd = {}
        return (eng, i)

    def dma(self, q, out, in_, ch, **kw):
        reads = self._flat([in_]) if isinstance(in_, A) else []
        writes = self._flat([out]) if isinstance(out, A) else []
        self._deps(q, reads, writes, skip=ch)
        o = out.ap if isinstance(out, A) else out
        i_ = in_.ap if isinstance(in_, A) else in_
        self.eng[q].dma_start(out=o, in_=i_, **kw).then_inc(ch.sem, 16)
        ch.val += 16
        self.ninst += 1
        for r in reads:
            r.rd[ch] = ch.val
        for w in writes:
            w.lw = (ch, ch.val)
            w.rd = {}
        return reads + writes

    def batch_begin(self, q, ch):
        if ch.val:
            self._wait(q, ch, ch.val)
        self._batch = []

    def batch_add(self, touched):
        self._batch += touched

    def batch_end(self, ch):
        for r in self._batch:
            if r.lw is not None and r.lw[0] is ch:
                r.lw = (ch, ch.val)
            if ch in r.rd:
                r.rd[ch] = ch.val

    def barrier(self):
        last = {e: self.idx[e] - 1 for e in self.ENG}
        for e in ("pe", "act", "dve", "pool", "sp"):
            for s in self.ENG:
                if s != e and last[s] >= 0:
                    self._wait(e, s, last[s])
            for c in self.chans:
                if c.val:
                    self._wait(e, c, c.val)

    def _rowsync(self, out, kbase):
        b = out.bank
        if b is not None and b.res_pe is not None and b.res_pe[0] != kbase:
            self._wait("pe", "pe", b.res_pe[1])

    def mm(self, out, lhsT, rhs, start=True, stop=True, sgc=False):
        self._rowsync(out, lhsT.p0)
        t = self.emit("pe", lambda: self.nc.tensor.matmul(out.ap, lhsT=lhsT.ap, rhs=rhs.ap, start=start, stop=stop,
                                                          skip_group_check=sgc), [lhsT, rhs], [out])
        if out.bank is not None:
            out.bank.res_pe = (lhsT.p0, t[1])
        return t

    def tr(self, out, in_, ident):
        self._rowsync(out, in_.p0)
        t = self.emit("pe", lambda: self.nc.tensor.transpose(out.ap, in_.ap, ident.ap), [in_, ident], [out])
        if out.bank is not None:
            out.bank.res_pe = (in_.p0, t[1])
        return t

    def trx(self, out, in_, ident, p):
        if p == 0:
            return self.tr(out, in_, ident)
        return self.mm(out, in_, ident)

    def act(self, out, in_, func, bias=None, scale=1.0):
        kw = {}
        if bias is not None:
            kw["bias"] = bias.ap if isinstance(bias, A) else bias
        kw["scale"] = scale.ap if isinstance(scale, A) else scale
        return self.emit("act", lambda: self.nc.scalar.activation(out=out.ap, in_=in_.ap, func=func, **kw),
                         [in_, bias, scale], [out])

    def tt(self, out, in0, in1, op, eng="dve"):
        return self.emit(eng, lambda: self.eng[eng].tensor_tensor(out=out.ap, in0=in0.ap, in1=in1.ap, op=op),
                         [in0, in1], [out])

    def ts(self, out, in0, s1, s2, op0, op1=None, eng="dve"):
        a1 = s1.ap if isinstance(s1, A) else s1
        a2 = s2.ap if isinstance(s2, A) else s2
        if op1 is None:
            f = lambda: self.eng[eng].tensor_scalar(out=out.ap, in0=in0.ap, scalar1=a1, scalar2=None, op0=op0)
        else:
            f = lambda: self.eng[eng].tensor_scalar(out=out.ap, in0=in0.ap, scalar1=a1, scalar2=a2, op0=op0, op1=op1)
        return self.emit(eng, f, [in0, s1, s2], [out])

    def stt(self, out, in0, sc, in1, op0, op1, eng="dve"):
        a = sc.ap if isinstance(sc, A) else sc
        return self.emit(eng, lambda: self.eng[eng].scalar_tensor_tensor(out=out.ap, in0=in0.ap, scalar=a, in1=in1.ap,
                                                                          op0=op0, op1=op1), [in0, sc, in1], [out])

    def cp(self, out, in_, eng="dve"):
        return self.emit(eng, lambda: self.eng[eng].tensor_copy(out=out.ap, in_=in_.ap), [in_], [out])

    def ms(self, out, val, eng="dve"):
        return self.emit(eng, lambda: self.eng[eng].memset(out.ap, val), [], [out])

    def scan(self, out, d0, d1, init, op0, op1):
        a = init.ap if isinstance(init, A) else init
        return self.emit("dve", lambda: self.nc.vector.tensor_tensor_scan(out=out.ap, data0=d0.ap, data1=d1.ap, initial=a,
                                                                          op0=op0, op1=op1), [d0, d1, init], [out])

    def recip(self, out, in_):
        return self.emit("dve", lambda: self.nc.vector.reciprocal(out=out.ap, in_=in_.ap), [in_], [out])

    def asel(self, out, in_, pattern, op, fill, base, cm):
        return self.emit("pool", lambda: self.nc.gpsimd.affine_select(out=out.ap, in_=in_.ap, pattern=pattern, compare_op=op,
                                                                       fill=fill, base=base, channel_multiplier=cm),
                         [in_], [out])


class Grp:
    def __init__(self, kind, N, Lc, tiles, seq=0, gi=0, last=False):
        self.kind, self.N, self.Lc, self.tiles, self.seq, self.gi, self.last = kind, N, Lc, tiles, seq, gi, last
        self.nch = N // Lc


def dram_ap(t, off, dims):
    return bass.AP(t, off, [list(d) for d in dims])


CONV_ENG = "dve"
S1_ENG = "pool"


def build(needed=None, NSEQ=2, NGRP=4, NSAMP=16, debug=False):
    kb = KB(needed)
    nc = kb.nc
    SEQ = NGRP * 512
    NMAX = 512

    def din(name, shape):
        return nc.dram_tensor(name, list(shape), F32, kind="ExternalInput")

    def dout(name, shape):
        return nc.dram_tensor(name, list(shape), F32, kind="ExternalOutput")

    x_prompt = din("x_prompt", [NSEQ, SEQ, D])
    x_sample = din("x_sample", [NSAMP, 4, D])
    st_hgrn = din("state_hgrn", [L, NSAMP, 4, 64, 64])
    st_c = din("state_mlstm_c", [L, NSAMP, 4, 64, 64])
    st_n = din("state_mlstm_n", [L, NSAMP, 4, 64])
    st_m = din("state_mlstm_m", [L, NSAMP, 4])
    st_conv = din("state_mlstm_conv", [L, NSAMP, 3, 512])
    c_k = din("cache_swa_k", [L, NSAMP, 128, 2, 64])
    c_v = din("cache_swa_v", [L, NSAMP, 128, 2, 64])
    meta_tokens = din("meta_tokens", [16, D])
    norm_mix = din("norm_mix", [L, D])
    norm_ffn = din("norm_ffn", [L, D])
    norm_final = din("norm_final", [D])
    w_in = din("w_in", [L, D, DIN])
    lb_logits = din("hgrn_lb_logits", [L, 256])
    hgrn_norm = din("hgrn_norm", [L, 256])
    conv_w = din("mlstm_conv_w", [L, 4, 512])
    conv_b = din("mlstm_conv_b", [L, 512])
    i_bias = din("mlstm_i_bias", [L, 4])
    f_bias = din("mlstm_f_bias", [L, 4])
    mlstm_norm = din("mlstm_norm", [L, 256])
    sinks = din("attn_sinks", [L, 8])
    w_out = din("w_out", [L, D, D])
    w_ffn_in = din("w_ffn_in", [L, D, 2 * DFF])
    w_ffn_out = din("w_ffn_out", [L, DFF, D])

    y_prompt = dout("y_prompt", [NSEQ, SEQ, D])
    y_sample = dout("y_sample", [NSAMP, 4, D])
    o_hgrn_p = dout("hgrn_p", [L, NSEQ, 4, 64, 64])
    o_c_p = dout("mlstm_c_p", [L, NSEQ, 4, 64, 64])
    o_n_p = dout("mlstm_n_p", [L, NSEQ, 4, 64])
    o_m_p = dout("mlstm_m_p", [L, NSEQ, 4])
    o_conv_p = dout("mlstm_conv_p", [L, NSEQ, 3, 512])
    o_k_p = dout("swa_k_p", [L, NSEQ, 128, 2, 64])
    o_v_p = dout("swa_v_p", [L, NSEQ, 128, 2, 64])
    o_hgrn_s = dout("hgrn_s", [L, NSAMP, 4, 64, 64])
    o_c_s = dout("mlstm_c_s", [L, NSAMP, 4, 64, 64])
    o_n_s = dout("mlstm_n_s", [L, NSAMP, 4, 64])
    o_m_s = dout("mlstm_m_s", [L, NSAMP, 4])
    o_conv_s = dout("mlstm_conv_s", [L, NSAMP, 3, 512])
    o_k_s = dout("swa_k_s", [L, NSAMP, 128, 2, 64])
    o_v_s = dout("swa_v_s", [L, NSAMP, 128, 2, 64])

    def sb(name, F, dt=F32):
        return TT(kb, name, F, dt)

    ident_f = sb("ident_f", 128)
    ident_b = sb("ident_b", 128, BF16)
    ones_b = sb("ones_b", 128, BF16)
    ones_f = sb("ones_f", 128)
    bones_b = sb("bones_b", 128, BF16)
    maskA = sb("maskA", 128, BF16)
    maskC = sb("maskC", 128, BF16)
    maskP = sb("maskP", 128, BF16)
    maskM = sb("maskM", 128, BF16)
    sel = sb("sel", 256)
    epsT = sb("epsT", 1)
    zerosT = sb("zerosT", NMAX)
    scr = sb("scr", 256)
    gmix = sb("gmix", L * 8)
    gffn = sb("gffn", L * 8)
    gfin = sb("gfin", 8)
    lbT = sb("lbT", L * 2)
    omlT = sb("omlT", L * 2)
    nomlT = sb("nomlT", L * 2)
    lgT = sb("lgT", L * 2)
    hnormT = sb("hnormT", L * 2)
    mnormT = sb("mnormT", L * 2)
    cwT = sb("cwT", L * 16)
    cbT = sb("cbT", L * 4)
    ibT = sb("ibT", L)
    fbT = sb("fbT", L)
    skrow = sb("skrow", 16)
    esink = sb("esink", L * 4)

    WIN = [sb(f"win{i}", 8 * 512, BF16) for i in range(2)]
    WOUT = sb("woutb", 8 * 1024, BF16)
    NFIN, NFOUT = 3, 3
    FIN = [sb(f"fin{i}", 2 * 8 * 256, BF16) for i in range(NFIN)]
    FOUT = [sb(f"fout{i}", 22 * 128, BF16) for i in range(NFOUT)]
    ch_win = [kb.chan(f"win{i}") for i in range(2)]
    ch_wout = kb.chan("wout")
    ch_fin = [kb.chan(f"fin{i}") for i in range(NFIN)]
    ch_fout = [kb.chan(f"fout{i}") for i in range(NFOUT)]
    ch_par = kb.chan("par")
    ch_x = [kb.chan("x0"), kb.chan("x1")]
    ch_y = [kb.chan("y0"), kb.chan("y1")]
    ch_st = kb.chan("stout")
    ch_ld = kb.chan("stld")
    ch_dd = kb.chan("d2d")

    def stset(tag):
        return dict(
            SH=[[sb(f"SH{tag}{l}{j}", 64) for j in range(2)] for l in range(L)],
            SM=[[sb(f"SM{tag}{l}{j}", 128) for j in range(2)] for l in range(L)],
            mst=[sb(f"mst{tag}{l}", 1) for l in range(L)],
            cvh=[sb(f"cvh{tag}{l}", 12) for l in range(L)],
            sK=[sb(f"sK{tag}{l}", 128, BF16) for l in range(L)],
            sV=[sb(f"sV{tag}{l}", 128, BF16) for l in range(L)],
        )

    STM = stset("M")
    STW = stset("W")

    PB = [Bank(kb, f"pb{i}") for i in range(8)]

    def build_consts():
        kb.ms(ident_f.a(), 1.0, eng="pool")
        kb.asel(ident_f.a(), ident_f.a(), [[-1, 128]], ALU.is_equal, 0.0, 0, 1)
        kb.cp(ident_b.a(), ident_f.a(), eng="pool")
        kb.ms(ones_b.a(), 1.0, eng="pool")
        kb.ms(ones_f.a(), 1.0, eng="pool")
        kb.ms(bones_b.a(), 1.0, eng="pool")
        kb.ms(bones_b.a(64, [[1, 64]], 0, 64), 0.0, eng="pool")
        kb.ms(bones_b.a(0, [[1, 64]], 64, 64), 0.0, eng="pool")
        t = scr
        kb.ms(t.a(0, [[1, 128]]), 1.0, eng="pool")
        kb.asel(t.a(0, [[1, 128]]), t.a(0, [[1, 128]]), [[1, 128]], ALU.is_ge, 0.0, 0, -1)
        kb.cp(maskC.a(), t.a(0, [[1, 128]]), eng="pool")
        kb.cp(maskA.a(), t.a(0, [[1, 128]]), eng="pool")
        kb.ms(maskA.a(64, [[1, 64]], 0, 64), 0.0, eng="pool")
        kb.ms(t.a(0, [[1, 128]]), 1.0, eng="pool")
        kb.asel(t.a(0, [[1, 128]]), t.a(0, [[1, 128]]), [[-1, 128]], ALU.is_gt, 0.0, 0, 1)
        kb.cp(maskP.a(), t.a(0, [[1, 128]]), eng="pool")
        kb.ms(t.a(0, [[1, 128]]), 1.0, eng="pool")
        kb.asel(t.a(0, [[1, 128]]), t.a(0, [[1, 128]]), [[-1, 128]], ALU.is_gt, 0.0, 112, 1)
        kb.cp(maskM.a(), t.a(0, [[1, 128]]), eng="pool")
        kb.ms(sel.a(), 1.0, eng="pool")
        kb.asel(sel.a(), sel.a(), [[1, 256]], ALU.is_ge, 0.0, 0, -64)
        kb.asel(sel.a(), sel.a(), [[-1, 256]], ALU.is_ge, 0.0, 63, 64)
        kb.ms(epsT.a(), EPS, eng="pool")
        kb.ms(zerosT.a(), 0.0, eng="pool")
        for S in (STM,):
            for l in range(L):
                for j in range(2):
                    kb.ms(S["SH"][l][j].a(), 0.0, eng="pool")
                    kb.ms(S["SM"][l][j].a(), 0.0, eng="pool")
                kb.ms(S["mst"][l].a(), 0.0, eng="pool")
                kb.ms(S["cvh"][l].a(), 0.0, eng="pool")

    def load_params():
        q = "sp"
        kb.batch_begin(q, ch_par)
        with nc.allow_non_contiguous_dma(reason="tiny param layouts"):
            def ld(dst, src):
                kb.batch_add(kb.dma(q, dst, src, ch_par))
            for l in range(L):
                ld(gmix.a(l * 8, [[1, 8]]), dram_ap(norm_mix, l * D, [[1, 128], [128, 8]]))
                ld(gffn.a(l * 8, [[1, 8]]), dram_ap(norm_ffn, l * D, [[1, 128], [128, 8]]))
                ld(lgT.a(l * 2, [[1, 2]]), dram_ap(lb_logits, l * 256, [[1, 128], [128, 2]]))
                ld(hnormT.a(l * 2, [[1, 2]]), dram_ap(hgrn_norm, l * 256, [[1, 128], [128, 2]]))
                ld(mnormT.a(l * 2, [[1, 2]]), dram_ap(mlstm_norm, l * 256, [[1, 128], [128, 2]]))
                for j in range(4):
                    ld(cwT.a(l * 16 + j * 4, [[1, 4]]), dram_ap(conv_w, (l * 4 + j) * 512, [[1, 128], [128, 4]]))
                ld(cbT.a(l * 4, [[1, 4]]), dram_ap(conv_b, l * 512, [[1, 128], [128, 4]]))
                ld(ibT.a(l, [[1, 1]], 0, 4), dram_ap(i_bias, l * 4, [[1, 4], [1, 1]]))
                ld(fbT.a(l, [[1, 1]], 0, 4), dram_ap(f_bias, l * 4, [[1, 4], [1, 1]]))
            ld(gfin.a(), dram_ap(norm_final, 0, [[1, 128], [128, 8]]))
            ld(skrow.a(0, [[1, 16]], 0, 1), dram_ap(sinks, 0, [[16, 1], [1, 16]]))
        kb.batch_end(ch_par)
        t = scr
        l0, l1 = lgT.a(0, [[1, 2]]), lgT.a(2, [[1, 2]])
        mx, e0, e1, s_, r_ = (t.a(i * 2, [[1, 2]]) for i in range(5))
        kb.tt(mx, l0, l1, ALU.max)
        kb.tt(e0, l0, mx, ALU.subtract)
        kb.tt(e1, l1, mx, ALU.subtract)
        kb.act(e0, e0, AF.Exp)
        kb.act(e1, e1, AF.Exp)
        kb.tt(s_, e0, e1, ALU.add)
        kb.recip(r_, s_)
        kb.tt(e0, e0, r_, ALU.mult)
        kb.tt(e1, e1, r_, ALU.mult)
        kb.tt(lbT.a(0, [[1, 2]]), e0, e0, ALU.subtract)
        kb.tt(s_, e0, e1, ALU.add)
        kb.tt(lbT.a(2, [[1, 2]]), s_, e0, ALU.subtract)
        kb.ts(omlT.a(), lbT.a(), -1.0, 1.0, ALU.mult, ALU.add)
        kb.ts(nomlT.a(), omlT.a(), -1.0, None, ALU.mult)
        ps = PB[0]
        kb.mm(ps.a(0, [[1, 16]]), ones_f.a(0, [[1, 128]], 0, 1), skrow.a(0, [[1, 16]], 0, 1))
        for l in range(L):
            for p in range(2):
                kb.act(esink.a(l * 4, [[1, 4]], p * 64, 64), ps.a(l * 8 + p * 4, [[1, 4]], p * 64, 64), AF.Exp)

    wl_win, wl_fin, wl_fout = [], [], []
    state = dict(win_i=0, fin_i=0, fout_i=0, win_e=0, fin_e=0, fout_e=0)

    def wsrc(t, l, rows, col0, ncol, row0=0, nk=8, pn=128, ld=None):
        return dram_ap(t, (l * rows + row0) * ld + col0, [[ld, pn], [128 * ld, nk], [1, ncol]])

    def win_blocks(l):
        W = lambda c0, nc_: wsrc(w_in, l, D, c0, nc_, ld=DIN)
        b = []
        b.append([((0, 512), W(0, 512))])
        b.append([((0, 256), W(768, 256)), ((256, 256), W(1792, 256))])
        b.append([((0, 512), W(1024, 512))])
        f4 = []
        for p in range(2):
            for j in range(4):
                f4.append(((j * 128 + p * 64, 64), W(2056 + (j + 4 * p) * 64, 64)))
        b.append([((0, 128), W(2568, 128)), ((128, 8), W(2048, 8))])
        b.append(f4)
        b.append([((0, 256), W(512, 256)), ((256, 256), W(1536, 256))])
        b.append([((0, 256), W(2568, 256))])
        return b

    def emit_win(i):
        l, blk = wl_win[i]
        slot = i % 2
        T = WIN[slot]
        for (c0, ncol), src in win_blocks(l)[blk]:
            dst = T.a(c0, [[512, 8], [1, ncol]])
            kb.dma("pool", dst, src, ch_win[slot])

    def emit_wout(l):
        T = WOUT
        kb.dma("pool", T.a(0, [[1024, 4], [1, 1024]]), wsrc(w_out, l, D, 0, 1024, nk=4, ld=D), ch_wout)
        for p in range(2):
            src = dram_ap(w_out, (l * D + 512 + p * 256) * D, [[D, 64], [64 * D, 4], [1, 1024]])
            kb.dma("pool", T.a(4 * 1024, [[1024, 4], [1, 1024]], p * 64, 64), src, ch_wout)

    def emit_fin(i):
        l, b = wl_fin[i]
        slot = i % NFIN
        T = FIN[slot]
        kb.dma("pool", T.a(0, [[256, 8], [1, 256]]), wsrc(w_ffn_in, l, D, b * 256, 256, ld=2 * DFF), ch_fin[slot])
        kb.dma("pool", T.a(2048, [[256, 8], [1, 256]]), wsrc(w_ffn_in, l, D, DFF + b * 256, 256, ld=2 * DFF), ch_fin[slot])

    def emit_fout(i):
        l, dc = wl_fout[i]
        slot = i % NFOUT
        T = FOUT[slot]
        kb.dma("pool", T.a(0, [[128, 22], [1, 128]]), wsrc(w_ffn_out, l, DFF, dc * 128, 128, nk=22, ld=D), ch_fout[slot])

    def get_win():
        i = state["win_i"]
        state["win_i"] += 1
        while state["win_e"] < min(len(wl_win), i + 2):
            emit_win(state["win_e"])
            state["win_e"] += 1
        return WIN[i % 2]

    def get_fin():
        i = state["fin_i"]
        state["fin_i"] += 1
        while state["fin_e"] < min(len(wl_fin), i + NFIN):
            emit_fin(state["fin_e"])
            state["fin_e"] += 1
        return FIN[i % NFIN]

    def get_fout():
        i = state["fout_i"]
        state["fout_i"] += 1
        while state["fout_e"] < min(len(wl_fout), i + NFOUT):
            emit_fout(state["fout_e"])
            state["fout_e"] += 1
        return FOUT[i % NFOUT]

    groups1 = [Grp("meta", 16, 16, [(0, 16, [(0, 0, 16)])])]
    if NSAMP:
        groups1.append(Grp("sample", NSAMP * 4, 4, [(b * 4, 4, [(b, 0, 4)]) for b in range(NSAMP)]))
    groups2 = []
    for s in range(NSEQ):
        for gi in range(NGRP):
            groups2.append(Grp("prompt", 512, 64, [(i * 128, 128, [(2 * i, 0, 64), (2 * i + 1, 64, 128)]) for i in range(4)],
                               seq=s, gi=gi, last=(gi == NGRP - 1)))
    for g in groups1 + groups2:
        for l in range(L):
            for blk in range(7):
                wl_win.append((l, blk))
            if g.kind == "meta" and l == L - 1:
                continue
            for b in range(11):
                wl_fin.append((l, b))
            for dc in range(8):
                wl_fout.append((l, dc))

    dbg_out = {}

    def dump(name, a, shape):
        if not debug:
            return
        t = nc.dram_tensor("dbg_" + name, list(shape), a.ap.dtype if hasattr(a.ap, "dtype") else F32, kind="ExternalOutput")
        c = kb.chan("dbg_" + name)
        kb.dma("sp", t.ap(), a, c)
        dbg_out[name] = c

    def run_phase(groups, NM):
        with ExitStack() as es2:
            old_es = kb.es
            kb.es = es2
            kb.tag = f"_n{NM}"
            try:
                _run_phase(groups, NM)
            finally:
                kb.es = old_es
                kb.tag = ""

    def _run_phase(groups, NM):
        NT = max(len(g.tiles) for g in groups)
        hT = sb("hT", 8 * NM)
        hn = sb("hn", 8 * NM, BF16)
        XPW = max((g.N + 3 * (len(g.tiles) if g.kind == "sample" else 1)) for g in groups)
        arena = Arena(kb, "arena", max(32 * NM + 16 * XPW, 44 * NM, 25600))
        actT = arena.view(0, 22 * NM, BF16)
        xs = [arena.view(4096 * i, 1024, F32) for i in range(2)]
        sqb = [sb(f"sqb{i}", NM, BF16) for i in range(2)]
        sqh = sb("sqh", NM, BF16)
        sqh2 = sb("sqh2", NM, BF16)
        lnv = sb("lnv", NM)
        rstd = sb("rstd", NM)
        qs = [arena.view(4 * NM * j, NM, F32) for j in range(2)]
        th = [arena.view(8 * NM + 4 * NM * j, NM, F32) for j in range(2)]
        ga = [sb(f"ga{j}", NM, BF16) for j in range(2)]
        go = [sb(f"go{j}", NM, BF16) for j in range(2)]
        xp = arena.view(16 * NM, 4 * XPW, F32)
        cvo = arena.view(2048, 512, F32)
        co = arena.view(16 * NM + 16 * XPW, 4 * NM, F32)
        QT = sb("QT", 4 * NM, BF16)
        KT = sb("KT", NM, BF16)
        igr = sb("igr", NM)
        fgr = sb("fgr", NM)
        Vall = sb("Vall", NT * 640, BF16)
        kvout = sb("kvout", 256)
        T = [sb(f"T{i}", NM + 1) for i in range(6)]
        R = [sb(f"R{i}", NM + 1) for i in range(2)]
        Bm = sb("Bm", NM + 1)
        mz = sb("mz", NM + 1)
        Qh = [[sb(f"Qh{x}{j}", NM, BF16) for j in range(2)] for x in range(2)]
        Kt = [[sb(f"Kt{x}{j}", NM, BF16) for j in range(2)] for x in range(2)]
        Kh = [[sb(f"Kh{x}{j}", NM, BF16) for j in range(2)] for x in range(2)]
        dec = [[sb(f"dec{x}{j}", 16) for j in range(2)] for x in range(2)]
        smid = [sb(f"smid{j}", 16) for j in range(2)]
        em = [arena.view(16 * NM + 4 * NM * j, NM, F32) for j in range(2)]
        oall = [[arena.view(4 * NM * (2 * x + j), NM, F32) for j in range(2)] for x in range(2)]
        dall = [arena.view(24 * NM + 4 * NM * j, NM, F32) for j in range(2)]
        att_sb = [sb(f"att{i}", 256, BF16) for i in range(4)]
        khtok = [sb(f"khtok{i}", 128, BF16) for i in range(4)]
        sbf = [sb(f"sbf{i}", 128, BF16) for i in range(4)]
        s1 = [sb(f"s1{i}", 128) for i in range(4)]
        pesb = [arena.view(18432 + 1024 * i, 512, BF16) for i in range(4)]
        tdc = arena.view(16384, 512, F32)
        cst = arena.view(0, 512, F32)
        has_sample = any(g.kind == "sample" for g in groups)
        if has_sample:
            NS = NSAMP
            SHs = sb("SHs", NS * 2 * 64)
            SMs = sb("SMs", NS * 2 * 128)
            Xc = arena.view(8192, NS * 2 * 64, F32)
            Xn = sb("Xn", 64)
            nl = sb("nl", NS * 2)
            m0 = sb("m0", NS)
            Xv = sb("Xv", 512)
            Kc = sb("Kc", NS * 128)
            Vc = sb("Vc", NS * 128)
            KTc = sb("KTc", NS * 128, BF16)
            Vcb = sb("Vcb", NS * 128, BF16)
            mout = sb("mout", NS)
            cvs = sb("cvs", 4 * NS * 3)
            cso = sb("cso", 512)

        print(f"[sbuf] phase NM={NM}: bytes remaining per partition = {kb.nc.sbuf_bytes_remaining}") if needed is None else None
        acc_rr = [0]

        acc_pool = [[0, 1, 2, 3]]

        def accbank():
            acc_rr[0] = (acc_rr[0] + 1) % len(acc_pool[0])
            return PB[acc_pool[0][acc_rr[0]]]

        WIDE = [0, 1, 2, 3, 5, 6, 7]

        rms_pending = []

        def rms_stats(g, c, flush=False):
            N = g.N
            ps = PB[4]
            if c is not None:
                s = sqb[c % 2]
                kb.act(s.a(0, [[1, N]]), hT.a(c * NM, [[1, N]], key=c), AF.Square)
                rms_pending.append((c, s))
            while rms_pending and (flush or len(rms_pending) > 1):
                c_, s_ = rms_pending.pop(0)
                kb.mm(ps.a(0, [[1, N]]), ones_b.a(), s_.a(0, [[1, N]]), start=(c_ == 0), stop=(c_ == 7))

        def rmsnorm(g, gam, goff, dst_bf=True, have_stats=False):
            N = g.N
            ps = PB[4]
            if not have_stats:
                for c in range(8):
                    rms_stats(g, c)
            rms_stats(g, None, flush=True)
            kb.act(lnv.a(0, [[1, N]]), ps.a(0, [[1, N]]), AF.Ln, bias=epsT.a(), scale=1.0 / D)
            kb.act(rstd.a(0, [[1, N]]), lnv.a(0, [[1, N]]), AF.Exp, scale=-0.5)
            for c in range(8):
                dst = hn.a(c * NM, [[1, N]], key=c) if dst_bf else hT.a(c * NM, [[1, N]], key=c)
                kb.stt(dst, hT.a(c * NM, [[1, N]], key=c), gam.a(goff + c, [[1, 1]]), rstd.a(0, [[1, N]]), ALU.mult, ALU.mult)

        def fm(W, col, M, ps, N):
            for k_ in range(8):
                kb.mm(ps.a(0, [[1, N]], 0, M), W.a(k_ * 512 + col, [[1, M]]), hn.a(k_ * NM, [[1, N]], key=k_),
                      start=(k_ == 0), stop=(k_ == 7))

        def tm(W, col, ncol, ps, t0, n):
            for k_ in range(8):
                kb.mm(ps.a(0, [[1, ncol]], 0, n), hn.a(k_ * NM + t0, [[1, n]], key=k_), W.a(k_ * 512 + col, [[1, ncol]]),
                      start=(k_ == 0), stop=(k_ == 7))

        def load_x(g):
            N = g.N
            if g.kind == "prompt":
                srcs = [(t0, n, dram_ap(x_prompt, (g.seq * SEQ + g.gi * 512 + t0) * D, [[D, n], [1, D]])) for (t0, n, _) in g.tiles]
            elif g.kind == "meta":
                srcs = [(0, 16, dram_ap(meta_tokens, 0, [[D, 16], [1, D]]))]
            else:
                srcs = [(0, N, dram_ap(x_sample, 0, [[D, N], [1, D]]))]
            for i, (t0, n, src) in enumerate(srcs):
                s = i % 2
                kb.dma("sp", xs[s].a(0, [[1, 1024]], 0, n), src, ch_x[s])
                for half in range(2):
                    ps = accbank()
                    for c4 in range(4):
                        c = half * 4 + c4
                        kb.tr(ps.a(c4 * 128, [[1, n]]), xs[s].a(c * 128, [[1, 128]], 0, n), ident_f.a(0, [[1, n]], 0, n))
                    for c4 in range(4):
                        c = half * 4 + c4
                        kb.cp(hT.a(c * NM + t0, [[1, n]], key=c), ps.a(c4 * 128, [[1, n]]), eng=("dve" if c4 % 2 else "act_cp"))

        def store_y(g):
            N = g.N
            if g.kind == "prompt":
                dsts = [(t0, n, dram_ap(y_prompt, (g.seq * SEQ + g.gi * 512 + t0) * D, [[D, n], [1, D]])) for (t0, n, _) in g.tiles]
            else:
                dsts = [(0, N, dram_ap(y_sample, 0, [[D, N], [1, D]]))]
            for i, (t0, n, dst) in enumerate(dsts):
                s = i % 2
                for half in range(2):
                    ps = accbank()
                    for c4 in range(4):
                        c = half * 4 + c4
                        kb.tr(ps.a(c4 * 128, [[1, 128]], 0, n), hT.a(c * NM + t0, [[1, n]], key=c), ident_f.a())
                    kb.cp(xs[s].a(half * 512, [[1, 512]], 0, n), ps.a(0, [[1, 512]], 0, n), eng=("dve" if half else "act_cp"))
                kb.dma("sp", dst, xs[s].a(0, [[1, 1024]], 0, n), ch_y[s])

        def stageA(g, l, S):
            N, Lc, nch = g.N, g.Lc, g.nch
            rmsnorm(g, gmix, l * 8, have_stats=(l > 0))
            W = get_win()
            for j in range(2):
                ps = accbank()
                fm(W, j * 128, 128, ps, N)
                kb.act(qs[j].a(0, [[1, N]]), ps.a(0, [[1, N]]), AF.Silu)
            for j in range(2):
                ps = accbank()
                fm(W, 256 + j * 128, 128, ps, N)
                kb.act(th[j].a(0, [[1, N]]), ps.a(0, [[1, N]]), AF.Tanh, scale=0.5)
            W = get_win()
            for j in range(2):
                ps = accbank()
                fm(W, j * 128, 128, ps, N)
                kb.act(ga[j].a(0, [[1, N]]), ps.a(0, [[1, N]]), AF.Silu)
            for j in range(2):
                ps = accbank()
                fm(W, 256 + j * 128, 128, ps, N)
                kb.act(T[0].a(0, [[1, N]]), ps.a(0, [[1, N]]), AF.Tanh, scale=0.5)
                kb.ts(go[j].a(0, [[1, N]]), T[0].a(0, [[1, N]]), 0.5, 0.5, ALU.mult, ALU.add)
            W = get_win()
            nsg = len(g.tiles) if g.kind == "sample" else 1
            Ls = N // nsg
            SW = Ls + 3
            XW = nsg * SW
            for ch in range(4):
                if g.kind == "sample":
                    pass
                else:
                    kb.cp(xp.a(ch * XW, [[1, 3]], key=ch), S["cvh"][l].a(ch * 3, [[1, 3]]))
                ps = accbank()
                fm(W, ch * 128, 128, ps, N)
                kb.cp(xp.a(ch * XW + 3, [[SW, nsg], [1, Ls]], key=ch), ps.a(0, [[Ls, nsg], [1, Ls]]), eng="act_cp")
                cw = lambda j_: cwT.a(l * 16 + j_ * 4 + ch, [[1, 1]])
                o_ = co.a(ch * NM, [[Ls, nsg], [1, Ls]], key=ch)
                kb.ts(o_, xp.a(ch * XW, [[SW, nsg], [1, Ls]], key=ch), cw(0), cbT.a(l * 4 + ch, [[1, 1]]), ALU.mult, ALU.add, eng=CONV_ENG)
                for j_ in range(1, 4):
                    kb.stt(o_, xp.a(ch * XW + j_, [[SW, nsg], [1, Ls]], key=ch), cw(j_), o_, ALU.mult, ALU.add, eng=CONV_ENG)
                kb.act(co.a(ch * NM, [[1, N]], key=ch), co.a(ch * NM, [[1, N]], key=ch), AF.Silu)
                if g.kind != "sample":
                    kb.cp(S["cvh"][l].a(ch * 3, [[1, 3]]), xp.a(ch * XW + N, [[1, 3]], key=ch))
                else:
                    kb.cp(cvs.a(ch * nsg * 3, [[3, nsg], [1, 3]]), xp.a(ch * XW + 4, [[SW, nsg], [1, 3]], key=ch))
            W = get_win()
            ps = accbank()
            fm(W, 0, 128, ps, N)
            kb.cp(KT.a(0, [[1, N]]), ps.a(0, [[1, N]]))
            ps = accbank()
            fm(W, 128, 4, ps, N)
            kb.ts(igr.a(0, [[1, N]], 0, 4), ps.a(0, [[1, N]], 0, 4), ibT.a(l, [[1, 1]], 0, 4), None, ALU.add)
            ps = accbank()
            fm(W, 132, 4, ps, N)
            kb.ts(fgr.a(0, [[1, N]], 0, 4), ps.a(0, [[1, N]], 0, 4), fbT.a(l, [[1, 1]], 0, 4), None, ALU.add)

        def stageA_tail(g, l):
            N = g.N
            W = get_win()
            for j in range(4):
                ps = accbank()
                fm(W, j * 128, 128, ps, N)
                kb.cp(QT.a(j * NM, [[1, N]], key=j), ps.a(0, [[1, N]]), eng=("dve" if j % 2 else "act_cp"))
                yield
            W = get_win()
            for ti, (t0, n, _) in enumerate(g.tiles):
                ps = accbank()
                tm(W, 0, 512, ps, t0, n)
                kb.cp(Vall.a(ti * 640, [[1, 512]], 0, n, key=ti), ps.a(0, [[1, 512]], 0, n), eng=("act_cp" if ti % 2 else "dve"))
                yield
            W = get_win()
            for ti, (t0, n, _) in enumerate(g.tiles):
                ps = accbank()
                tm(W, 0, 256, ps, t0, n)
                kb.cp(Vall.a(ti * 640 + 512, [[1, 128]], 0, n, key=ti), ps.a(128, [[1, 128]], 0, n), eng=("dve" if ti % 2 else "act_cp"))
                if g.kind == "sample":
                    kb.cp(Kc.a(ti * 128, [[1, 128]], 0, n), ps.a(0, [[1, 128]], 0, n))
                    kb.cp(Vc.a(ti * 128, [[1, 128]], 0, n), ps.a(128, [[1, 128]], 0, n))
                elif g.kind == "prompt" and g.last and ti == len(g.tiles) - 1:
                    kb.cp(kvout.a(0, [[1, 256]], 0, n), ps.a(0, [[1, 256]], 0, n))
                yield

        def prepH(g, l):
            N, Lc, nch = g.N, g.Lc, g.nch
            mi = Lc // 2 - 1
            tsig, tkk, tf, Bz, tbp, teb = T
            v = lambda t_, o=0: t_.a(o, [[1, N]])
            v3 = lambda t_, o=0: t_.a(o, [[Lc, nch], [1, Lc]])
            bc = lambda t_, o: t_.a(o, [[Lc, nch], [0, Lc]])
            cv_ = lambda t_, o: t_.a(o, [[Lc, nch]])
            for j in range(2):
                pj = lambda P_: P_.a(l * 2 + j, [[1, 1]])
                kb.ts(v(tsig), v(th[j]), 0.5, 0.5, ALU.mult, ALU.add)
                kb.ts(v(tkk), v(tsig), pj(nomlT), pj(omlT), ALU.mult, ALU.add)
                kb.ts(v(tf), v(tsig), pj(omlT), pj(lbT), ALU.mult, ALU.add)
                kb.ts(v(tf), v(tf), 1e-30, None, ALU.max)
                yield
                kb.act(v(tf), v(tf), AF.Ln)
                kb.ms(Bz.a(0, [[1, 1]]), 0.0)
                kb.scan(v(Bz, 1), v(tf), zerosT.a(0, [[1, N]]), 0.0, ALU.add, ALU.add)
                kb.tt(v3(tbp), v3(Bz, 1), bc(Bz, 1 + mi), ALU.subtract)
                yield
                kb.act(v(teb), v(tbp), AF.Exp)
                kb.act(v(tbp), v(tbp), AF.Exp, scale=-1.0)
                kb.tt(Qh[0][j].a(0, [[1, N]]), v(qs[j]), v(teb), ALU.mult)
                kb.tt(Kt[0][j].a(0, [[1, N]]), v(tkk), v(tbp), ALU.mult)
                yield
                kb.tt(Kh[0][j].a(0, [[Lc, nch], [1, Lc]]), Kt[0][j].a(0, [[Lc, nch], [1, Lc]]), bc(teb, Lc - 1), ALU.mult)
                kb.tt(tsig.a(0, [[1, nch]]), cv_(Bz, Lc), cv_(Bz, 0), ALU.subtract)
                kb.act(dec[0][j].a(0, [[1, nch]]), tsig.a(0, [[1, nch]]), AF.Exp)
                kb.tt(tsig.a(0, [[1, nch]]), cv_(Bz, 1 + mi), cv_(Bz, 0), ALU.subtract)
                kb.act(smid[j].a(0, [[1, nch]]), tsig.a(0, [[1, nch]]), AF.Exp)
                yield

        def prepM(g, l, S):
            N, Lc, nch = g.N, g.Lc, g.nch
            r4 = lambda t_, o=0: t_.a(o, [[1, N]], 0, 4)
            r3 = lambda t_, o=0: t_.a(o, [[Lc, nch], [1, Lc]], 0, 4)
            rb = lambda t_, o: t_.a(o, [[Lc, nch], [0, Lc]], 0, 4)
            rbl, rml = R
            rlf, ra2, ru = fgr, rml, rbl
            kb.act(r4(fgr), r4(fgr), AF.Exp, scale=-1.0)
            kb.act(r4(fgr), r4(fgr), AF.Ln, bias=1.0)
            kb.ts(r4(rlf), r4(fgr), -1.0, None, ALU.mult)
            kb.ms(Bm.a(0, [[1, 1]], 0, 4), 0.0)
            kb.scan(r4(Bm, 1), r4(rlf), zerosT.a(0, [[1, N]], 0, 4), 0.0, ALU.add, ALU.add)
            if g.kind == "sample":
                for b, (t0, n, _) in enumerate(g.tiles):
                    kb.scan(mz.a(1 + t0, [[1, n]], 0, 4), rlf.a(t0, [[1, n]], 0, 4), igr.a(t0, [[1, n]], 0, 4),
                            m0.a(b, [[1, 1]], 0, 4), ALU.add, ALU.max)
                mprev_b = m0.a(0, [[1, nch], [0, Lc]], 0, 4)
            else:
                kb.cp(mz.a(0, [[1, 1]], 0, 4), S["mst"][l].a(0, [[1, 1]], 0, 4))
                kb.scan(r4(mz, 1), r4(rlf), r4(igr), mz.a(0, [[1, 1]], 0, 4), ALU.add, ALU.max)
                kb.cp(S["mst"][l].a(0, [[1, 1]], 0, 4), mz.a(N, [[1, 1]], 0, 4))
                mprev_b = rb(mz, 0)
            yield
            kb.tt(r3(rbl), r3(Bm, 1), rb(Bm, 0), ALU.subtract)
            kb.tt(r3(rml), r3(mz, 1), mprev_b, ALU.subtract)
            kb.tt(r4(ra2), r4(rbl), r4(rml), ALU.subtract)
            kb.tt(r4(ru), r4(igr), r4(rbl), ALU.subtract)
            kb.tt(r3(ru), r3(ru), mprev_b, ALU.subtract)
            tea, teu = lnv, rstd
            v = lambda t_, o=0: t_.a(o, [[1, N]])
            yield
            for j in range(2):
                lhs = sel.a(j * 128, [[1, 128]], 0, 4)
                ps = accbank()
                kb.mm(ps.a(0, [[1, N]]), lhs, r4(ra2))
                kb.act(v(tea), ps.a(0, [[1, N]]), AF.Exp)
                ps = accbank()
                kb.mm(ps.a(0, [[1, N]]), lhs, r4(ru))
                kb.act(v(teu), ps.a(0, [[1, N]]), AF.Exp)
                ps = accbank()
                kb.mm(ps.a(0, [[1, N]]), lhs, r4(mz, 1))
                kb.act(v(em[j]), ps.a(0, [[1, N]]), AF.Exp, scale=-1.0)
                yield
                kb.tt(Qh[1][j].a(0, [[1, N]]), co.a(j * NM, [[1, N]], key=j), v(tea), ALU.mult)
                kb.stt(Kt[1][j].a(0, [[1, N]]), co.a((2 + j) * NM, [[1, N]], key=2 + j), 0.125, v(teu), ALU.mult, ALU.mult)
                kb.tt(Kh[1][j].a(0, [[Lc, nch], [1, Lc]]), Kt[1][j].a(0, [[Lc, nch], [1, Lc]]),
                      tea.a(Lc - 1, [[Lc, nch], [0, Lc]]), ALU.mult)
                kb.cp(dec[1][j].a(0, [[1, nch]]), tea.a(Lc - 1, [[Lc, nch]]))
                yield

        la_rr = [0]

        def la_tile(x, j, g, ti, Sst, Wd):
            t0, n, chunks = g.tiles[ti]
            q_ = 2 * x + j
            Q, K_, KH = Qh[x][j], Kt[x][j], Kh[x][j]
            vcol = (0 if x == 0 else 256) + j * 128
            asb, kt_, sb_, s1_ = att_sb[q_], khtok[q_], sbf[q_], s1[q_]
            pa = PB[q_]
            kb.mm(pa.a(0, [[1, n]], 0, n), K_.a(t0, [[1, n]], 0, 64), Q.a(t0, [[1, n]], 0, 64), sgc=True)
            kb.tr(pa.a(512, [[1, 128]], 0, n, bf=True), KH.a(t0, [[1, n]]), ident_b.a())
            yield
            kb.mm(pa.a(128, [[1, n]], 0, n), K_.a(t0, [[1, n]], 64, 64), Q.a(t0, [[1, n]], 64, 64), sgc=True)
            yield
            kb.tt(asb.a(0, [[128, 2], [1, n]], 0, n), pa.a(0, [[128, 2], [1, n]], 0, n),
                  maskA.a(0, [[0, 2], [1, n]], 0, n), ALU.mult)
            kb.cp(kt_.a(0, [[1, 128]], 0, n), pa.a(512, [[1, 128]], 0, n, bf=True), eng="act_cp")
            yield
            po = PB[4 + q_]
            for p in range(2):
                kb.mm(po.a(0, [[1, n]], p * 64, 64), Vall.a(ti * 640 + vcol + p * 64, [[1, 64]], 0, n, key=ti),
                      asb.a(p * 128, [[1, n]], 0, n), start=True, stop=False, sgc=True)
                if x == 1:
                    kb.mm(po.a(128, [[1, n]], p * 64, 64), ones_b.a(0, [[1, 64]], 0, n),
                          asb.a(p * 128, [[1, n]], 0, n), start=False, stop=False, sgc=True)
            for ci, (c, cs, ce) in enumerate(chunks):
                last = ci == len(chunks) - 1
                Sf = Sst(0, 128, 0, Wd)
                if x == 0:
                    kb.ts(sb_.a(0, [[1, Wd]]), Sf, smid[j].a(c, [[1, 1]]), None, ALU.mult)
                else:
                    kb.act(sb_.a(0, [[1, Wd]]), Sf, AF.Copy)
                pP = pa
                for p in range(2):
                    kb.mm(pP.a(0, [[1, 64]], p * 64, 64), kt_.a(p * 64, [[1, 64]], cs, ce - cs),
                          Vall.a(ti * 640 + vcol + p * 64, [[1, 64]], cs, ce - cs, key=ti), start=True, stop=(x == 0), sgc=True)
                    if x == 1:
                        kb.mm(pP.a(64, [[1, 64]], p * 64, 64), kt_.a(p * 64, [[1, 64]], cs, ce - cs),
                              ones_b.a(0, [[1, 64]], cs, ce - cs), start=False, stop=True, sgc=True)
                yield
                for p in range(2):
                    kb.mm(po.a(cs, [[1, ce - cs]], p * 64, 64), sb_.a(0, [[1, 64]], p * 64, 64),
                          Q.a(t0 + cs, [[1, ce - cs]], p * 64, 64), start=False, stop=last, sgc=True)
                    if x == 1:
                        kb.mm(po.a(128 + cs, [[1, ce - cs]], p * 64, 64), sb_.a(64, [[1, 64]], p * 64, 64),
                              Q.a(t0 + cs, [[1, ce - cs]], p * 64, 64), start=False, stop=last, sgc=True)
                    if p == 0:
                        yield
                kb.stt(Sf, Sf, dec[x][j].a(c, [[1, 1]]), pP.a(0, [[1, Wd]]), ALU.mult, ALU.add)
                yield
            kb.cp(oall[x][j].a(t0, [[1, n]]), po.a(0, [[1, n]]), eng="act_cp")
            if x == 1:
                kb.cp(dall[j].a(t0, [[1, n]]), po.a(128, [[1, n]]))

        def run_interleaved(gens):
            gens = list(gens)
            while gens:
                for g_ in list(gens):
                    try:
                        next(g_)
                    except StopIteration:
                        gens.remove(g_)

        def head_norm(g, l, src, nrm, gate, dstc, q_):
            N = g.N
            v = lambda t_: t_.a(0, [[1, N]])
            sq_ = (sqb[0], sqb[1], sqh, sqh2)[q_]
            rs_ = T[q_]
            kb.act(sq_.a(0, [[1, N]]), v(src), AF.Square)
            yield
            ps = accbank()
            kb.mm(ps.a(0, [[1, N]]), bones_b.a(), sq_.a(0, [[1, N]]))
            kb.act(v(rs_), ps.a(0, [[1, N]]), AF.Ln, bias=epsT.a(), scale=1.0 / 64)
            yield
            kb.act(v(rs_), v(rs_), AF.Exp, scale=-0.5)
            yield
            kb.stt(v(src), v(src), nrm, v(rs_), ALU.mult, ALU.mult)
            yield
            kb.tt(hn.a(dstc * NM, [[1, N]], key=dstc), v(src), gate.a(0, [[1, N]]), ALU.mult)

        def postH_gen(g, l, j):
            yield from head_norm(g, l, oall[0][j], hnormT.a(l * 2 + j, [[1, 1]]), ga[j], j, j)

        def postM_gen(g, l, j):
            N = g.N
            v = lambda t_: t_.a(0, [[1, N]])
            t4 = T[4 + j]
            kb.stt(v(t4), v(dall[j]), -1.0, v(dall[j]), ALU.mult, ALU.max)
            kb.tt(v(t4), v(t4), v(em[j]), ALU.max)
            yield
            kb.act(v(t4), v(t4), AF.Ln)
            yield
            kb.act(v(t4), v(t4), AF.Exp, scale=-1.0)
            yield
            kb.tt(v(oall[1][j]), v(oall[1][j]), v(t4), ALU.mult)
            yield
            yield from head_norm(g, l, oall[1][j], mnormT.a(l * 2 + j, [[1, 1]]), go[j], 2 + j, 2 + j)

        def post_all(g, l):
            run_interleaved([postM_gen(g, l, 0), postM_gen(g, l, 1), postH_gen(g, l, 0), postH_gen(g, l, 1)])

        def swa_norm(g, l, ti, po, pd):
            t0, n, _ = g.tiles[ti]
            kb.tt(tdc.a(0, [[128, 4], [1, n]]), pd.a(0, [[128, 4], [1, n]]), esink.a(l * 4, [[1, 4], [0, n]]), ALU.add)
            kb.act(tdc.a(0, [[128, 4], [1, n]]), tdc.a(0, [[128, 4], [1, n]]), AF.Ln)
            kb.act(tdc.a(0, [[128, 4], [1, n]]), tdc.a(0, [[128, 4], [1, n]]), AF.Exp, scale=-1.0)
            dst = hn.a(4 * NM + t0, [[NM, 4], [1, n]], key=(4, 5, 6, 7))
            kb.tt(dst, po.a(0, [[128, 4], [1, n]]), tdc.a(0, [[128, 4], [1, n]]), ALU.mult)

        def swa_tile(g, l, ti, kbs, bset=0, part="S"):
            t0, n, _ = g.tiles[ti]
            po, pd = PB[4 + 2 * bset], PB[5 + 2 * bset]
            if part == "N":
                return swa_norm(g, l, ti, po, pd)
            for bi, (Kf, Vf, nk, mk) in enumerate(kbs):
                for p in range(2):
                    sl = bi * 2 + p
                    ps = accbank()
                    for j in range(4):
                        kb.mm(ps.a(j * 128, [[1, n]], 0, nk), Kf(p), QT.a(j * NM + t0, [[1, n]], p * 64, 64, key=j), sgc=True)
                    kb.act(pesb[sl].a(0, [[128, 4], [1, n]], 0, nk), ps.a(0, [[128, 4], [1, n]], 0, nk), AF.Exp, scale=0.125)
                    kb.tt(pesb[sl].a(0, [[128, 4], [1, n]], 0, nk), pesb[sl].a(0, [[128, 4], [1, n]], 0, nk), mk(nk, n), ALU.mult)
            nb = len(kbs)
            for j in range(4):
                for p in range(2):
                    for bi, (Kf, Vf, nk, mk) in enumerate(kbs):
                        sl = bi * 2 + p
                        kb.mm(po.a(j * 128, [[1, n]], p * 64, 64), Vf(p), pesb[sl].a(j * 128, [[1, n]], 0, nk),
                              start=(j == 0 and bi == 0), stop=(j == 3 and bi == nb - 1), sgc=True)
            for j in range(4):
                for p in range(2):
                    for bi, (Kf, Vf, nk, mk) in enumerate(kbs):
                        sl = bi * 2 + p
                        kb.mm(pd.a(j * 128, [[1, n]], p * 64, 64), ones_b.a(0, [[1, 64]], 0, nk),
                              pesb[sl].a(j * 128, [[1, n]], 0, nk), start=(j == 0 and bi == 0), stop=(j == 3 and bi == nb - 1), sgc=True)

        def wout_stage(g, l):
            N = g.N
            for dc in range(8):
                ps = accbank()
                for k_ in range(8):
                    rhs = hn.a(k_ * NM, [[1, N]], key=k_)
                    kb.mm(ps.a(0, [[1, N]]), WOUT.a(k_ * 1024 + dc * 128, [[1, 128]]), rhs, start=(k_ == 0), stop=(k_ == 7))
                kb.tt(hT.a(dc * NM, [[1, N]], key=dc), hT.a(dc * NM, [[1, N]], key=dc), ps.a(0, [[1, N]]), ALU.add)
                rms_stats(g, dc)

        def ffn_stage(g, l):
            N = g.N
            rmsnorm(g, gffn, l * 8, have_stats=True)
            for b in range(11):
                W = get_fin()
                for cc in range(2):
                    c = 2 * b + cc
                    bp = ((0, 1), (2, 3), (5, 6))[c % 3]
                    pg, pu = PB[bp[0]], PB[bp[1]]
                    for k_ in range(8):
                        kb.mm(pg.a(0, [[1, N]]), W.a(k_ * 256 + cc * 128, [[1, 128]]), hn.a(k_ * NM, [[1, N]], key=k_),
                              start=(k_ == 0), stop=(k_ == 7))
                    for k_ in range(8):
                        kb.mm(pu.a(0, [[1, N]]), W.a(2048 + k_ * 256 + cc * 128, [[1, 128]]), hn.a(k_ * NM, [[1, N]], key=k_),
                              start=(k_ == 0), stop=(k_ == 7))
                    sg = (lnv, rstd)[cc]
                    kb.act(sg.a(0, [[1, N]]), pg.a(0, [[1, N]]), AF.Silu)
                    kb.tt(actT.a(c * NM, [[1, N]], key=c), sg.a(0, [[1, N]]), pu.a(0, [[1, N]]), ALU.mult)
            for dc in range(8):
                W = get_fout()
                ps = PB[WIDE[dc % 7]]
                for c in range(22):
                    kb.mm(ps.a(0, [[1, N]]), W.a(c * 128, [[1, 128]]), actT.a(c * NM, [[1, N]], key=c), start=(c == 0), stop=(c == 21))
                kb.tt(hT.a(dc * NM, [[1, N]], key=dc), hT.a(dc * NM, [[1, N]], key=dc), ps.a(0, [[1, N]]), ALU.add)
                rms_stats(g, dc)

        def seq_init():
            for l in range(L):
                for j in range(2):
                    kb.cp(STW["SH"][l][j].a(), STM["SH"][l][j].a())
                    kb.cp(STW["SM"][l][j].a(), STM["SM"][l][j].a(), eng="act_cp")
                kb.cp(STW["mst"][l].a(0, [[1, 1]], 0, 4), STM["mst"][l].a(0, [[1, 1]], 0, 4))
                kb.cp(STW["cvh"][l].a(), STM["cvh"][l].a())

        def load_sample_states(g, l):
            q = "sp"
            NS = NSAMP
            kb.batch_begin(q, ch_ld)
            with nc.allow_non_contiguous_dma(reason="state layouts"):
                for p in range(2):
                    kb.batch_add(kb.dma(q, SHs.a(0, [[64, NS * 2], [1, 64]], p * 64, 64, key=tuple(range(NS * 2))),
                                        dram_ap(st_hgrn, l * NS * 16384 + p * 4096, [[64, 64], [8192, NS * 2], [1, 64]]), ch_ld))
                    kb.batch_add(kb.dma(q, Xc.a(0, [[64, NS * 2], [1, 64]], p * 64, 64),
                                        dram_ap(st_c, l * NS * 16384 + p * 4096, [[64, 64], [8192, NS * 2], [1, 64]]), ch_ld))
                kb.batch_add(kb.dma(q, Xn.a(0, [[1, 64]], 0, NS * 4), dram_ap(st_n, l * NS * 256, [[64, NS * 4], [1, 64]]), ch_ld))
                kb.batch_add(kb.dma(q, m0.a(0, [[1, NS]], 0, 4), dram_ap(st_m, l * NS * 4, [[1, 4], [4, NS]]), ch_ld))
                kb.batch_add(kb.dma(q, Xv.a(0, [[1, 512]], 0, NS * 3), dram_ap(st_conv, l * NS * 1536, [[512, NS * 3], [1, 512]]), ch_ld))
                kb.batch_add(kb.dma(q, Kc.a(0, [[128, NS], [1, 128]]), dram_ap(c_k, l * NS * 16384, [[128, 128], [16384, NS], [1, 128]]), ch_ld))
                kb.batch_add(kb.dma(q, Vc.a(0, [[128, NS], [1, 128]]), dram_ap(c_v, l * NS * 16384, [[128, 128], [16384, NS], [1, 128]]), ch_ld))
            kb.batch_end(ch_ld)
            for grp8 in range(NS * 2 // 8):
                ps = accbank()
                for q8 in range(8):
                    bj = grp8 * 8 + q8
                    for p in range(2):
                        kb.trx(ps.a(q8 * 64, [[1, 64]], p * 64, 64), Xc.a(bj * 64, [[1, 64]], p * 64, 64),
                               ident_f.a(p * 64, [[1, 64]], p * 64, 64), p)
                kb.cp(SMs.a(grp8 * 8 * 128, [[128, 8], [1, 64]], key=tuple(range(grp8 * 8, grp8 * 8 + 8))), ps.a(0, [[64, 8], [1, 64]]))
            ps = accbank()
            for p in range(2):
                kb.trx(ps.a(0, [[1, NS * 4]], p * 64, 64), Xn.a(0, [[1, 64]], 0, NS * 4), ident_f.a(0, [[1, NS * 4]], 0, NS * 4), p)
                kb.cp(nl.a(0, [[1, NS * 2]], p * 64, 64), ps.a(p, [[2, NS * 2]], p * 64, 64))
            kb.cp(SMs.a(64, [[128, NS * 2], [1, 64]], key=tuple(range(NS * 2))), nl.a(0, [[1, NS * 2], [0, 64]]))
            ps = accbank()
            for ch in range(4):
                kb.tr(ps.a(ch * 128, [[1, NS * 3]]), Xv.a(ch * 128, [[1, 128]], 0, NS * 3), ident_f.a(0, [[1, NS * 3]], 0, NS * 3))
            for ch in range(4):
                kb.cp(xp.a(ch * NS * 7, [[7, NS], [1, 3]], key=ch), ps.a(ch * 128, [[3, NS], [1, 3]]))
            for b in range(NS):
                ps = accbank()
                kb.tr(ps.a(0, [[1, 128]]), Kc.a(b * 128, [[1, 128]]), ident_f.a())
                kb.cp(KTc.a(b * 128, [[1, 128]]), ps.a(0, [[1, 128]]), eng=("act_cp" if b % 2 else "dve"))
            kb.cp(Vcb.a(), Vc.a())

        def store_sample_states(g, l):
            q = "sp"
            NS = NSAMP
            for grp8 in range(NS * 2 // 8):
                ps = accbank()
                for q8 in range(8):
                    bj = grp8 * 8 + q8
                    for p in range(2):
                        kb.trx(ps.a(q8 * 64, [[1, 64]], p * 64, 64), SMs.a(bj * 128, [[1, 64]], p * 64, 64, key=bj),
                               ident_f.a(p * 64, [[1, 64]], p * 64, 64), p)
                kb.cp(Xc.a(grp8 * 8 * 64, [[1, 512]]), ps.a(0, [[1, 512]]))
            kb.cp(nl.a(0, [[1, NS * 2]]), SMs.a(64, [[128, NS * 2]], key=tuple(range(NS * 2))))
            kb.cp(mout.a(0, [[1, NS]], 0, 4), mz.a(4, [[4, NS]], 0, 4))
            ps = accbank()
            for ch in range(4):
                kb.tr(ps.a(ch * 128, [[1, 128]], 0, NS * 3), cvs.a(ch * NS * 3, [[1, NS * 3]]), ident_f.a())
            kb.cp(cso.a(0, [[1, 512]], 0, NS * 3), ps.a(0, [[1, 512]], 0, NS * 3))
            kb.batch_begin(q, ch_st)
            with nc.allow_non_contiguous_dma(reason="state layouts"):
                for p in range(2):
                    kb.batch_add(kb.dma(q, dram_ap(o_hgrn_s, l * NS * 16384 + p * 4096, [[64, 64], [8192, NS * 2], [1, 64]]),
                                        SHs.a(0, [[64, NS * 2], [1, 64]], p * 64, 64, key=tuple(range(NS * 2))), ch_st))
                    kb.batch_add(kb.dma(q, dram_ap(o_c_s, l * NS * 16384 + p * 4096, [[64, 64], [8192, NS * 2], [1, 64]]),
                                        Xc.a(0, [[64, NS * 2], [1, 64]], p * 64, 64), ch_st))
                    kb.batch_add(kb.dma(q, dram_ap(o_n_s, l * NS * 256 + p * 64, [[1, 64], [128, NS * 2]]),
                                        nl.a(0, [[1, NS * 2]], p * 64, 64), ch_st))
                kb.batch_add(kb.dma(q, dram_ap(o_m_s, l * NS * 4, [[1, 4], [4, NS]]), mout.a(0, [[1, NS]], 0, 4), ch_st))
                kb.batch_add(kb.dma(q, dram_ap(o_conv_s, l * NS * 1536, [[512, NS * 3], [1, 512]]), cso.a(0, [[1, 512]], 0, NS * 3), ch_st))
                for b in range(NS):
                    kb.batch_add(kb.dma(q, dram_ap(o_k_s, (l * NS + b) * 16384 + 124 * 128, [[128, 4], [1, 128]]),
                                        Kc.a(b * 128, [[1, 128]], 0, 4), ch_st))
                    kb.batch_add(kb.dma(q, dram_ap(o_v_s, (l * NS + b) * 16384 + 124 * 128, [[128, 4], [1, 128]]),
                                        Vc.a(b * 128, [[1, 128]], 0, 4), ch_st))
            kb.batch_end(ch_st)
            kb.eng[q].dma_start(out=dram_ap(o_k_s, l * NS * 16384, [[16384, NS], [1, 124 * 128]]),
                                in_=dram_ap(c_k, l * NS * 16384 + 4 * 128, [[16384, NS], [1, 124 * 128]])).then_inc(ch_dd.sem, 16)
            ch_dd.val += 16
            kb.eng[q].dma_start(out=dram_ap(o_v_s, l * NS * 16384, [[16384, NS], [1, 124 * 128]]),
                                in_=dram_ap(c_v, l * NS * 16384 + 4 * 128, [[16384, NS], [1, 124 * 128]])).then_inc(ch_dd.sem, 16)
            ch_dd.val += 16

        def store_prompt_states(g, l):
            q = "sp"
            N = g.N
            s = g.seq
            S = STW
            ps = accbank()
            for j in range(2):
                for p in range(2):
                    kb.trx(ps.a(j * 64, [[1, 64]], p * 64, 64), S["SM"][l][j].a(0, [[1, 64]], p * 64, 64),
                           ident_f.a(p * 64, [[1, 64]], p * 64, 64), p)
            kb.cp(cst.a(0, [[1, 128]]), ps.a(0, [[1, 128]]))
            ps2 = accbank()
            for ch in range(4):
                kb.tr(ps2.a(ch * 128, [[1, 128]], 0, 3), S["cvh"][l].a(ch * 3, [[1, 3]]), ident_f.a())
            kb.cp(cvo.a(0, [[1, 512]], 0, 3), ps2.a(0, [[1, 512]], 0, 3))
            kb.batch_begin(q, ch_st)
            with nc.allow_non_contiguous_dma(reason="state layouts"):
                for j in range(2):
                    for p in range(2):
                        h = 2 * j + p
                        base = ((l * NSEQ + s) * 4 + h)
                        kb.batch_add(kb.dma(q, dram_ap(o_hgrn_p, base * 4096, [[64, 64], [1, 64]]),
                                            S["SH"][l][j].a(0, [[1, 64]], p * 64, 64), ch_st))
                        kb.batch_add(kb.dma(q, dram_ap(o_c_p, base * 4096, [[64, 64], [1, 64]]),
                                            cst.a(j * 64, [[1, 64]], p * 64, 64), ch_st))
                        kb.batch_add(kb.dma(q, dram_ap(o_n_p, base * 64, [[1, 64], [1, 1]]),
                                            S["SM"][l][j].a(64, [[1, 1]], p * 64, 64), ch_st))
                kb.batch_add(kb.dma(q, dram_ap(o_m_p, (l * NSEQ + s) * 4, [[1, 4], [1, 1]]), S["mst"][l].a(0, [[1, 1]], 0, 4), ch_st))
                kb.batch_add(kb.dma(q, dram_ap(o_conv_p, (l * NSEQ + s) * 1536, [[512, 3], [1, 512]]),
                                    cvo.a(0, [[1, 512]], 0, 3), ch_st))
                kb.batch_add(kb.dma(q, dram_ap(o_k_p, (l * NSEQ + s) * 16384, [[128, 128], [1, 128]]), kvout.a(0, [[1, 128]]), ch_st))
                kb.batch_add(kb.dma(q, dram_ap(o_v_p, (l * NSEQ + s) * 16384, [[128, 128], [1, 128]]), kvout.a(128, [[1, 128]]), ch_st))
            kb.batch_end(ch_st)

        def run_group(g):
            N = g.N
            S = STM if g.kind == "meta" else STW
            if g.kind == "prompt" and g.gi == 0:
                seq_init()
            gl = f"{g.kind[0]}{g.seq}{g.gi}"
            kb.lab = gl + ":loadx"
            load_x(g)
            for l in range(L):
                if g.kind == "sample":
                    kb.lab = gl + f":L{l}:ldst"
                    load_sample_states(g, l)
                kb.lab = gl + f":L{l}:A"
                acc_pool[0] = WIDE
                stageA(g, l, S)
                dead_tail = (g.kind == "meta" and l == L - 1)
                if not dead_tail:
                    emit_wout(l)
                kb.lab = gl + f":L{l}:prep"
                run_interleaved([stageA_tail(g, l), prepH(g, l), prepM(g, l, S)])
                kb.lab = gl + f":L{l}:tiles"
                acc_pool[0] = [0, 1, 2, 3]
                acc_rr[0] = 0
                for ti, (t0, n, chunks) in enumerate(g.tiles):
                    gens = []
                    for j in range(2):
                        if g.kind == "sample":
                            stH = (lambda b_, j_: (lambda p0, pn, off, w: SHs.a((b_ * 2 + j_) * 64 + off, [[1, w]], p0, pn, key=b_ * 2 + j_)))(ti, j)
                            stM = (lambda b_, j_: (lambda p0, pn, off, w: SMs.a((b_ * 2 + j_) * 128 + off, [[1, w]], p0, pn, key=b_ * 2 + j_)))(ti, j)
                        else:
                            stH = (lambda j_: (lambda p0, pn, off, w: S["SH"][l][j_].a(off, [[1, w]], p0, pn)))(j)
                            stM = (lambda j_: (lambda p0, pn, off, w: S["SM"][l][j_].a(off, [[1, w]], p0, pn)))(j)
                        gens.append(la_tile(0, j, g, ti, stH, 64))
                        gens.append(la_tile(1, j, g, ti, stM, 128))
                    run_interleaved(gens)
                for ti, (t0, n, chunks) in enumerate([] if dead_tail else g.tiles):
                    cur = (lambda t0_, n_, ti_: ((lambda p: KT.a(t0_, [[1, n_]], p * 64, 64)),
                                                 (lambda p: Vall.a(ti_ * 640 + 512 + p * 64, [[1, 64]], 0, n_, key=ti_)),
                                                 n_, (lambda nk, nq: maskC.a(0, [[0, 4], [1, nq]], 0, nk))))(t0, n, ti)
                    if g.kind == "meta":
                        kbs = [cur]
                    elif g.kind == "sample":
                        b = ti
                        prev = ((lambda p, b=b: KTc.a(b * 128, [[1, 128]], p * 64, 64)),
                                (lambda p, b=b: Vcb.a(b * 128 + p * 64, [[1, 64]])),
                                128, (lambda nk, nq: maskP.a(0, [[0, 4], [1, nq]], 0, nk)))
                        kbs = [prev, cur]
                    else:
                        if ti > 0:
                            pt0 = g.tiles[ti - 1][0]
                            prev = ((lambda p, pt0=pt0: KT.a(pt0, [[1, 128]], p * 64, 64)),
                                    (lambda p, ti=ti: Vall.a((ti - 1) * 640 + 512 + p * 64, [[1, 64]], key=ti - 1)),
                                    128, (lambda nk, nq: maskP.a(0, [[0, 4], [1, nq]], 0, nk)))
                        elif g.gi > 0:
                            prev = ((lambda p: STW["sK"][l].a(0, [[1, 128]], p * 64, 64)),
                                    (lambda p: STW["sV"][l].a(p * 64, [[1, 64]])),
                                    128, (lambda nk, nq: maskP.a(0, [[0, 4], [1, nq]], 0, nk)))
                        else:
                            prev = ((lambda p: STM["sK"][l].a(0, [[1, 16]], p * 64, 64)),
                                    (lambda p: STM["sV"][l].a(p * 64, [[1, 64]], 0, 16)),
                                    16, (lambda nk, nq: maskM.a(0, [[0, 4], [1, nq]], 0, nk)))
                        kbs = [prev, cur]
                    swa_tile(g, l, ti, kbs, bset=ti % 2, part="S")
                    if ti > 0:
                        swa_tile(g, l, ti - 1, None, bset=(ti - 1) % 2, part="N")
                if not dead_tail:
                    swa_tile(g, l, len(g.tiles) - 1, None, bset=(len(g.tiles) - 1) % 2, part="N")
                if g.kind == "meta":
                    kb.cp(STM["sK"][l].a(0, [[1, 16]]), KT.a(0, [[1, 16]]))
                    kb.cp(STM["sV"][l].a(0, [[1, 128]], 0, 16), Vall.a(512, [[1, 128]], 0, 16, key=0))
                elif g.kind == "prompt" and not g.last:
                    lt = len(g.tiles) - 1
                    kb.cp(STW["sK"][l].a(), KT.a(g.tiles[lt][0], [[1, 128]]))
                    kb.cp(STW["sV"][l].a(), Vall.a(lt * 640 + 512, [[1, 128]], key=lt))
                kb.lab = gl + f":L{l}:post"
                if not dead_tail:
                    post_all(g, l)
                if g.kind == "sample":
                    store_sample_states(g, l)
                if g.kind == "prompt" and g.last:
                    store_prompt_states(g, l)
                if dead_tail:
                    continue
                kb.lab = gl + f":L{l}:wout"
                wout_stage(g, l)
                kb.lab = gl + f":L{l}:ffn"
                ffn_stage(g, l)
            kb.lab = gl + ":final"
            if g.kind == "meta":
                rms_stats(g, None, flush=True)
            if g.kind != "meta":
                rmsnorm(g, gfin, 0, dst_bf=False, have_stats=True)
                store_y(g)

        for g in groups:
            run_group(g)

    _cp = kb.cp

    def cp2(out, in_, eng="dve"):
        if eng == "act_cp":
            return kb.act(out, in_, AF.Copy)
        return _cp(out, in_, eng)

    kb.cp = cp2

    build_consts()
    load_params()
    run_phase(groups1, 64 if NSAMP else 16)
    kb.barrier()
    if groups2:
        run_phase(groups2, 512)
    for c in kb.chans:
        if c.val:
            kb._wait("sp", c, c.val)
    kb.es.close()
    return kb


OUT_NAMES = ["y_prompt", "y_sample", "hgrn_p", "mlstm_c_p", "mlstm_n_p", "mlstm_m_p", "mlstm_conv_p", "swa_k_p", "swa_v_p",
             "hgrn_s", "mlstm_c_s", "mlstm_n_s", "mlstm_m_s", "mlstm_conv_s", "swa_k_s", "swa_v_s"]
PROMPT_OUT = set(OUT_NAMES[:1] + OUT_NAMES[2:9])
BATCH_AXIS = {"y_prompt": 0, "y_sample": 0}
SAMPLE_IN = ["state_hgrn", "state_mlstm_c", "state_mlstm_n", "state_mlstm_m", "state_mlstm_conv", "cache_swa_k", "cache_swa_v"]


def make_in_maps(inputs, ncores, nseq, nsamp):
    maps = []
    for c in range(ncores):
        m = {}
        for k_, v in inputs.items():
            v = np.asarray(v)
            if k_ == "x_prompt":
                m[k_] = np.ascontiguousarray(v[c * nseq:(c + 1) * nseq])
            elif k_ == "x_sample":
                m[k_] = np.ascontiguousarray(v[c * nsamp:(c + 1) * nsamp])
            elif k_ in SAMPLE_IN:
                m[k_] = np.ascontiguousarray(v[:, c * nsamp:(c + 1) * nsamp])
            else:
                m[k_] = np.ascontiguousarray(v)
        maps.append(m)
    return maps


def run(inputs, ncores=NCORES, NSEQ=2, NGRP=4, NSAMP=16, debug=False, trace=False):
    k1 = build(None, NSEQ, NGRP, NSAMP, debug)
    needed = k1.rec
    k2 = build(needed, NSEQ, NGRP, NSAMP, debug)
    in_maps = make_in_maps(inputs, ncores, NSEQ, NSAMP)
    res = run_bass_kernel_spmd(k2.nc, in_maps, core_ids=list(range(ncores)), trace=trace)
    outs = []
    for name in OUT_NAMES:
        ax = 0 if name in BATCH_AXIS else 1
        outs.append(np.concatenate([np.asarray(r[name]) for r in res.results], axis=ax).astype(np.float32))
    return tuple(outs), res


def kernel(**inputs):
    outs, _ = run(inputs)
    return outs
```

```python
import numpy as np
from contextlib import ExitStack
import concourse.bass as bass
import concourse.mybir as mybir
from concourse.bass_utils import run_bass_kernel_spmd

F32 = mybir.dt.float32
BF16 = mybir.dt.bfloat16
ALU = mybir.AluOpType
AF = mybir.ActivationFunctionType

D = 1024
DIN = 2824
DFF = 2816
L = 2
NCORES = 8
EPS = 1e-6


class Res:
    __slots__ = ("lw", "rd", "name")

    def __init__(self, name):
        self.lw = None
        self.rd = {}
        self.name = name


class A:
    __slots__ = ("ap", "res", "p0", "bank")

    def __init__(self, ap, res, p0=0):
        self.ap = ap
        self.res = res
        self.p0 = p0
        self.bank = None


class Chan:
    def __init__(self, sem, name):
        self.sem = sem
        self.val = 0
        self.name = name


class TT:
    def __init__(self, kb, name, F, dt, psum=False):
        self.kb = kb
        name = name + kb.tag
        self.name = name
        self.F = F
        self.dt = dt
        if psum:
            self.t = kb.es.enter_context(kb.nc.psum_tensor(name, [128, F], dt))
        else:
            self.t = kb.es.enter_context(kb.nc.sbuf_tensor(name, [128, F], dt))
        self.res = {}

    def r(self, key=0):
        if key not in self.res:
            self.res[key] = Res(f"{self.name}:{key}")
        return self.res[key]

    def a(self, off=0, dims=None, p0=0, pn=128, key=0):
        if dims is None:
            dims = [[1, self.F - off]]
        ap = bass.AP(self.t, p0 * self.F + off, [[self.F, pn]] + [list(d) for d in dims])
        if isinstance(key, tuple):
            return A(ap, tuple(self.r(k_) for k_ in key), p0)
        return A(ap, self.r(key), p0)


class Bank:
    def __init__(self, kb, name):
        self.t = kb.es.enter_context(kb.nc.psum_tensor(name, [128, 512], F32))
        self.tb = self.t.bitcast(BF16)
        self.res = Res(name)
        self.res_pe = None

    def a(self, off=0, dims=None, p0=0, pn=128, key=0, bf=False):
        F = 1024 if bf else 512
        if dims is None:
            dims = [[1, F - off]]
        ap = bass.AP(self.tb if bf else self.t, p0 * F + off, [[F, pn]] + [list(d) for d in dims])
        a = A(ap, self.res, p0)
        a.bank = self
        return a


class Arena:
    PAGE = 1024

    def __init__(self, kb, name, nbytes):
        nbytes = -(-nbytes // self.PAGE) * self.PAGE
        self.nbytes = nbytes
        self.t32 = kb.es.enter_context(kb.nc.sbuf_tensor(name + kb.tag, [128, nbytes // 4], F32))
        self.tb = self.t32.bitcast(BF16)
        self.pages = [Res(f"{name}:pg{i}") for i in range(nbytes // self.PAGE)]

    def view(self, boff, F, dt):
        return AV(self, boff, F, dt)


class AV:
    def __init__(self, ar, boff, F, dt):
        self.ar, self.boff, self.F, self.dt = ar, boff, F, dt
        self.es = 4 if dt == F32 else 2
        assert boff % self.es == 0
        self.h = ar.t32 if dt == F32 else ar.tb
        self.ps = ar.nbytes // self.es

    def a(self, off=0, dims=None, p0=0, pn=128, key=0):
        if dims is None:
            dims = [[1, self.F - off]]
        base = self.boff // self.es + off
        ap = bass.AP(self.h, p0 * self.ps + base, [[self.ps, pn]] + [list(d) for d in dims])
        span = sum((c - 1) * abs(st) for st, c in dims)
        b0 = base * self.es
        b1 = (base + span + 1) * self.es - 1
        pg = self.ar.pages[b0 // Arena.PAGE: b1 // Arena.PAGE + 1]
        return A(ap, tuple(pg), p0)


class KB:
    ENG = ("pe", "act", "dve", "pool")

    def __init__(self, needed=None):
        self.nc = bass.Bass("TRN2", target_bir_lowering=False)
        self.es = ExitStack()
        nc = self.nc
        self.eng = {"pe": nc.tensor, "act": nc.scalar, "dve": nc.vector, "pool": nc.gpsimd, "sp": nc.sync}
        self.sem = {e: self.es.enter_context(nc.semaphore("s_" + e)) for e in self.ENG}
        self.idx = {e: 0 for e in self.ENG}
        self.needed = needed
        if needed is not None:
            self.rank = {e: {i: r for r, i in enumerate(sorted(needed[e]))} for e in self.ENG}
        self.rec = {e: set() for e in self.ENG}
        self.waited = {e: {} for e in ("pe", "act", "dve", "pool", "sp")}
        self.chans = []
        self.tag = ""
        self.lab = ""
        self.labels = {e: [] for e in self.ENG}
        self.dbg = []
        self.ninst = 0

    def chan(self, name):
        c = Chan(self.es.enter_context(self.nc.semaphore("c_" + name)), name)
        self.chans.append(c)
        return c

    def _wait(self, eng, src, i):
        if isinstance(src, Chan):
            val, sem = i, src.sem
        else:
            self.rec[src].add(i)
            val = (i + 1) if self.needed is None else self.rank[src][i] + 1
            sem = self.sem[src]
        w = self.waited[eng]
        if w.get(src, 0) >= val:
            return
        w[src] = val
        self.eng[eng].wait_ge(sem, val)

    def _deps(self, eng, reads, writes, skip=None):
        for r in reads:
            if r.lw is not None:
                src, i = r.lw
                if (src == eng and eng == "pe") or src is skip:
                    continue
                self._wait(eng, src, i)
        for w in writes:
            if w.lw is not None:
                src, i = w.lw
                if (src != eng or eng != "pe") and src is not skip:
                    self._wait(eng, src, i)
            for src, i in w.rd.items():
                if (src != eng or eng != "pe") and src is not skip:
                    self._wait(eng, src, i)

    @staticmethod
    def _flat(xs):
        out = []
        for x in xs:
            if isinstance(x, A):
                x = x.res
            if isinstance(x, Res):
                out.append(x)
            elif isinstance(x, (tuple, list)):
                out.extend(x)
        return out

    def emit(self, eng, fn, reads, writes):
        writes = list(writes) + [x for x in reads if isinstance(x, A) and x.bank is not None]
        reads = self._flat(reads)
        writes = self._flat(writes)
        self._deps(eng, reads, writes)
        ins = fn()
        i = self.idx[eng]
        self.idx[eng] = i + 1
        self.labels[eng].append(self.lab)
        self.ninst += 1
        if self.needed is None or i in self.rank[eng]:
            ins.then_inc(self.sem[eng], 1)
        for r in reads:
            r.rd[eng] = i
        for w in writes:
            w.lw = (eng, i)
            w.rd = {}
        return (eng, i)

    def dma(self, q, out, in_, ch, **kw):
        reads = self._flat([in_]) if isinstance(in_, A) else []
        writes = self._flat([out]) if isinstance(out, A) else []
        self._deps(q, reads, writes, skip=ch)
        o = out.ap if isinstance(out, A) else out
        i_ = in_.ap if isinstance(in_, A) else in_
        self.eng[q].dma_start(out=o, in_=i_, **kw).then_inc(ch.sem, 16)
        ch.val += 16
        self.ninst += 1
        for r in reads:
            r.rd[ch] = ch.val
        for w in writes:
            w.lw = (ch, ch.val)
            w.rd = {}
        return reads + writes

    def batch_begin(self, q, ch):
        if ch.val:
            self._wait(q, ch, ch.val)
        self._batch = []

    def batch_add(self, touched):
        self._batch += touched

    def batch_end(self, ch):
        for r in self._batch:
            if r.lw is not None and r.lw[0] is ch:
                r.lw = (ch, ch.val)
            if ch in r.rd:
                r.rd[ch] = ch.val

    def barrier(self):
        last = {e: self.idx[e] - 1 for e in self.ENG}
        for e in ("pe", "act", "dve", "pool", "sp"):
            for s in self.ENG:
                if s != e and last[s] >= 0:
                    self._wait(e, s, last[s])
            for c in self.chans:
                if c.val:
                    self._wait(e, c, c.val)

    def _rowsync(self, out, kbase):
        b = out.bank
        if b is not None and b.res_pe is not None and b.res_pe[0] != kbase:
            self._wait("pe", "pe", b.res_pe[1])

    def mm(self, out, lhsT, rhs, start=True, stop=True, sgc=False):
        self._rowsync(out, lhsT.p0)
        t = self.emit("pe", lambda: self.nc.tensor.matmul(out.ap, lhsT=lhsT.ap, rhs=rhs.ap, start=start, stop=stop,
                                                          skip_group_check=sgc), [lhsT, rhs], [out])
        if out.bank is not None:
            out.bank.res_pe = (lhsT.p0, t[1])
        return t

    def tr(self, out, in_, ident):
        self._rowsync(out, in_.p0)
        t = self.emit("pe", lambda: self.nc.tensor.transpose(out.ap, in_.ap, ident.ap), [in_, ident], [out])
        if out.bank is not None:
            out.bank.res_pe = (in_.p0, t[1])
        return t

    def trx(self, out, in_, ident, p):
        if p == 0:
            return self.tr(out, in_, ident)
        return self.mm(out, in_, ident)

    def act(self, out, in_, func, bias=None, scale=1.0):
        kw = {}
        if bias is not None:
            kw["bias"] = bias.ap if isinstance(bias, A) else bias
        kw["scale"] = scale.ap if isinstance(scale, A) else scale
        return self.emit("act", lambda: self.nc.scalar.activation(out=out.ap, in_=in_.ap, func=func, **kw),
                         [in_, bias, scale], [out])

    def tt(self, out, in0, in1, op, eng="dve"):
        return self.emit(eng, lambda: self.eng[eng].tensor_tensor(out=out.ap, in0=in0.ap, in1=in1.ap, op=op),
                         [in0, in1], [out])

    def ts(self, out, in0, s1, s2, op0, op1=None, eng="dve"):
        a1 = s1.ap if isinstance(s1, A) else s1
        a2 = s2.ap if isinstance(s2, A) else s2
        if op1 is None:
            f = lambda: self.eng[eng].tensor_scalar(out=out.ap, in0=in0.ap, scalar1=a1, scalar2=None, op0=op0)
        else:
            f = lambda: self.eng[eng].tensor_scalar(out=out.ap, in0=in0.ap, scalar1=a1, scalar2=a2, op0=op0, op1=op1)
        return self.emit(eng, f, [in0, s1, s2], [out])

    def stt(self, out, in0, sc, in1, op0, op1, eng="dve"):
        a = sc.ap if isinstance(sc, A) else sc
        return self.emit(eng, lambda: self.eng[eng].scalar_tensor_tensor(out=out.ap, in0=in0.ap, scalar=a, in1=in1.ap,
                                                                          op0=op0, op1=op1), [in0, sc, in1], [out])

    def cp(self, out, in_, eng="dve"):
        return self.emit(eng, lambda: self.eng[eng].tensor_copy(out=out.ap, in_=in_.ap), [in_], [out])

    def ms(self, out, val, eng="dve"):
        return self.emit(eng, lambda: self.eng[eng].memset(out.ap, val), [], [out])

    def scan(self, out, d0, d1, init, op0, op1):
        a = init.ap if isinstance(init, A) else init
        return self.emit("dve", lambda: self.nc.vector.tensor_tensor_scan(out=out.ap, data0=d0.ap, data1=d1.ap, initial=a,
                                                                          op0=op0, op1=op1), [d0, d1, init], [out])

    def recip(self, out, in_):
        return self.emit("dve", lambda: self.nc.vector.reciprocal(out=out.ap, in_=in_.ap), [in_], [out])

    def asel(self, out, in_, pattern, op, fill, base, cm):
        return self.emit("pool", lambda: self.nc.gpsimd.affine_select(out=out.ap, in_=in_.ap, pattern=pattern, compare_op=op,
                                                                       fill=fill, base=base, channel_multiplier=cm),
                         [in_], [out])


class Grp:
    def __init__(self, kind, N, Lc, tiles, seq=0, gi=0, last=False):
        self.kind, self.N, self.Lc, self.tiles, self.seq, self.gi, self.last = kind, N, Lc, tiles, seq, gi, last
        self.nch = N // Lc


def dram_ap(t, off, dims):
    return bass.AP(t, off, [list(d) for d in dims])


CONV_ENG = "dve"
S1_ENG = "pool"


def build(needed=None, NSEQ=2, NGRP=4, NSAMP=16, debug=False):
    kb = KB(needed)
    nc = kb.nc
    SEQ = NGRP * 512
    NMAX = 512

    def din(name, shape):
        return nc.dram_tensor(name, list(shape), F32, kind="ExternalInput")

    def dout(name, shape):
        return nc.dram_tensor(name, list(shape), F32, kind="ExternalOutput")

    x_prompt = din("x_prompt", [NSEQ, SEQ, D])
    x_sample = din("x_sample", [NSAMP, 4, D])
    st_hgrn = din("state_hgrn", [L, NSAMP, 4, 64, 64])
    st_c = din("state_mlstm_c", [L, NSAMP, 4, 64, 64])
    st_n = din("state_mlstm_n", [L, NSAMP, 4, 64])
    st_m = din("state_mlstm_m", [L, NSAMP, 4])
    st_conv = din("state_mlstm_conv", [L, NSAMP, 3, 512])
    c_k = din("cache_swa_k", [L, NSAMP, 128, 2, 64])
    c_v = din("cache_swa_v", [L, NSAMP, 128, 2, 64])
    meta_tokens = din("meta_tokens", [16, D])
    norm_mix = din("norm_mix", [L, D])
    norm_ffn = din("norm_ffn", [L, D])
    norm_final = din("norm_final", [D])
    w_in = din("w_in", [L, D, DIN])
    lb_logits = din("hgrn_lb_logits", [L, 256])
    hgrn_norm = din("hgrn_norm", [L, 256])
    conv_w = din("mlstm_conv_w", [L, 4, 512])
    conv_b = din("mlstm_conv_b", [L, 512])
    i_bias = din("mlstm_i_bias", [L, 4])
    f_bias = din("mlstm_f_bias", [L, 4])
    mlstm_norm = din("mlstm_norm", [L, 256])
    sinks = din("attn_sinks", [L, 8])
    w_out = din("w_out", [L, D, D])
    w_ffn_in = din("w_ffn_in", [L, D, 2 * DFF])
    w_ffn_out = din("w_ffn_out", [L, DFF, D])

    y_prompt = dout("y_prompt", [NSEQ, SEQ, D])
    y_sample = dout("y_sample", [NSAMP, 4, D])
    o_hgrn_p = dout("hgrn_p", [L, NSEQ, 4, 64, 64])
    o_c_p = dout("mlstm_c_p", [L, NSEQ, 4, 64, 64])
    o_n_p = dout("mlstm_n_p", [L, NSEQ, 4, 64])
    o_m_p = dout("mlstm_m_p", [L, NSEQ, 4])
    o_conv_p = dout("mlstm_conv_p", [L, NSEQ, 3, 512])
    o_k_p = dout("swa_k_p", [L, NSEQ, 128, 2, 64])
    o_v_p = dout("swa_v_p", [L, NSEQ, 128, 2, 64])
    o_hgrn_s = dout("hgrn_s", [L, NSAMP, 4, 64, 64])
    o_c_s = dout("mlstm_c_s", [L, NSAMP, 4, 64, 64])
    o_n_s = dout("mlstm_n_s", [L, NSAMP, 4, 64])
    o_m_s = dout("mlstm_m_s", [L, NSAMP, 4])
    o_conv_s = dout("mlstm_conv_s", [L, NSAMP, 3, 512])
    o_k_s = dout("swa_k_s", [L, NSAMP, 128, 2, 64])
    o_v_s = dout("swa_v_s", [L, NSAMP, 128, 2, 64])

    def sb(name, F, dt=F32):
        return TT(kb, name, F, dt)

    ident_f = sb("ident_f", 128)
    ident_b = sb("ident_b", 128, BF16)
    ones_b = sb("ones_b", 128, BF16)
    ones_f = sb("ones_f", 128)
    bones_b = sb("bones_b", 128, BF16)
    maskA = sb("maskA", 128, BF16)
    maskC = sb("maskC", 128, BF16)
    maskP = sb("maskP", 128, BF16)
    maskM = sb("maskM", 128, BF16)
    sel = sb("sel", 256)
    epsT = sb("epsT", 1)
    zerosT = sb("zerosT", NMAX)
    scr = sb("scr", 256)
    gmix = sb("gmix", L * 8)
    gffn = sb("gffn", L * 8)
    gfin = sb("gfin", 8)
    lbT = sb("lbT", L * 2)
    omlT = sb("omlT", L * 2)
    nomlT = sb("nomlT", L * 2)
    lgT = sb("lgT", L * 2)
    hnormT = sb("hnormT", L * 2)
    mnormT = sb("mnormT", L * 2)
    cwT = sb("cwT", L * 16)
    cbT = sb("cbT", L * 4)
    ibT = sb("ibT", L)
    fbT = sb("fbT", L)
    skrow = sb("skrow", 16)
    esink = sb("esink", L * 4)

    WIN = [sb(f"win{i}", 8 * 512, BF16) for i in range(2)]
    WOUT = sb("woutb", 8 * 1024, BF16)
    NFIN, NFOUT = 3, 3
    FIN = [sb(f"fin{i}", 2 * 8 * 256, BF16) for i in range(NFIN)]
    FOUT = [sb(f"fout{i}", 22 * 128, BF16) for i in range(NFOUT)]
    ch_win = [kb.chan(f"win{i}") for i in range(2)]
    ch_wout = kb.chan("wout")
    ch_fin = [kb.chan(f"fin{i}") for i in range(NFIN)]
    ch_fout = [kb.chan(f"fout{i}") for i in range(NFOUT)]
    ch_par = kb.chan("par")
    ch_x = [kb.chan("x0"), kb.chan("x1")]
    ch_y = [kb.chan("y0"), kb.chan("y1")]
    ch_st = kb.chan("stout")
    ch_ld = kb.chan("stld")
    ch_dd = kb.chan("d2d")

    def stset(tag):
        return dict(
            SH=[[sb(f"SH{tag}{l}{j}", 64) for j in range(2)] for l in range(L)],
            SM=[[sb(f"SM{tag}{l}{j}", 128) for j in range(2)] for l in range(L)],
            mst=[sb(f"mst{tag}{l}", 1) for l in range(L)],
            cvh=[sb(f"cvh{tag}{l}", 12) for l in range(L)],
            sK=[sb(f"sK{tag}{l}", 128, BF16) for l in range(L)],
            sV=[sb(f"sV{tag}{l}", 128, BF16) for l in range(L)],
        )

    STM = stset("M")
    STW = stset("W")

    PB = [Bank(kb, f"pb{i}") for i in range(8)]

    def build_consts():
        kb.ms(ident_f.a(), 1.0, eng="pool")
        kb.asel(ident_f.a(), ident_f.a(), [[-1, 128]], ALU.is_equal, 0.0, 0, 1)
        kb.cp(ident_b.a(), ident_f.a(), eng="pool")
        kb.ms(ones_b.a(), 1.0, eng="pool")
        kb.ms(ones_f.a(), 1.0, eng="pool")
        kb.ms(bones_b.a(), 1.0, eng="pool")
        kb.ms(bones_b.a(64, [[1, 64]], 0, 64), 0.0, eng="pool")
        kb.ms(bones_b.a(0, [[1, 64]], 64, 64), 0.0, eng="pool")
        t = scr
        kb.ms(t.a(0, [[1, 128]]), 1.0, eng="pool")
        kb.asel(t.a(0, [[1, 128]]), t.a(0, [[1, 128]]), [[1, 128]], ALU.is_ge, 0.0, 0, -1)
        kb.cp(maskC.a(), t.a(0, [[1, 128]]), eng="pool")
        kb.cp(maskA.a(), t.a(0, [[1, 128]]), eng="pool")
        kb.ms(maskA.a(64, [[1, 64]], 0, 64), 0.0, eng="pool")
        kb.ms(t.a(0, [[1, 128]]), 1.0, eng="pool")
        kb.asel(t.a(0, [[1, 128]]), t.a(0, [[1, 128]]), [[-1, 128]], ALU.is_gt, 0.0, 0, 1)
        kb.cp(maskP.a(), t.a(0, [[1, 128]]), eng="pool")
        kb.ms(t.a(0, [[1, 128]]), 1.0, eng="pool")
        kb.asel(t.a(0, [[1, 128]]), t.a(0, [[1, 128]]), [[-1, 128]], ALU.is_gt, 0.0, 112, 1)
        kb.cp(maskM.a(), t.a(0, [[1, 128]]), eng="pool")
        kb.ms(sel.a(), 1.0, eng="pool")
        kb.asel(sel.a(), sel.a(), [[1, 256]], ALU.is_ge, 0.0, 0, -64)
        kb.asel(sel.a(), sel.a(), [[-1, 256]], ALU.is_ge, 0.0, 63, 64)
        kb.ms(epsT.a(), EPS, eng="pool")
        kb.ms(zerosT.a(), 0.0, eng="pool")
        for S in (STM,):
            for l in range(L):
                for j in range(2):
                    kb.ms(S["SH"][l][j].a(), 0.0, eng="pool")
                    kb.ms(S["SM"][l][j].a(), 0.0, eng="pool")
                kb.ms(S["mst"][l].a(), 0.0, eng="pool")
                kb.ms(S["cvh"][l].a(), 0.0, eng="pool")

    def load_params():
        q = "sp"
        kb.batch_begin(q, ch_par)
        with nc.allow_non_contiguous_dma(reason="tiny param layouts"):
            def ld(dst, src):
                kb.batch_add(kb.dma(q, dst, src, ch_par))
            for l in range(L):
                ld(gmix.a(l * 8, [[1, 8]]), dram_ap(norm_mix, l * D, [[1, 128], [128, 8]]))
                ld(gffn.a(l * 8, [[1, 8]]), dram_ap(norm_ffn, l * D, [[1, 128], [128, 8]]))
                ld(lgT.a(l * 2, [[1, 2]]), dram_ap(lb_logits, l * 256, [[1, 128], [128, 2]]))
                ld(hnormT.a(l * 2, [[1, 2]]), dram_ap(hgrn_norm, l * 256, [[1, 128], [128, 2]]))
                ld(mnormT.a(l * 2, [[1, 2]]), dram_ap(mlstm_norm, l * 256, [[1, 128], [128, 2]]))
                for j in range(4):
                    ld(cwT.a(l * 16 + j * 4, [[1, 4]]), dram_ap(conv_w, (l * 4 + j) * 512, [[1, 128], [128, 4]]))
                ld(cbT.a(l * 4, [[1, 4]]), dram_ap(conv_b, l * 512, [[1, 128], [128, 4]]))
                ld(ibT.a(l, [[1, 1]], 0, 4), dram_ap(i_bias, l * 4, [[1, 4], [1, 1]]))
                ld(fbT.a(l, [[1, 1]], 0, 4), dram_ap(f_bias, l * 4, [[1, 4], [1, 1]]))
            ld(gfin.a(), dram_ap(norm_final, 0, [[1, 128], [128, 8]]))
            ld(skrow.a(0, [[1, 16]], 0, 1), dram_ap(sinks, 0, [[16, 1], [1, 16]]))
        kb.batch_end(ch_par)
        t = scr
        l0, l1 = lgT.a(0, [[1, 2]]), lgT.a(2, [[1, 2]])
        mx, e0, e1, s_, r_ = (t.a(i * 2, [[1, 2]]) for i in range(5))
        kb.tt(mx, l0, l1, ALU.max)
        kb.tt(e0, l0, mx, ALU.subtract)
        kb.tt(e1, l1, mx, ALU.subtract)
        kb.act(e0, e0, AF.Exp)
        kb.act(e1, e1, AF.Exp)
        kb.tt(s_, e0, e1, ALU.add)
        kb.recip(r_, s_)
        kb.tt(e0, e0, r_, ALU.mult)
        kb.tt(e1, e1, r_, ALU.mult)
        kb.tt(lbT.a(0, [[1, 2]]), e0, e0, ALU.subtract)
        kb.tt(s_, e0, e1, ALU.add)
        kb.tt(lbT.a(2, [[1, 2]]), s_, e0, ALU.subtract)
        kb.ts(omlT.a(), lbT.a(), -1.0, 1.0, ALU.mult, ALU.add)
        kb.ts(nomlT.a(), omlT.a(), -1.0, None, ALU.mult)
        ps = PB[0]
        kb.mm(ps.a(0, [[1, 16]]), ones_f.a(0, [[1, 128]], 0, 1), skrow.a(0, [[1, 16]], 0, 1))
        for l in range(L):
            for p in range(2):
                kb.act(esink.a(l * 4, [[1, 4]], p * 64, 64), ps.a(l * 8 + p * 4, [[1, 4]], p * 64, 64), AF.Exp)

    wl_win, wl_fin, wl_fout = [], [], []
    state = dict(win_i=0, fin_i=0, fout_i=0, win_e=0, fin_e=0, fout_e=0)

    def wsrc(t, l, rows, col0, ncol, row0=0, nk=8, pn=128, ld=None):
        return dram_ap(t, (l * rows + row0) * ld + col0, [[ld, pn], [128 * ld, nk], [1, ncol]])

    def win_blocks(l):
        W = lambda c0, nc_: wsrc(w_in, l, D, c0, nc_, ld=DIN)
        b = []
        b.append([((0, 512), W(0, 512))])
        b.append([((0, 256), W(768, 256)), ((256, 256), W(1792, 256))])
        b.append([((0, 512), W(1024, 512))])
        f4 = []
        for p in range(2):
            for j in range(4):
                f4.append(((j * 128 + p * 64, 64), W(2056 + (j + 4 * p) * 64, 64)))
        b.append([((0, 128), W(2568, 128)), ((128, 8), W(2048, 8))])
        b.append(f4)
        b.append([((0, 256), W(512, 256)), ((256, 256), W(1536, 256))])
        b.append([((0, 256), W(2568, 256))])
        return b

    def emit_win(i):
        l, blk = wl_win[i]
        slot = i % 2
        T = WIN[slot]
        for (c0, ncol), src in win_blocks(l)[blk]:
            dst = T.a(c0, [[512, 8], [1, ncol]])
            kb.dma("pool", dst, src, ch_win[slot])

    def emit_wout(l):
        T = WOUT
        kb.dma("pool", T.a(0, [[1024, 4], [1, 1024]]), wsrc(w_out, l, D, 0, 1024, nk=4, ld=D), ch_wout)
        for p in range(2):
            src = dram_ap(w_out, (l * D + 512 + p * 256) * D, [[D, 64], [64 * D, 4], [1, 1024]])
            kb.dma("pool", T.a(4 * 1024, [[1024, 4], [1, 1024]], p * 64, 64), src, ch_wout)

    def emit_fin(i):
        l, b = wl_fin[i]
        slot = i % NFIN
        T = FIN[slot]
        kb.dma("pool", T.a(0, [[256, 8], [1, 256]]), wsrc(w_ffn_in, l, D, b * 256, 256, ld=2 * DFF), ch_fin[slot])
        kb.dma("pool", T.a(2048, [[256, 8], [1, 256]]), wsrc(w_ffn_in, l, D, DFF + b * 256, 256, ld=2 * DFF), ch_fin[slot])

    def emit_fout(i):
        l, dc = wl_fout[i]
        slot = i % NFOUT
        T = FOUT[slot]
        kb.dma("pool", T.a(0, [[128, 22], [1, 128]]), wsrc(w_ffn_out, l, DFF, dc * 128, 128, nk=22, ld=D), ch_fout[slot])

    def get_win():
        i = state["win_i"]
        state["win_i"] += 1
        while state["win_e"] < min(len(wl_win), i + 2):
            emit_win(state["win_e"])
            state["win_e"] += 1
        return WIN[i % 2]

    def get_fin():
        i = state["fin_i"]
        state["fin_i"] += 1
        while state["fin_e"] < min(len(wl_fin), i + NFIN):
            emit_fin(state["fin_e"])
            state["fin_e"] += 1
        return FIN[i % NFIN]

    def get_fout():
        i = state["fout_i"]
        state["fout_i"] += 1
        while state["fout_e"] < min(len(wl_fout), i + NFOUT):
            emit_fout(state["fout_e"])
            state["fout_e"] += 1
        return FOUT[i % NFOUT]

    groups1 = [Grp("meta", 16, 16, [(0, 16, [(0, 0, 16)])])]
    if NSAMP:
        groups1.append(Grp("sample", NSAMP * 4, 4, [(b * 4, 4, [(b, 0, 4)]) for b in range(NSAMP)]))
    groups2 = []
    for s in range(NSEQ):
        for gi in range(NGRP):
            groups2.append(Grp("prompt", 512, 64, [(i * 128, 128, [(2 * i, 0, 64), (2 * i + 1, 64, 128)]) for i in range(4)],
                               seq=s, gi=gi, last=(gi == NGRP - 1)))
    for g in groups1 + groups2:
        for l in range(L):
            for blk in range(7):
                wl_win.append((l, blk))
            if g.kind == "meta" and l == L - 1:
                continue
            for b in range(11):
                wl_fin.append((l, b))
            for dc in range(8):
                wl_fout.append((l, dc))

    dbg_out = {}

    def dump(name, a, shape):
        if not debug:
            return
        t = nc.dram_tensor("dbg_" + name, list(shape), a.ap.dtype if hasattr(a.ap, "dtype") else F32, kind="ExternalOutput")
        c = kb.chan("dbg_" + name)
        kb.dma("sp", t.ap(), a, c)
        dbg_out[name] = c

    def run_phase(groups, NM):
        with ExitStack() as es2:
            old_es = kb.es
            kb.es = es2
            kb.tag = f"_n{NM}"
            try:
                _run_phase(groups, NM)
            finally:
                kb.es = old_es
                kb.tag = ""

    def _run_phase(groups, NM):
        NT = max(len(g.tiles) for g in groups)
        hT = sb("hT", 8 * NM)
        hn = sb("hn", 8 * NM, BF16)
        XPW = max((g.N + 3 * (len(g.tiles) if g.kind == "sample" else 1)) for g in groups)
        arena = Arena(kb, "arena", max(32 * NM + 16 * XPW, 44 * NM, 25600))
        actT = arena.view(0, 22 * NM, BF16)
        xs = [arena.view(4096 * i, 1024, F32) for i in range(2)]
        sqb = [sb(f"sqb{i}", NM, BF16) for i in range(2)]
        sqh = sb("sqh", NM, BF16)
        sqh2 = sb("sqh2", NM, BF16)
        lnv = sb("lnv", NM)
        rstd = sb("rstd", NM)
        qs = [arena.view(4 * NM * j, NM, F32) for j in range(2)]
        th = [arena.view(8 * NM + 4 * NM * j, NM, F32) for j in range(2)]
        ga = [sb(f"ga{j}", NM, BF16) for j in range(2)]
        go = [sb(f"go{j}", NM, BF16) for j in range(2)]
        xp = arena.view(16 * NM, 4 * XPW, F32)
        cvo = arena.view(2048, 512, F32)
        co = arena.view(16 * NM + 16 * XPW, 4 * NM, F32)
        QT = sb("QT", 4 * NM, BF16)
        KT = sb("KT", NM, BF16)
        igr = sb("igr", NM)
        fgr = sb("fgr", NM)
        Vall = sb("Vall", NT * 640, BF16)
        kvout = sb("kvout", 256)
        T = [sb(f"T{i}", NM + 1) for i in range(6)]
        R = [sb(f"R{i}", NM + 1) for i in range(2)]
        Bm = sb("Bm", NM + 1)
        mz = sb("mz", NM + 1)
        Qh = [[sb(f"Qh{x}{j}", NM, BF16) for j in range(2)] for x in range(2)]
        Kt = [[sb(f"Kt{x}{j}", NM, BF16) for j in range(2)] for x in range(2)]
        Kh = [[sb(f"Kh{x}{j}", NM, BF16) for j in range(2)] for x in range(2)]
        dec = [[sb(f"dec{x}{j}", 16) for j in range(2)] for x in range(2)]
        smid = [sb(f"smid{j}", 16) for j in range(2)]
        em = [arena.view(16 * NM + 4 * NM * j, NM, F32) for j in range(2)]
        oall = [[arena.view(4 * NM * (2 * x + j), NM, F32) for j in range(2)] for x in range(2)]
        dall = [arena.view(24 * NM + 4 * NM * j, NM, F32) for j in range(2)]
        att_sb = [sb(f"att{i}", 256, BF16) for i in range(4)]
        khtok = [sb(f"khtok{i}", 128, BF16) for i in range(4)]
        sbf = [sb(f"sbf{i}", 128, BF16) for i in range(4)]
        s1 = [sb(f"s1{i}", 128) for i in range(4)]
        pesb = [arena.view(18432 + 1024 * i, 512, BF16) for i in range(4)]
        tdc = arena.view(16384, 512, F32)
        cst = arena.view(0, 512, F32)
        has_sample = any(g.kind == "sample" for g in groups)
        if has_sample:
            NS = NSAMP
            SHs = sb("SHs", NS * 2 * 64)
            SMs = sb("SMs", NS * 2 * 128)
            Xc = arena.view(8192, NS * 2 * 64, F32)
            Xn = sb("Xn", 64)
            nl = sb("nl", NS * 2)
            m0 = sb("m0", NS)
            Xv = sb("Xv", 512)
            Kc = sb("Kc", NS * 128)
            Vc = sb("Vc", NS * 128)
            KTc = sb("KTc", NS * 128, BF16)
            Vcb = sb("Vcb", NS * 128, BF16)
            mout = sb("mout", NS)
            cvs = sb("cvs", 4 * NS * 3)
            cso = sb("cso", 512)

        print(f"[sbuf] phase NM={NM}: bytes remaining per partition = {kb.nc.sbuf_bytes_remaining}") if needed is None else None
        acc_rr = [0]

        acc_pool = [[0, 1, 2, 3]]

        def accbank():
            acc_rr[0] = (acc_rr[0] + 1) % len(acc_pool[0])
            return PB[acc_pool[0][acc_rr[0]]]

        WIDE = [0, 1, 2, 3, 5, 6, 7]

        rms_pending = []

        def rms_stats(g, c, flush=False):
            N = g.N
            ps = PB[4]
            if c is not None:
                s = sqb[c % 2]
                kb.act(s.a(0, [[1, N]]), hT.a(c * NM, [[1, N]], key=c), AF.Square)
                rms_pending.append((c, s))
            while rms_pending and (flush or len(rms_pending) > 1):
                c_, s_ = rms_pending.pop(0)
                kb.mm(ps.a(0, [[1, N]]), ones_b.a(), s_.a(0, [[1, N]]), start=(c_ == 0), stop=(c_ == 7))

        def rmsnorm(g, gam, goff, dst_bf=True, have_stats=False):
            N = g.N
            ps = PB[4]
            if not have_stats:
                for c in range(8):
                    rms_stats(g, c)
            rms_stats(g, None, flush=True)
            kb.act(lnv.a(0, [[1, N]]), ps.a(0, [[1, N]]), AF.Ln, bias=epsT.a(), scale=1.0 / D)
            kb.act(rstd.a(0, [[1, N]]), lnv.a(0, [[1, N]]), AF.Exp, scale=-0.5)
            for c in range(8):
                dst = hn.a(c * NM, [[1, N]], key=c) if dst_bf else hT.a(c * NM, [[1, N]], key=c)
                kb.stt(dst, hT.a(c * NM, [[1, N]], key=c), gam.a(goff + c, [[1, 1]]), rstd.a(0, [[1, N]]), ALU.mult, ALU.mult)

        def fm(W, col, M, ps, N):
            for k_ in range(8):
                kb.mm(ps.a(0, [[1, N]], 0, M), W.a(k_ * 512 + col, [[1, M]]), hn.a(k_ * NM, [[1, N]], key=k_),
                      start=(k_ == 0), stop=(k_ == 7))

        def tm(W, col, ncol, ps, t0, n):
            for k_ in range(8):
                kb.mm(ps.a(0, [[1, ncol]], 0, n), hn.a(k_ * NM + t0, [[1, n]], key=k_), W.a(k_ * 512 + col, [[1, ncol]]),
                      start=(k_ == 0), stop=(k_ == 7))

        def load_x(g):
            N = g.N
            if g.kind == "prompt":
                srcs = [(t0, n, dram_ap(x_prompt, (g.seq * SEQ + g.gi * 512 + t0) * D, [[D, n], [1, D]])) for (t0, n, _) in g.tiles]
            elif g.kind == "meta":
                srcs = [(0, 16, dram_ap(meta_tokens, 0, [[D, 16], [1, D]]))]
            else:
                srcs = [(0, N, dram_ap(x_sample, 0, [[D, N], [1, D]]))]
            for i, (t0, n, src) in enumerate(srcs):
                s = i % 2
                kb.dma("sp", xs[s].a(0, [[1, 1024]], 0, n), src, ch_x[s])
                for half in range(2):
                    ps = accbank()
                    for c4 in range(4):
                        c = half * 4 + c4
                        kb.tr(ps.a(c4 * 128, [[1, n]]), xs[s].a(c * 128, [[1, 128]], 0, n), ident_f.a(0, [[1, n]], 0, n))
                    for c4 in range(4):
                        c = half * 4 + c4
                        kb.cp(hT.a(c * NM + t0, [[1, n]], key=c), ps.a(c4 * 128, [[1, n]]), eng=("dve" if c4 % 2 else "act_cp"))

        def store_y(g):
            N = g.N
            if g.kind == "prompt":
                dsts = [(t0, n, dram_ap(y_prompt, (g.seq * SEQ + g.gi * 512 + t0) * D, [[D, n], [1, D]])) for (t0, n, _) in g.tiles]
            else:
                dsts = [(0, N, dram_ap(y_sample, 0, [[D, N], [1, D]]))]
            for i, (t0, n, dst) in enumerate(dsts):
                s = i % 2
                for half in range(2):
                    ps = accbank()
                    for c4 in range(4):
                        c = half * 4 + c4
                        kb.tr(ps.a(c4 * 128, [[1, 128]], 0, n), hT.a(c * NM + t0, [[1, n]], key=c), ident_f.a())
                    kb.cp(xs[s].a(half * 512, [[1, 512]], 0, n), ps.a(0, [[1, 512]], 0, n), eng=("dve" if half else "act_cp"))
                kb.dma("sp", dst, xs[s].a(0, [[1, 1024]], 0, n), ch_y[s])

        def stageA(g, l, S):
            N, Lc, nch = g.N, g.Lc, g.nch
            rmsnorm(g, gmix, l * 8, have_stats=(l > 0))
            W = get_win()
            for j in range(2):
                ps = accbank()
                fm(W, j * 128, 128, ps, N)
                kb.act(qs[j].a(0, [[1, N]]), ps.a(0, [[1, N]]), AF.Silu)
            for j in range(2):
                ps = accbank()
                fm(W, 256 + j * 128, 128, ps, N)
                kb.act(th[j].a(0, [[1, N]]), ps.a(0, [[1, N]]), AF.Tanh, scale=0.5)
            W = get_win()
            for j in range(2):
                ps = accbank()
                fm(W, j * 128, 128, ps, N)
                kb.act(ga[j].a(0, [[1, N]]), ps.a(0, [[1, N]]), AF.Silu)
            for j in range(2):
                ps = accbank()
                fm(W, 256 + j * 128, 128, ps, N)
                kb.act(T[0].a(0, [[1, N]]), ps.a(0, [[1, N]]), AF.Tanh, scale=0.5)
                kb.ts(go[j].a(0, [[1, N]]), T[0].a(0, [[1, N]]), 0.5, 0.5, ALU.mult, ALU.add)
            W = get_win()
            nsg = len(g.tiles) if g.kind == "sample" else 1
            Ls = N // nsg
            SW = Ls + 3
            XW = nsg * SW
            for ch in range(4):
                if g.kind == "sample":
                    pass
                else:
                    kb.cp(xp.a(ch * XW, [[1, 3]], key=ch), S["cvh"][l].a(ch * 3, [[1, 3]]))
                ps = accbank()
                fm(W, ch * 128, 128, ps, N)
                kb.cp(xp.a(ch * XW + 3, [[SW, nsg], [1, Ls]], key=ch), ps.a(0, [[Ls, nsg], [1, Ls]]), eng="act_cp")
                cw = lambda j_: cwT.a(l * 16 + j_ * 4 + ch, [[1, 1]])
                o_ = co.a(ch * NM, [[Ls, nsg], [1, Ls]], key=ch)
                kb.ts(o_, xp.a(ch * XW, [[SW, nsg], [1, Ls]], key=ch), cw(0), cbT.a(l * 4 + ch, [[1, 1]]), ALU.mult, ALU.add, eng=CONV_ENG)
                for j_ in range(1, 4):
                    kb.stt(o_, xp.a(ch * XW + j_, [[SW, nsg], [1, Ls]], key=ch), cw(j_), o_, ALU.mult, ALU.add, eng=CONV_ENG)
                kb.act(co.a(ch * NM, [[1, N]], key=ch), co.a(ch * NM, [[1, N]], key=ch), AF.Silu)
                if g.kind != "sample":
                    kb.cp(S["cvh"][l].a(ch * 3, [[1, 3]]), xp.a(ch * XW + N, [[1, 3]], key=ch))
                else:
                    kb.cp(cvs.a(ch * nsg * 3, [[3, nsg], [1, 3]]), xp.a(ch * XW + 4, [[SW, nsg], [1, 3]], key=ch))
            W = get_win()
            ps = accbank()
            fm(W, 0, 128, ps, N)
            kb.cp(KT.a(0, [[1, N]]), ps.a(0, [[1, N]]))
            ps = accbank()
            fm(W, 128, 4, ps, N)
            kb.ts(igr.a(0, [[1, N]], 0, 4), ps.a(0, [[1, N]], 0, 4), ibT.a(l, [[1, 1]], 0, 4), None, ALU.add)
            ps = accbank()
            fm(W, 132, 4, ps, N)
            kb.ts(fgr.a(0, [[1, N]], 0, 4), ps.a(0, [[1, N]], 0, 4), fbT.a(l, [[1, 1]], 0, 4), None, ALU.add)

        def stageA_tail(g, l):
            N = g.N
            W = get_win()
            for j in range(4):
                ps = accbank()
                fm(W, j * 128, 128, ps, N)
                kb.cp(QT.a(j * NM, [[1, N]], key=j), ps.a(0, [[1, N]]), eng=("dve" if j % 2 else "act_cp"))
                yield
            W = get_win()
            for ti, (t0, n, _) in enumerate(g.tiles):
                ps = accbank()
                tm(W, 0, 512, ps, t0, n)
                kb.cp(Vall.a(ti * 640, [[1, 512]], 0, n, key=ti), ps.a(0, [[1, 512]], 0, n), eng=("act_cp" if ti % 2 else "dve"))
                yield
            W = get_win()
            for ti, (t0, n, _) in enumerate(g.tiles):
                ps = accbank()
                tm(W, 0, 256, ps, t0, n)
                kb.cp(Vall.a(ti * 640 + 512, [[1, 128]], 0, n, key=ti), ps.a(128, [[1, 128]], 0, n), eng=("dve" if ti % 2 else "act_cp"))
                if g.kind == "sample":
                    kb.cp(Kc.a(ti * 128, [[1, 128]], 0, n), ps.a(0, [[1, 128]], 0, n))
                    kb.cp(Vc.a(ti * 128, [[1, 128]], 0, n), ps.a(128, [[1, 128]], 0, n))
                elif g.kind == "prompt" and g.last and ti == len(g.tiles) - 1:
                    kb.cp(kvout.a(0, [[1, 256]], 0, n), ps.a(0, [[1, 256]], 0, n))
                yield

        def prepH(g, l):
            N, Lc, nch = g.N, g.Lc, g.nch
            mi = Lc // 2 - 1
            tsig, tkk, tf, Bz, tbp, teb = T
            v = lambda t_, o=0: t_.a(o, [[1, N]])
            v3 = lambda t_, o=0: t_.a(o, [[Lc, nch], [1, Lc]])
            bc = lambda t_, o: t_.a(o, [[Lc, nch], [0, Lc]])
            cv_ = lambda t_, o: t_.a(o, [[Lc, nch]])
            for j in range(2):
                pj = lambda P_: P_.a(l * 2 + j, [[1, 1]])
                kb.ts(v(tsig), v(th[j]), 0.5, 0.5, ALU.mult, ALU.add)
                kb.ts(v(tkk), v(tsig), pj(nomlT), pj(omlT), ALU.mult, ALU.add)
                kb.ts(v(tf), v(tsig), pj(omlT), pj(lbT), ALU.mult, ALU.add)
                kb.ts(v(tf), v(tf), 1e-30, None, ALU.max)
                yield
                kb.act(v(tf), v(tf), AF.Ln)
                kb.ms(Bz.a(0, [[1, 1]]), 0.0)
                kb.scan(v(Bz, 1), v(tf), zerosT.a(0, [[1, N]]), 0.0, ALU.add, ALU.add)
                kb.tt(v3(tbp), v3(Bz, 1), bc(Bz, 1 + mi), ALU.subtract)
                yield
                kb.act(v(teb), v(tbp), AF.Exp)
                kb.act(v(tbp), v(tbp), AF.Exp, scale=-1.0)
                kb.tt(Qh[0][j].a(0, [[1, N]]), v(qs[j]), v(teb), ALU.mult)
                kb.tt(Kt[0][j].a(0, [[1, N]]), v(tkk), v(tbp), ALU.mult)
                yield
                kb.tt(Kh[0][j].a(0, [[Lc, nch], [1, Lc]]), Kt[0][j].a(0, [[Lc, nch], [1, Lc]]), bc(teb, Lc - 1), ALU.mult)
                kb.tt(tsig.a(0, [[1, nch]]), cv_(Bz, Lc), cv_(Bz, 0), ALU.subtract)
                kb.act(dec[0][j].a(0, [[1, nch]]), tsig.a(0, [[1, nch]]), AF.Exp)
                kb.tt(tsig.a(0, [[1, nch]]), cv_(Bz, 1 + mi), cv_(Bz, 0), ALU.subtract)
                kb.act(smid[j].a(0, [[1, nch]]), tsig.a(0, [[1, nch]]), AF.Exp)
                yield

        def prepM(g, l, S):
            N, Lc, nch = g.N, g.Lc, g.nch
            r4 = lambda t_, o=0: t_.a(o, [[1, N]], 0, 4)
            r3 = lambda t_, o=0: t_.a(o, [[Lc, nch], [1, Lc]], 0, 4)
            rb = lambda t_, o: t_.a(o, [[Lc, nch], [0, Lc]], 0, 4)
            rbl, rml = R
            rlf, ra2, ru = fgr, rml, rbl
            kb.act(r4(fgr), r4(fgr), AF.Exp, scale=-1.0)
            kb.act(r4(fgr), r4(fgr), AF.Ln, bias=1.0)
            kb.ts(r4(rlf), r4(fgr), -1.0, None, ALU.mult)
            kb.ms(Bm.a(0, [[1, 1]], 0, 4), 0.0)
            kb.scan(r4(Bm, 1), r4(rlf), zerosT.a(0, [[1, N]], 0, 4), 0.0, ALU.add, ALU.add)
            if g.kind == "sample":
                for b, (t0, n, _) in enumerate(g.tiles):
                    kb.scan(mz.a(1 + t0, [[1, n]], 0, 4), rlf.a(t0, [[1, n]], 0, 4), igr.a(t0, [[1, n]], 0, 4),
                            m0.a(b, [[1, 1]], 0, 4), ALU.add, ALU.max)
                mprev_b = m0.a(0, [[1, nch], [0, Lc]], 0, 4)
            else:
                kb.cp(mz.a(0, [[1, 1]], 0, 4), S["mst"][l].a(0, [[1, 1]], 0, 4))
                kb.scan(r4(mz, 1), r4(rlf), r4(igr), mz.a(0, [[1, 1]], 0, 4), ALU.add, ALU.max)
                kb.cp(S["mst"][l].a(0, [[1, 1]], 0, 4), mz.a(N, [[1, 1]], 0, 4))
                mprev_b = rb(mz, 0)
            yield
            kb.tt(r3(rbl), r3(Bm, 1), rb(Bm, 0), ALU.subtract)
            kb.tt(r3(rml), r3(mz, 1), mprev_b, ALU.subtract)
            kb.tt(r4(ra2), r4(rbl), r4(rml), ALU.subtract)
            kb.tt(r4(ru), r4(igr), r4(rbl), ALU.subtract)
            kb.tt(r3(ru), r3(ru), mprev_b, ALU.subtract)
            tea, teu = lnv, rstd
            v = lambda t_, o=0: t_.a(o, [[1, N]])
            yield
            for j in range(2):
                lhs = sel.a(j * 128, [[1, 128]], 0, 4)
                ps = accbank()
                kb.mm(ps.a(0, [[1, N]]), lhs, r4(ra2))
                kb.act(v(tea), ps.a(0, [[1, N]]), AF.Exp)
                ps = accbank()
                kb.mm(ps.a(0, [[1, N]]), lhs, r4(ru))
                kb.act(v(teu), ps.a(0, [[1, N]]), AF.Exp)
                ps = accbank()
                kb.mm(ps.a(0, [[1, N]]), lhs, r4(mz, 1))
                kb.act(v(em[j]), ps.a(0, [[1, N]]), AF.Exp, scale=-1.0)
                yield
                kb.tt(Qh[1][j].a(0, [[1, N]]), co.a(j * NM, [[1, N]], key=j), v(tea), ALU.mult)
                kb.stt(Kt[1][j].a(0, [[1, N]]), co.a((2 + j) * NM, [[1, N]], key=2 + j), 0.125, v(teu), ALU.mult, ALU.mult)
                kb.tt(Kh[1][j].a(0, [[Lc, nch], [1, Lc]]), Kt[1][j].a(0, [[Lc, nch], [1, Lc]]),
                      tea.a(Lc - 1, [[Lc, nch], [0, Lc]]), ALU.mult)
                kb.cp(dec[1][j].a(0, [[1, nch]]), tea.a(Lc - 1, [[Lc, nch]]))
                yield

        la_rr = [0]

        def la_tile(x, j, g, ti, Sst, Wd):
            t0, n, chunks = g.tiles[ti]
            q_ = 2 * x + j
            Q, K_, KH = Qh[x][j], Kt[x][j], Kh[x][j]
            vcol = (0 if x == 0 else 256) + j * 128
            asb, kt_, sb_, s1_ = att_sb[q_], khtok[q_], sbf[q_], s1[q_]
            pa = PB[q_]
            kb.mm(pa.a(0, [[1, n]], 0, n), K_.a(t0, [[1, n]], 0, 64), Q.a(t0, [[1, n]], 0, 64), sgc=True)
            kb.tr(pa.a(512, [[1, 128]], 0, n, bf=True), KH.a(t0, [[1, n]]), ident_b.a())
            yield
            kb.mm(pa.a(128, [[1, n]], 0, n), K_.a(t0, [[1, n]], 64, 64), Q.a(t0, [[1, n]], 64, 64), sgc=True)
            yield
            kb.tt(asb.a(0, [[128, 2], [1, n]], 0, n), pa.a(0, [[128, 2], [1, n]], 0, n),
                  maskA.a(0, [[0, 2], [1, n]], 0, n), ALU.mult)
            kb.cp(kt_.a(0, [[1, 128]], 0, n), pa.a(512, [[1, 128]], 0, n, bf=True), eng="act_cp")
            yield
            po = PB[4 + q_]
            for p in range(2):
                kb.mm(po.a(0, [[1, n]], p * 64, 64), Vall.a(ti * 640 + vcol + p * 64, [[1, 64]], 0, n, key=ti),
                      asb.a(p * 128, [[1, n]], 0, n), start=True, stop=False, sgc=True)
                if x == 1:
                    kb.mm(po.a(128, [[1, n]], p * 64, 64), ones_b.a(0, [[1, 64]], 0, n),
                          asb.a(p * 128, [[1, n]], 0, n), start=False, stop=False, sgc=True)
            for ci, (c, cs, ce) in enumerate(chunks):
                last = ci == len(chunks) - 1
                Sf = Sst(0, 128, 0, Wd)
                if x == 0:
                    kb.ts(sb_.a(0, [[1, Wd]]), Sf, smid[j].a(c, [[1, 1]]), None, ALU.mult)
                else:
                    kb.act(sb_.a(0, [[1, Wd]]), Sf, AF.Copy)
                pP = pa
                for p in range(2):
                    kb.mm(pP.a(0, [[1, 64]], p * 64, 64), kt_.a(p * 64, [[1, 64]], cs, ce - cs),
                          Vall.a(ti * 640 + vcol + p * 64, [[1, 64]], cs, ce - cs, key=ti), start=True, stop=(x == 0), sgc=True)
                    if x == 1:
                        kb.mm(pP.a(64, [[1, 64]], p * 64, 64), kt_.a(p * 64, [[1, 64]], cs, ce - cs),
                              ones_b.a(0, [[1, 64]], cs, ce - cs), start=False, stop=True, sgc=True)
                yield
                for p in range(2):
                    kb.mm(po.a(cs, [[1, ce - cs]], p * 64, 64), sb_.a(0, [[1, 64]], p * 64, 64),
                          Q.a(t0 + cs, [[1, ce - cs]], p * 64, 64), start=False, stop=last, sgc=True)
                    if x == 1:
                        kb.mm(po.a(128 + cs, [[1, ce - cs]], p * 64, 64), sb_.a(64, [[1, 64]], p * 64, 64),
                              Q.a(t0 + cs, [[1, ce - cs]], p * 64, 64), start=False, stop=last, sgc=True)
                    if p == 0:
                        yield
                kb.stt(Sf, Sf, dec[x][j].a(c, [[1, 1]]), pP.a(0, [[1, Wd]]), ALU.mult, ALU.add)
                yield
            kb.cp(oall[x][j].a(t0, [[1, n]]), po.a(0, [[1, n]]), eng="act_cp")
            if x == 1:
                kb.cp(dall[j].a(t0, [[1, n]]), po.a(128, [[1, n]]))

        def run_interleaved(gens):
            gens = list(gens)
            while gens:
                for g_ in list(gens):
                    try:
                        next(g_)
                    except StopIteration:
                        gens.remove(g_)

        def head_norm(g, l, src, nrm, gate, dstc, q_):
            N = g.N
            v = lambda t_: t_.a(0, [[1, N]])
            sq_ = (sqb[0], sqb[1], sqh, sqh2)[q_]
            rs_ = T[q_]
            kb.act(sq_.a(0, [[1, N]]), v(src), AF.Square)
            yield
            ps = accbank()
            kb.mm(ps.a(0, [[1, N]]), bones_b.a(), sq_.a(0, [[1, N]]))
            kb.act(v(rs_), ps.a(0, [[1, N]]), AF.Ln, bias=epsT.a(), scale=1.0 / 64)
            yield
            kb.act(v(rs_), v(rs_), AF.Exp, scale=-0.5)
            yield
            kb.stt(v(src), v(src), nrm, v(rs_), ALU.mult, ALU.mult)
            yield
            kb.tt(hn.a(dstc * NM, [[1, N]], key=dstc), v(src), gate.a(0, [[1, N]]), ALU.mult)

        def postH_gen(g, l, j):
            yield from head_norm(g, l, oall[0][j], hnormT.a(l * 2 + j, [[1, 1]]), ga[j], j, j)

        def postM_gen(g, l, j):
            N = g.N
            v = lambda t_: t_.a(0, [[1, N]])
            t4 = T[4 + j]
            kb.stt(v(t4), v(dall[j]), -1.0, v(dall[j]), ALU.mult, ALU.max)
            kb.tt(v(t4), v(t4), v(em[j]), ALU.max)
            yield
            kb.act(v(t4), v(t4), AF.Ln)
            yield
            kb.act(v(t4), v(t4), AF.Exp, scale=-1.0)
            yield
            kb.tt(v(oall[1][j]), v(oall[1][j]), v(t4), ALU.mult)
            yield
            yield from head_norm(g, l, oall[1][j], mnormT.a(l * 2 + j, [[1, 1]]), go[j], 2 + j, 2 + j)

        def post_all(g, l):
            run_interleaved([postM_gen(g, l, 0), postM_gen(g, l, 1), postH_gen(g, l, 0), postH_gen(g, l, 1)])

        def swa_norm(g, l, ti, po, pd):
            t0, n, _ = g.tiles[ti]
            kb.tt(tdc.a(0, [[128, 4], [1, n]]), pd.a(0, [[128, 4], [1, n]]), esink.a(l * 4, [[1, 4], [0, n]]), ALU.add)
            kb.act(tdc.a(0, [[128, 4], [1, n]]), tdc.a(0, [[128, 4], [1, n]]), AF.Ln)
            kb.act(tdc.a(0, [[128, 4], [1, n]]), tdc.a(0, [[128, 4], [1, n]]), AF.Exp, scale=-1.0)
            dst = hn.a(4 * NM + t0, [[NM, 4], [1, n]], key=(4, 5, 6, 7))
            kb.tt(dst, po.a(0, [[128, 4], [1, n]]), tdc.a(0, [[128, 4], [1, n]]), ALU.mult)

        def swa_tile(g, l, ti, kbs, bset=0, part="S"):
            t0, n, _ = g.tiles[ti]
            po, pd = PB[4 + 2 * bset], PB[5 + 2 * bset]
            if part == "N":
                return swa_norm(g, l, ti, po, pd)
            for bi, (Kf, Vf, nk, mk) in enumerate(kbs):
                for p in range(2):
                    sl = bi * 2 + p
                    ps = accbank()
                    for j in range(4):
                        kb.mm(ps.a(j * 128, [[1, n]], 0, nk), Kf(p), QT.a(j * NM + t0, [[1, n]], p * 64, 64, key=j), sgc=True)
                    kb.act(pesb[sl].a(0, [[128, 4], [1, n]], 0, nk), ps.a(0, [[128, 4], [1, n]], 0, nk), AF.Exp, scale=0.125)
                    kb.tt(pesb[sl].a(0, [[128, 4], [1, n]], 0, nk), pesb[sl].a(0, [[128, 4], [1, n]], 0, nk), mk(nk, n), ALU.mult)
            nb = len(kbs)
            for j in range(4):
                for p in range(2):
                    for bi, (Kf, Vf, nk, mk) in enumerate(kbs):
                        sl = bi * 2 + p
                        kb.mm(po.a(j * 128, [[1, n]], p * 64, 64), Vf(p), pesb[sl].a(j * 128, [[1, n]], 0, nk),
                              start=(j == 0 and bi == 0), stop=(j == 3 and bi == nb - 1), sgc=True)
            for j in range(4):
                for p in range(2):
                    for bi, (Kf, Vf, nk, mk) in enumerate(kbs):
                        sl = bi * 2 + p
                        kb.mm(pd.a(j * 128, [[1, n]], p * 64, 64), ones_b.a(0, [[1, 64]], 0, nk),
                              pesb[sl].a(j * 128, [[1, n]], 0, nk), start=(j == 0 and bi == 0), stop=(j == 3 and bi == nb - 1), sgc=True)

        def wout_stage(g, l):
            N = g.N
            for dc in range(8):
                ps = accbank()
                for k_ in range(8):
                    rhs = hn.a(k_ * NM, [[1, N]], key=k_)
                    kb.mm(ps.a(0, [[1, N]]), WOUT.a(k_ * 1024 + dc * 128, [[1, 128]]), rhs, start=(k_ == 0), stop=(k_ == 7))
                kb.tt(hT.a(dc * NM, [[1, N]], key=dc), hT.a(dc * NM, [[1, N]], key=dc), ps.a(0, [[1, N]]), ALU.add)
                rms_stats(g, dc)

        def ffn_stage(g, l):
            N = g.N
            rmsnorm(g, gffn, l * 8, have_stats=True)
            for b in range(11):
                W = get_fin()
                for cc in range(2):
                    c = 2 * b + cc
                    bp = ((0, 1), (2, 3), (5, 6))[c % 3]
                    pg, pu = PB[bp[0]], PB[bp[1]]
                    for k_ in range(8):
                        kb.mm(pg.a(0, [[1, N]]), W.a(k_ * 256 + cc * 128, [[1, 128]]), hn.a(k_ * NM, [[1, N]], key=k_),
                              start=(k_ == 0), stop=(k_ == 7))
                    for k_ in range(8):
                        kb.mm(pu.a(0, [[1, N]]), W.a(2048 + k_ * 256 + cc * 128, [[1, 128]]), hn.a(k_ * NM, [[1, N]], key=k_),
                              start=(k_ == 0), stop=(k_ == 7))
                    sg = (lnv, rstd)[cc]
                    kb.act(sg.a(0, [[1, N]]), pg.a(0, [[1, N]]), AF.Silu)
                    kb.tt(actT.a(c * NM, [[1, N]], key=c), sg.a(0, [[1, N]]), pu.a(0, [[1, N]]), ALU.mult)
            for dc in range(8):
                W = get_fout()
                ps = PB[WIDE[dc % 7]]
                for c in range(22):
                    kb.mm(ps.a(0, [[1, N]]), W.a(c * 128, [[1, 128]]), actT.a(c * NM, [[1, N]], key=c), start=(c == 0), stop=(c == 21))
                kb.tt(hT.a(dc * NM, [[1, N]], key=dc), hT.a(dc * NM, [[1, N]], key=dc), ps.a(0, [[1, N]]), ALU.add)
                rms_stats(g, dc)

        def seq_init():
            for l in range(L):
                for j in range(2):
                    kb.cp(STW["SH"][l][j].a(), STM["SH"][l][j].a())
                    kb.cp(STW["SM"][l][j].a(), STM["SM"][l][j].a(), eng="act_cp")
                kb.cp(STW["mst"][l].a(0, [[1, 1]], 0, 4), STM["mst"][l].a(0, [[1, 1]], 0, 4))
                kb.cp(STW["cvh"][l].a(), STM["cvh"][l].a())

        def load_sample_states(g, l):
            q = "sp"
            NS = NSAMP
            kb.batch_begin(q, ch_ld)
            with nc.allow_non_contiguous_dma(reason="state layouts"):
                for p in range(2):
                    kb.batch_add(kb.dma(q, SHs.a(0, [[64, NS * 2], [1, 64]], p * 64, 64, key=tuple(range(NS * 2))),
                                        dram_ap(st_hgrn, l * NS * 16384 + p * 4096, [[64, 64], [8192, NS * 2], [1, 64]]), ch_ld))
                    kb.batch_add(kb.dma(q, Xc.a(0, [[64, NS * 2], [1, 64]], p * 64, 64),
                                        dram_ap(st_c, l * NS * 16384 + p * 4096, [[64, 64], [8192, NS * 2], [1, 64]]), ch_ld))
                kb.batch_add(kb.dma(q, Xn.a(0, [[1, 64]], 0, NS * 4), dram_ap(st_n, l * NS * 256, [[64, NS * 4], [1, 64]]), ch_ld))
                kb.batch_add(kb.dma(q, m0.a(0, [[1, NS]], 0, 4), dram_ap(st_m, l * NS * 4, [[1, 4], [4, NS]]), ch_ld))
                kb.batch_add(kb.dma(q, Xv.a(0, [[1, 512]], 0, NS * 3), dram_ap(st_conv, l * NS * 1536, [[512, NS * 3], [1, 512]]), ch_ld))
                kb.batch_add(kb.dma(q, Kc.a(0, [[128, NS], [1, 128]]), dram_ap(c_k, l * NS * 16384, [[128, 128], [16384, NS], [1, 128]]), ch_ld))
                kb.batch_add(kb.dma(q, Vc.a(0, [[128, NS], [1, 128]]), dram_ap(c_v, l * NS * 16384, [[128, 128], [16384, NS], [1, 128]]), ch_ld))
            kb.batch_end(ch_ld)
            for grp8 in range(NS * 2 // 8):
                ps = accbank()
                for p in range(2):
                    for q8 in range(8):
                        bj = grp8 * 8 + q8
                        kb.trx(ps.a(q8 * 64, [[1, 64]], p * 64, 64), Xc.a(bj * 64, [[1, 64]], p * 64, 64),
                               ident_f.a(p * 64, [[1, 64]], p * 64, 64), p)
                kb.cp(SMs.a(grp8 * 8 * 128, [[128, 8], [1, 64]], key=tuple(range(grp8 * 8, grp8 * 8 + 8))), ps.a(0, [[64, 8], [1, 64]]))
            ps = accbank()
            for p in range(2):
                kb.trx(ps.a(0, [[1, NS * 4]], p * 64, 64), Xn.a(0, [[1, 64]], 0, NS * 4), ident_f.a(0, [[1, NS * 4]], 0, NS * 4), p)
                kb.cp(nl.a(0, [[1, NS * 2]], p * 64, 64), ps.a(p, [[2, NS * 2]], p * 64, 64))
            kb.cp(SMs.a(64, [[128, NS * 2], [1, 64]], key=tuple(range(NS * 2))), nl.a(0, [[1, NS * 2], [0, 64]]))
            ps = accbank()
            for ch in range(4):
                kb.tr(ps.a(ch * 128, [[1, NS * 3]]), Xv.a(ch * 128, [[1, 128]], 0, NS * 3), ident_f.a(0, [[1, NS * 3]], 0, NS * 3))
            for ch in range(4):
                kb.cp(xp.a(ch * NS * 7, [[7, NS], [1, 3]], key=ch), ps.a(ch * 128, [[3, NS], [1, 3]]))
            for b in range(NS):
                ps = accbank()
                kb.tr(ps.a(0, [[1, 128]]), Kc.a(b * 128, [[1, 128]]), ident_f.a())
                kb.cp(KTc.a(b * 128, [[1, 128]]), ps.a(0, [[1, 128]]), eng=("act_cp" if b % 2 else "dve"))
            kb.cp(Vcb.a(), Vc.a())

        def store_sample_states(g, l):
            q = "sp"
            NS = NSAMP
            for grp8 in range(NS * 2 // 8):
                ps = accbank()
                for p in range(2):
                    for q8 in range(8):
                        bj = grp8 * 8 + q8
                        kb.trx(ps.a(q8 * 64, [[1, 64]], p * 64, 64), SMs.a(bj * 128, [[1, 64]], p * 64, 64, key=bj),
                               ident_f.a(p * 64, [[1, 64]], p * 64, 64), p)
                kb.cp(Xc.a(grp8 * 8 * 64, [[1, 512]]), ps.a(0, [[1, 512]]))
            kb.cp(nl.a(0, [[1, NS * 2]]), SMs.a(64, [[128, NS * 2]], key=tuple(range(NS * 2))))
            kb.cp(mout.a(0, [[1, NS]], 0, 4), mz.a(4, [[4, NS]], 0, 4))
            ps = accbank()
            for ch in range(4):
                kb.tr(ps.a(ch * 128, [[1, 128]], 0, NS * 3), cvs.a(ch * NS * 3, [[1, NS * 3]]), ident_f.a())
            kb.cp(cso.a(0, [[1, 512]], 0, NS * 3), ps.a(0, [[1, 512]], 0, NS * 3))
            kb.batch_begin(q, ch_st)
            with nc.allow_non_contiguous_dma(reason="state layouts"):
                for p in range(2):
                    kb.batch_add(kb.dma(q, dram_ap(o_hgrn_s, l * NS * 16384 + p * 4096, [[64, 64], [8192, NS * 2], [1, 64]]),
                                        SHs.a(0, [[64, NS * 2], [1, 64]], p * 64, 64, key=tuple(range(NS * 2))), ch_st))
                    kb.batch_add(kb.dma(q, dram_ap(o_c_s, l * NS * 16384 + p * 4096, [[64, 64], [8192, NS * 2], [1, 64]]),
                                        Xc.a(0, [[64, NS * 2], [1, 64]], p * 64, 64), ch_st))
                    kb.batch_add(kb.dma(q, dram_ap(o_n_s, l * NS * 256 + p * 64, [[1, 64], [128, NS * 2]]),
                                        nl.a(0, [[1, NS * 2]], p * 64, 64), ch_st))
                kb.batch_add(kb.dma(q, dram_ap(o_m_s, l * NS * 4, [[1, 4], [4, NS]]), mout.a(0, [[1, NS]], 0, 4), ch_st))
                kb.batch_add(kb.dma(q, dram_ap(o_conv_s, l * NS * 1536, [[512, NS * 3], [1, 512]]), cso.a(0, [[1, 512]], 0, NS * 3), ch_st))
                for b in range(NS):
                    kb.batch_add(kb.dma(q, dram_ap(o_k_s, (l * NS + b) * 16384 + 124 * 128, [[128, 4], [1, 128]]),
                                        Kc.a(b * 128, [[1, 128]], 0, 4), ch_st))
                    kb.batch_add(kb.dma(q, dram_ap(o_v_s, (l * NS + b) * 16384 + 124 * 128, [[128, 4], [1, 128]]),
                                        Vc.a(b * 128, [[1, 128]], 0, 4), ch_st))
            kb.batch_end(ch_st)
            kb.eng[q].dma_start(out=dram_ap(o_k_s, l * NS * 16384, [[16384, NS], [1, 124 * 128]]),
                                in_=dram_ap(c_k, l * NS * 16384 + 4 * 128, [[16384, NS], [1, 124 * 128]])).then_inc(ch_dd.sem, 16)
            ch_dd.val += 16
            kb.eng[q].dma_start(out=dram_ap(o_v_s, l * NS * 16384, [[16384, NS], [1, 124 * 128]]),
                                in_=dram_ap(c_v, l * NS * 16384 + 4 * 128, [[16384, NS], [1, 124 * 128]])).then_inc(ch_dd.sem, 16)
            ch_dd.val += 16

        def store_prompt_states(g, l):
            q = "sp"
            N = g.N
            s = g.seq
            S = STW
            ps = accbank()
            for j in range(2):
                for p in range(2):
                    kb.trx(ps.a(j * 64, [[1, 64]], p * 64, 64), S["SM"][l][j].a(0, [[1, 64]], p * 64, 64),
                           ident_f.a(p * 64, [[1, 64]], p * 64, 64), p)
            kb.cp(cst.a(0, [[1, 128]]), ps.a(0, [[1, 128]]))
            ps2 = accbank()
            for ch in range(4):
                kb.tr(ps2.a(ch * 128, [[1, 128]], 0, 3), S["cvh"][l].a(ch * 3, [[1, 3]]), ident_f.a())
            kb.cp(cvo.a(0, [[1, 512]], 0, 3), ps2.a(0, [[1, 512]], 0, 3))
            kb.batch_begin(q, ch_st)
            with nc.allow_non_contiguous_dma(reason="state layouts"):
                for j in range(2):
                    for p in range(2):
                        h = 2 * j + p
                        base = ((l * NSEQ + s) * 4 + h)
                        kb.batch_add(kb.dma(q, dram_ap(o_hgrn_p, base * 4096, [[64, 64], [1, 64]]),
                                            S["SH"][l][j].a(0, [[1, 64]], p * 64, 64), ch_st))
                        kb.batch_add(kb.dma(q, dram_ap(o_c_p, base * 4096, [[64, 64], [1, 64]]),
                                            cst.a(j * 64, [[1, 64]], p * 64, 64), ch_st))
                        kb.batch_add(kb.dma(q, dram_ap(o_n_p, base * 64, [[1, 64], [1, 1]]),
                                            S["SM"][l][j].a(64, [[1, 1]], p * 64, 64), ch_st))
                kb.batch_add(kb.dma(q, dram_ap(o_m_p, (l * NSEQ + s) * 4, [[1, 4], [1, 1]]), S["mst"][l].a(0, [[1, 1]], 0, 4), ch_st))
                kb.batch_add(kb.dma(q, dram_ap(o_conv_p, (l * NSEQ + s) * 1536, [[512, 3], [1, 512]]),
                                    cvo.a(0, [[1, 512]], 0, 3), ch_st))
                kb.batch_add(kb.dma(q, dram_ap(o_k_p, (l * NSEQ + s) * 16384, [[128, 128], [1, 128]]), kvout.a(0, [[1, 128]]), ch_st))
                kb.batch_add(kb.dma(q, dram_ap(o_v_p, (l * NSEQ + s) * 16384, [[128, 128], [1, 128]]), kvout.a(128, [[1, 128]]), ch_st))
            kb.batch_end(ch_st)

        def run_group(g):
            N = g.N
            S = STM if g.kind == "meta" else STW
            if g.kind == "prompt" and g.gi == 0:
                seq_init()
            gl = f"{g.kind[0]}{g.seq}{g.gi}"
            kb.lab = gl + ":loadx"
            load_x(g)
            for l in range(L):
                if g.kind == "sample":
                    kb.lab = gl + f":L{l}:ldst"
                    load_sample_states(g, l)
                kb.lab = gl + f":L{l}:A"
                acc_pool[0] = WIDE
                stageA(g, l, S)
                dead_tail = (g.kind == "meta" and l == L - 1)
                if not dead_tail:
                    emit_wout(l)
                kb.lab = gl + f":L{l}:prep"
                run_interleaved([stageA_tail(g, l), prepH(g, l), prepM(g, l, S)])
                kb.lab = gl + f":L{l}:tiles"
                acc_pool[0] = [0, 1, 2, 3]
                acc_rr[0] = 0
                def chain_gen(x, j):
                    for ti in range(len(g.tiles)):
                        if g.kind == "sample":
                            if x == 0:
                                st = (lambda b_, j_: (lambda p0, pn, off, w: SHs.a((b_ * 2 + j_) * 64 + off, [[1, w]], p0, pn, key=b_ * 2 + j_)))(ti, j)
                            else:
                                st = (lambda b_, j_: (lambda p0, pn, off, w: SMs.a((b_ * 2 + j_) * 128 + off, [[1, w]], p0, pn, key=b_ * 2 + j_)))(ti, j)
                        else:
                            if x == 0:
                                st = (lambda j_: (lambda p0, pn, off, w: S["SH"][l][j_].a(off, [[1, w]], p0, pn)))(j)
                            else:
                                st = (lambda j_: (lambda p0, pn, off, w: S["SM"][l][j_].a(off, [[1, w]], p0, pn)))(j)
                        yield from la_tile(x, j, g, ti, st, 64 if x == 0 else 128)
                        yield

                run_interleaved([chain_gen(1, 0), chain_gen(1, 1), chain_gen(0, 0), chain_gen(0, 1)])
                for ti, (t0, n, chunks) in enumerate([] if dead_tail else g.tiles):
                    cur = (lambda t0_, n_, ti_: ((lambda p: KT.a(t0_, [[1, n_]], p * 64, 64)),
                                                 (lambda p: Vall.a(ti_ * 640 + 512 + p * 64, [[1, 64]], 0, n_, key=ti_)),
                                                 n_, (lambda nk, nq: maskC.a(0, [[0, 4], [1, nq]], 0, nk))))(t0, n, ti)
                    if g.kind == "meta":
                        kbs = [cur]
                    elif g.kind == "sample":
                        b = ti
                        prev = ((lambda p, b=b: KTc.a(b * 128, [[1, 128]], p * 64, 64)),
                                (lambda p, b=b: Vcb.a(b * 128 + p * 64, [[1, 64]])),
                                128, (lambda nk, nq: maskP.a(0, [[0, 4], [1, nq]], 0, nk)))
                        kbs = [prev, cur]
                    else:
                        if ti > 0:
                            pt0 = g.tiles[ti - 1][0]
                            prev = ((lambda p, pt0=pt0: KT.a(pt0, [[1, 128]], p * 64, 64)),
                                    (lambda p, ti=ti: Vall.a((ti - 1) * 640 + 512 + p * 64, [[1, 64]], key=ti - 1)),
                                    128, (lambda nk, nq: maskP.a(0, [[0, 4], [1, nq]], 0, nk)))
                        elif g.gi > 0:
                            prev = ((lambda p: STW["sK"][l].a(0, [[1, 128]], p * 64, 64)),
                                    (lambda p: STW["sV"][l].a(p * 64, [[1, 64]])),
                                    128, (lambda nk, nq: maskP.a(0, [[0, 4], [1, nq]], 0, nk)))
                        else:
                            prev = ((lambda p: STM["sK"][l].a(0, [[1, 16]], p * 64, 64)),
                                    (lambda p: STM["sV"][l].a(p * 64, [[1, 64]], 0, 16)),
                                    16, (lambda nk, nq: maskM.a(0, [[0, 4], [1, nq]], 0, nk)))
                        kbs = [prev, cur]
                    swa_tile(g, l, ti, kbs, bset=ti % 2, part="S")
                    if ti > 0:
                        swa_tile(g, l, ti - 1, None, bset=(ti - 1) % 2, part="N")
                if not dead_tail:
                    swa_tile(g, l, len(g.tiles) - 1, None, bset=(len(g.tiles) - 1) % 2, part="N")
                if g.kind == "meta":
                    kb.cp(STM["sK"][l].a(0, [[1, 16]]), KT.a(0, [[1, 16]]))
                    kb.cp(STM["sV"][l].a(0, [[1, 128]], 0, 16), Vall.a(512, [[1, 128]], 0, 16, key=0))
                elif g.kind == "prompt" and not g.last:
                    lt = len(g.tiles) - 1
                    kb.cp(STW["sK"][l].a(), KT.a(g.tiles[lt][0], [[1, 128]]))
                    kb.cp(STW["sV"][l].a(), Vall.a(lt * 640 + 512, [[1, 128]], key=lt))
                kb.lab = gl + f":L{l}:post"
                if not dead_tail:
                    post_all(g, l)
                if g.kind == "sample":
                    store_sample_states(g, l)
                if g.kind == "prompt" and g.last:
                    store_prompt_states(g, l)
                if dead_tail:
                    continue
                kb.lab = gl + f":L{l}:wout"
                wout_stage(g, l)
                kb.lab = gl + f":L{l}:ffn"
                ffn_stage(g, l)
            kb.lab = gl + ":final"
            if g.kind == "meta":
                rms_stats(g, None, flush=True)
            if g.kind != "meta":
                rmsnorm(g, gfin, 0, dst_bf=False, have_stats=True)
                store_y(g)

        for g in groups:
            run_group(g)

    _cp = kb.cp

    def cp2(out, in_, eng="dve"):
        if eng == "act_cp":
            return kb.act(out, in_, AF.Copy)
        return _cp(out, in_, eng)

    kb.cp = cp2

    build_consts()
    load_params()
    run_phase(groups1, 64 if NSAMP else 16)
    kb.barrier()
    if groups2:
        run_phase(groups2, 512)
    for c in kb.chans:
        if c.val:
            kb._wait("sp", c, c.val)
    kb.es.close()
    return kb


OUT_NAMES = ["y_prompt", "y_sample", "hgrn_p", "mlstm_c_p", "mlstm_n_p", "mlstm_m_p", "mlstm_conv_p", "swa_k_p", "swa_v_p",
             "hgrn_s", "mlstm_c_s", "mlstm_n_s", "mlstm_m_s", "mlstm_conv_s", "swa_k_s", "swa_v_s"]
PROMPT_OUT = set(OUT_NAMES[:1] + OUT_NAMES[2:9])
BATCH_AXIS = {"y_prompt": 0, "y_sample": 0}
SAMPLE_IN = ["state_hgrn", "state_mlstm_c", "state_mlstm_n", "state_mlstm_m", "state_mlstm_conv", "cache_swa_k", "cache_swa_v"]


def make_in_maps(inputs, ncores, nseq, nsamp):
    maps = []
    for c in range(ncores):
        m = {}
        for k_, v in inputs.items():
            v = np.asarray(v)
            if k_ == "x_prompt":
                m[k_] = np.ascontiguousarray(v[c * nseq:(c + 1) * nseq])
            elif k_ == "x_sample":
                m[k_] = np.ascontiguousarray(v[c * nsamp:(c + 1) * nsamp])
            elif k_ in SAMPLE_IN:
                m[k_] = np.ascontiguousarray(v[:, c * nsamp:(c + 1) * nsamp])
            else:
                m[k_] = np.ascontiguousarray(v)
        maps.append(m)
    return maps


def run(inputs, ncores=NCORES, NSEQ=2, NGRP=4, NSAMP=16, debug=False, trace=False):
    k1 = build(None, NSEQ, NGRP, NSAMP, debug)
    needed = k1.rec
    k2 = build(needed, NSEQ, NGRP, NSAMP, debug)
    in_maps = make_in_maps(inputs, ncores, NSEQ, NSAMP)
    res = run_bass_kernel_spmd(k2.nc, in_maps, core_ids=list(range(ncores)), trace=trace)
    outs = []
    for name in OUT_NAMES:
        ax = 0 if name in BATCH_AXIS else 1
        outs.append(np.concatenate([np.asarray(r[name]) for r in res.results], axis=ax).astype(np.float32))
    return tuple(outs), res


def kernel(**inputs):
    outs, _ = run(inputs)
    return outs
```

```python
import numpy as np
from contextlib import ExitStack
import concourse.bass as bass
import concourse.mybir as mybir
from concourse.bass_utils import run_bass_kernel_spmd

F32 = mybir.dt.float32
BF16 = mybir.dt.bfloat16
ALU = mybir.AluOpType
AF = mybir.ActivationFunctionType

D = 1024
DIN = 2824
DFF = 2816
L = 2
NCORES = 8
EPS = 1e-6


class Res:
    __slots__ = ("lw", "rd", "name")

    def __init__(self, name):
        self.lw = None
        self.rd = {}
        self.name = name


class A:
    __slots__ = ("ap", "res", "p0", "bank")

    def __init__(self, ap, res, p0=0):
        self.ap = ap
        self.res = res
        self.p0 = p0
        self.bank = None


class Chan:
    def __init__(self, sem, name):
        self.sem = sem
        self.val = 0
        self.name = name


class TT:
    def __init__(self, kb, name, F, dt, psum=False):
        self.kb = kb
        name = name + kb.tag
        self.name = name
        self.F = F
        self.dt = dt
        if psum:
            self.t = kb.es.enter_context(kb.nc.psum_tensor(name, [128, F], dt))
        else:
            self.t = kb.es.enter_context(kb.nc.sbuf_tensor(name, [128, F], dt))
        self.res = {}

    def r(self, key=0):
        if key not in self.res:
            self.res[key] = Res(f"{self.name}:{key}")
        return self.res[key]

    def a(self, off=0, dims=None, p0=0, pn=128, key=0):
        if dims is None:
            dims = [[1, self.F - off]]
        ap = bass.AP(self.t, p0 * self.F + off, [[self.F, pn]] + [list(d) for d in dims])
        if isinstance(key, tuple):
            return A(ap, tuple(self.r(k_) for k_ in key), p0)
        return A(ap, self.r(key), p0)


class Bank:
    def __init__(self, kb, name):
        self.t = kb.es.enter_context(kb.nc.psum_tensor(name, [128, 512], F32))
        self.tb = self.t.bitcast(BF16)
        self.res = Res(name)
        self.res_pe = None

    def a(self, off=0, dims=None, p0=0, pn=128, key=0, bf=False):
        F = 1024 if bf else 512
        if dims is None:
            dims = [[1, F - off]]
        ap = bass.AP(self.tb if bf else self.t, p0 * F + off, [[F, pn]] + [list(d) for d in dims])
        a = A(ap, self.res, p0)
        a.bank = self
        return a


class Arena:
    PAGE = 1024

    def __init__(self, kb, name, nbytes):
        nbytes = -(-nbytes // self.PAGE) * self.PAGE
        self.nbytes = nbytes
        self.t32 = kb.es.enter_context(kb.nc.sbuf_tensor(name + kb.tag, [128, nbytes // 4], F32))
        self.tb = self.t32.bitcast(BF16)
        self.pages = [Res(f"{name}:pg{i}") for i in range(nbytes // self.PAGE)]

    def view(self, boff, F, dt):
        return AV(self, boff, F, dt)


class AV:
    def __init__(self, ar, boff, F, dt):
        self.ar, self.boff, self.F, self.dt = ar, boff, F, dt
        self.es = 4 if dt == F32 else 2
        assert boff % self.es == 0
        self.h = ar.t32 if dt == F32 else ar.tb
        self.ps = ar.nbytes // self.es

    def a(self, off=0, dims=None, p0=0, pn=128, key=0):
        if dims is None:
            dims = [[1, self.F - off]]
        base = self.boff // self.es + off
        ap = bass.AP(self.h, p0 * self.ps + base, [[self.ps, pn]] + [list(d) for d in dims])
        span = sum((c - 1) * abs(st) for st, c in dims)
        b0 = base * self.es
        b1 = (base + span + 1) * self.es - 1
        pg = self.ar.pages[b0 // Arena.PAGE: b1 // Arena.PAGE + 1]
        return A(ap, tuple(pg), p0)


class KB:
    ENG = ("pe", "act", "dve", "pool")

    def __init__(self, needed=None):
        self.nc = bass.Bass("TRN2", target_bir_lowering=False)
        self.es = ExitStack()
        nc = self.nc
        self.eng = {"pe": nc.tensor, "act": nc.scalar, "dve": nc.vector, "pool": nc.gpsimd, "sp": nc.sync}
        self.sem = {e: self.es.enter_context(nc.semaphore("s_" + e)) for e in self.ENG}
        self.idx = {e: 0 for e in self.ENG}
        self.needed = needed
        if needed is not None:
            self.rank = {e: {i: r for r, i in enumerate(sorted(needed[e]))} for e in self.ENG}
        self.rec = {e: set() for e in self.ENG}
        self.waited = {e: {} for e in ("pe", "act", "dve", "pool", "sp")}
        self.chans = []
        self.tag = ""
        self.lab = ""
        self.labels = {e: [] for e in self.ENG}
        self.dbg = []
        self.ninst = 0

    def chan(self, name):
        c = Chan(self.es.enter_context(self.nc.semaphore("c_" + name)), name)
        self.chans.append(c)
        return c

    def _wait(self, eng, src, i):
        if isinstance(src, Chan):
            val, sem = i, src.sem
        else:
            self.rec[src].add(i)
            val = (i + 1) if self.needed is None else self.rank[src][i] + 1
            sem = self.sem[src]
        w = self.waited[eng]
        if w.get(src, 0) >= val:
            return
        w[src] = val
        self.eng[eng].wait_ge(sem, val)

    def _deps(self, eng, reads, writes, skip=None):
        for r in reads:
            if r.lw is not None:
                src, i = r.lw
                if (src == eng and eng == "pe") or src is skip:
                    continue
                self._wait(eng, src, i)
        for w in writes:
            if w.lw is not None:
                src, i = w.lw
                if (src != eng or eng != "pe") and src is not skip:
                    self._wait(eng, src, i)
            for src, i in w.rd.items():
                if (src != eng or eng != "pe") and src is not skip:
                    self._wait(eng, src, i)

    @staticmethod
    def _flat(xs):
        out = []
        for x in xs:
            if isinstance(x, A):
                x = x.res
            if isinstance(x, Res):
                out.append(x)
            elif isinstance(x, (tuple, list)):
                out.extend(x)
        return out

    def emit(self, eng, fn, reads, writes):
        writes = list(writes) + [x for x in reads if isinstance(x, A) and x.bank is not None]
        reads = self._flat(reads)
        writes = self._flat(writes)
        self._deps(eng, reads, writes)
        ins = fn()
        i = self.idx[eng]
        self.idx[eng] = i + 1
        self.labels[eng].append(self.lab)
        self.ninst += 1
        if self.needed is None or i in self.rank[eng]:
            ins.then_inc(self.sem[eng], 1)
        for r in reads:
            r.rd[eng] = i
        for w in writes:
            w.lw = (eng, i)
            w.rd = {}
        return (eng, i)

    def dma(self, q, out, in_, ch, **kw):
        reads = self._flat([in_]) if isinstance(in_, A) else []
        writes = self._flat([out]) if isinstance(out, A) else []
        self._deps(q, reads, writes, skip=ch)
        o = out.ap if isinstance(out, A) else out
        i_ = in_.ap if isinstance(in_, A) else in_
        self.eng[q].dma_start(out=o, in_=i_, **kw).then_inc(ch.sem, 16)
        ch.val += 16
        self.ninst += 1
        for r in reads:
            r.rd[ch] = ch.val
        for w in writes:
            w.lw = (ch, ch.val)
            w.rd = {}
        return reads + writes

    def batch_begin(self, q, ch):
        if ch.val:
            self._wait(q, ch, ch.val)
        self._batch = []

    def batch_add(self, touched):
        self._batch += touched

    def batch_end(self, ch):
        for r in self._batch:
            if r.lw is not None and r.lw[0] is ch:
                r.lw = (ch, ch.val)
            if ch in r.rd:
                r.rd[ch] = ch.val

    def barrier(self):
        last = {e: self.idx[e] - 1 for e in self.ENG}
        for e in ("pe", "act", "dve", "pool", "sp"):
            for s in self.ENG:
                if s != e and last[s] >= 0:
                    self._wait(e, s, last[s])
            for c in self.chans:
                if c.val:
                    self._wait(e, c, c.val)

    def _rowsync(self, out, kbase):
        b = out.bank
        if b is not None and b.res_pe is not None and b.res_pe[0] != kbase:
            self._wait("pe", "pe", b.res_pe[1])

    def mm(self, out, lhsT, rhs, start=True, stop=True, sgc=False):
        self._rowsync(out, lhsT.p0)
        t = self.emit("pe", lambda: self.nc.tensor.matmul(out.ap, lhsT=lhsT.ap, rhs=rhs.ap, start=start, stop=stop,
                                                          skip_group_check=sgc), [lhsT, rhs], [out])
        if out.bank is not None:
            out.bank.res_pe = (lhsT.p0, t[1])
        return t

    def tr(self, out, in_, ident):
        self._rowsync(out, in_.p0)
        t = self.emit("pe", lambda: self.nc.tensor.transpose(out.ap, in_.ap, ident.ap), [in_, ident], [out])
        if out.bank is not None:
            out.bank.res_pe = (in_.p0, t[1])
        return t

    def trx(self, out, in_, ident, p):
        if p == 0:
            return self.tr(out, in_, ident)
        return self.mm(out, in_, ident)

    def act(self, out, in_, func, bias=None, scale=1.0):
        kw = {}
        if bias is not None:
            kw["bias"] = bias.ap if isinstance(bias, A) else bias
        kw["scale"] = scale.ap if isinstance(scale, A) else scale
        return self.emit("act", lambda: self.nc.scalar.activation(out=out.ap, in_=in_.ap, func=func, **kw),
                         [in_, bias, scale], [out])

    def tt(self, out, in0, in1, op, eng="dve"):
        return self.emit(eng, lambda: self.eng[eng].tensor_tensor(out=out.ap, in0=in0.ap, in1=in1.ap, op=op),
                         [in0, in1], [out])

    def ts(self, out, in0, s1, s2, op0, op1=None, eng="dve"):
        a1 = s1.ap if isinstance(s1, A) else s1
        a2 = s2.ap if isinstance(s2, A) else s2
        if op1 is None:
            f = lambda: self.eng[eng].tensor_scalar(out=out.ap, in0=in0.ap, scalar1=a1, scalar2=None, op0=op0)
        else:
            f = lambda: self.eng[eng].tensor_scalar(out=out.ap, in0=in0.ap, scalar1=a1, scalar2=a2, op0=op0, op1=op1)
        return self.emit(eng, f, [in0, s1, s2], [out])

    def stt(self, out, in0, sc, in1, op0, op1, eng="dve"):
        a = sc.ap if isinstance(sc, A) else sc
        return self.emit(eng, lambda: self.eng[eng].scalar_tensor_tensor(out=out.ap, in0=in0.ap, scalar=a, in1=in1.ap,
                                                                          op0=op0, op1=op1), [in0, sc, in1], [out])

    def cp(self, out, in_, eng="dve"):
        return self.emit(eng, lambda: self.eng[eng].tensor_copy(out=out.ap, in_=in_.ap), [in_], [out])

    def ms(self, out, val, eng="dve"):
        return self.emit(eng, lambda: self.eng[eng].memset(out.ap, val), [], [out])

    def scan(self, out, d0, d1, init, op0, op1):
        a = init.ap if isinstance(init, A) else init
        return self.emit("dve", lambda: self.nc.vector.tensor_tensor_scan(out=out.ap, data0=d0.ap, data1=d1.ap, initial=a,
                                                                          op0=op0, op1=op1), [d0, d1, init], [out])

    def recip(self, out, in_):
        return self.emit("dve", lambda: self.nc.vector.reciprocal(out=out.ap, in_=in_.ap), [in_], [out])

    def asel(self, out, in_, pattern, op, fill, base, cm):
        return self.emit("pool", lambda: self.nc.gpsimd.affine_select(out=out.ap, in_=in_.ap, pattern=pattern, compare_op=op,
                                                                       fill=fill, base=base, channel_multiplier=cm),
                         [in_], [out])


class Grp:
    def __init__(self, kind, N, Lc, tiles, seq=0, gi=0, last=False):
        self.kind, self.N, self.Lc, self.tiles, self.seq, self.gi, self.last = kind, N, Lc, tiles, seq, gi, last
        self.nch = N // Lc


def dram_ap(t, off, dims):
    return bass.AP(t, off, [list(d) for d in dims])


CONV_ENG = "dve"
S1_ENG = "pool"


def build(needed=None, NSEQ=2, NGRP=4, NSAMP=16, debug=False):
    kb = KB(needed)
    nc = kb.nc
    SEQ = NGRP * 512
    NMAX = 512

    def din(name, shape):
        return nc.dram_tensor(name, list(shape), F32, kind="ExternalInput")

    def dout(name, shape):
        return nc.dram_tensor(name, list(shape), F32, kind="ExternalOutput")

    x_prompt = din("x_prompt", [NSEQ, SEQ, D])
    x_sample = din("x_sample", [NSAMP, 4, D])
    st_hgrn = din("state_hgrn", [L, NSAMP, 4, 64, 64])
    st_c = din("state_mlstm_c", [L, NSAMP, 4, 64, 64])
    st_n = din("state_mlstm_n", [L, NSAMP, 4, 64])
    st_m = din("state_mlstm_m", [L, NSAMP, 4])
    st_conv = din("state_mlstm_conv", [L, NSAMP, 3, 512])
    c_k = din("cache_swa_k", [L, NSAMP, 128, 2, 64])
    c_v = din("cache_swa_v", [L, NSAMP, 128, 2, 64])
    meta_tokens = din("meta_tokens", [16, D])
    norm_mix = din("norm_mix", [L, D])
    norm_ffn = din("norm_ffn", [L, D])
    norm_final = din("norm_final", [D])
    w_in = din("w_in", [L, D, DIN])
    lb_logits = din("hgrn_lb_logits", [L, 256])
    hgrn_norm = din("hgrn_norm", [L, 256])
    conv_w = din("mlstm_conv_w", [L, 4, 512])
    conv_b = din("mlstm_conv_b", [L, 512])
    i_bias = din("mlstm_i_bias", [L, 4])
    f_bias = din("mlstm_f_bias", [L, 4])
    mlstm_norm = din("mlstm_norm", [L, 256])
    sinks = din("attn_sinks", [L, 8])
    w_out = din("w_out", [L, D, D])
    w_ffn_in = din("w_ffn_in", [L, D, 2 * DFF])
    w_ffn_out = din("w_ffn_out", [L, DFF, D])

    y_prompt = dout("y_prompt", [NSEQ, SEQ, D])
    y_sample = dout("y_sample", [NSAMP, 4, D])
    o_hgrn_p = dout("hgrn_p", [L, NSEQ, 4, 64, 64])
    o_c_p = dout("mlstm_c_p", [L, NSEQ, 4, 64, 64])
    o_n_p = dout("mlstm_n_p", [L, NSEQ, 4, 64])
    o_m_p = dout("mlstm_m_p", [L, NSEQ, 4])
    o_conv_p = dout("mlstm_conv_p", [L, NSEQ, 3, 512])
    o_k_p = dout("swa_k_p", [L, NSEQ, 128, 2, 64])
    o_v_p = dout("swa_v_p", [L, NSEQ, 128, 2, 64])
    o_hgrn_s = dout("hgrn_s", [L, NSAMP, 4, 64, 64])
    o_c_s = dout("mlstm_c_s", [L, NSAMP, 4, 64, 64])
    o_n_s = dout("mlstm_n_s", [L, NSAMP, 4, 64])
    o_m_s = dout("mlstm_m_s", [L, NSAMP, 4])
    o_conv_s = dout("mlstm_conv_s", [L, NSAMP, 3, 512])
    o_k_s = dout("swa_k_s", [L, NSAMP, 128, 2, 64])
    o_v_s = dout("swa_v_s", [L, NSAMP, 128, 2, 64])

    def sb(name, F, dt=F32):
        return TT(kb, name, F, dt)

    ident_f = sb("ident_f", 128)
    ident_b = sb("ident_b", 128, BF16)
    ones_b = sb("ones_b", 128, BF16)
    ones_f = sb("ones_f", 128)
    bones_b = sb("bones_b", 128, BF16)
    maskA = sb("maskA", 128, BF16)
    maskC = sb("maskC", 128, BF16)
    maskP = sb("maskP", 128, BF16)
    maskM = sb("maskM", 128, BF16)
    sel = sb("sel", 256)
    epsT = sb("epsT", 1)
    zerosT = sb("zerosT", NMAX)
    scr = sb("scr", 256)
    gmix = sb("gmix", L * 8)
    gffn = sb("gffn", L * 8)
    gfin = sb("gfin", 8)
    lbT = sb("lbT", L * 2)
    omlT = sb("omlT", L * 2)
    nomlT = sb("nomlT", L * 2)
    lgT = sb("lgT", L * 2)
    hnormT = sb("hnormT", L * 2)
    mnormT = sb("mnormT", L * 2)
    cwT = sb("cwT", L * 16)
    cbT = sb("cbT", L * 4)
    ibT = sb("ibT", L)
    fbT = sb("fbT", L)
    skrow = sb("skrow", 16)
    esink = sb("esink", L * 4)

    WIN = [sb(f"win{i}", 8 * 512, BF16) for i in range(2)]
    WOUT = sb("woutb", 8 * 1024, BF16)
    NFIN, NFOUT = 3, 3
    FIN = [sb(f"fin{i}", 2 * 8 * 256, BF16) for i in range(NFIN)]
    FOUT = [sb(f"fout{i}", 22 * 128, BF16) for i in range(NFOUT)]
    ch_win = [kb.chan(f"win{i}") for i in range(2)]
    ch_wout = kb.chan("wout")
    ch_fin = [kb.chan(f"fin{i}") for i in range(NFIN)]
    ch_fout = [kb.chan(f"fout{i}") for i in range(NFOUT)]
    ch_par = kb.chan("par")
    ch_x = [kb.chan("x0"), kb.chan("x1")]
    ch_y = [kb.chan("y0"), kb.chan("y1")]
    ch_st = kb.chan("stout")
    ch_ld = kb.chan("stld")
    ch_dd = kb.chan("d2d")

    def stset(tag):
        return dict(
            SH=[[sb(f"SH{tag}{l}{j}", 64) for j in range(2)] for l in range(L)],
            SM=[[sb(f"SM{tag}{l}{j}", 128) for j in range(2)] for l in range(L)],
            mst=[sb(f"mst{tag}{l}", 1) for l in range(L)],
            cvh=[sb(f"cvh{tag}{l}", 12) for l in range(L)],
            sK=[sb(f"sK{tag}{l}", 128, BF16) for l in range(L)],
            sV=[sb(f"sV{tag}{l}", 128, BF16) for l in range(L)],
        )

    STM = stset("M")
    STW = stset("W")

    PB = [Bank(kb, f"pb{i}") for i in range(8)]

    def build_consts():
        kb.ms(ident_f.a(), 1.0, eng="pool")
        kb.asel(ident_f.a(), ident_f.a(), [[-1, 128]], ALU.is_equal, 0.0, 0, 1)
        kb.cp(ident_b.a(), ident_f.a(), eng="pool")
        kb.ms(ones_b.a(), 1.0, eng="pool")
        kb.ms(ones_f.a(), 1.0, eng="pool")
        kb.ms(bones_b.a(), 1.0, eng="pool")
        kb.ms(bones_b.a(64, [[1, 64]], 0, 64), 0.0, eng="pool")
        kb.ms(bones_b.a(0, [[1, 64]], 64, 64), 0.0, eng="pool")
        t = scr
        kb.ms(t.a(0, [[1, 128]]), 1.0, eng="pool")
        kb.asel(t.a(0, [[1, 128]]), t.a(0, [[1, 128]]), [[1, 128]], ALU.is_ge, 0.0, 0, -1)
        kb.cp(maskC.a(), t.a(0, [[1, 128]]), eng="pool")
        kb.cp(maskA.a(), t.a(0, [[1, 128]]), eng="pool")
        kb.ms(maskA.a(64, [[1, 64]], 0, 64), 0.0, eng="pool")
        kb.ms(t.a(0, [[1, 128]]), 1.0, eng="pool")
        kb.asel(t.a(0, [[1, 128]]), t.a(0, [[1, 128]]), [[-1, 128]], ALU.is_gt, 0.0, 0, 1)
        kb.cp(maskP.a(), t.a(0, [[1, 128]]), eng="pool")
        kb.ms(t.a(0, [[1, 128]]), 1.0, eng="pool")
        kb.asel(t.a(0, [[1, 128]]), t.a(0, [[1, 128]]), [[-1, 128]], ALU.is_gt, 0.0, 112, 1)
        kb.cp(maskM.a(), t.a(0, [[1, 128]]), eng="pool")
        kb.ms(sel.a(), 1.0, eng="pool")
        kb.asel(sel.a(), sel.a(), [[1, 256]], ALU.is_ge, 0.0, 0, -64)
        kb.asel(sel.a(), sel.a(), [[-1, 256]], ALU.is_ge, 0.0, 63, 64)
        kb.ms(epsT.a(), EPS, eng="pool")
        kb.ms(zerosT.a(), 0.0, eng="pool")
        for S in (STM,):
            for l in range(L):
                for j in range(2):
                    kb.ms(S["SH"][l][j].a(), 0.0, eng="pool")
                    kb.ms(S["SM"][l][j].a(), 0.0, eng="pool")
                kb.ms(S["mst"][l].a(), 0.0, eng="pool")
                kb.ms(S["cvh"][l].a(), 0.0, eng="pool")

    def load_params():
        q = "sp"
        kb.batch_begin(q, ch_par)
        with nc.allow_non_contiguous_dma(reason="tiny param layouts"):
            def ld(dst, src):
                kb.batch_add(kb.dma(q, dst, src, ch_par))
            for l in range(L):
                ld(gmix.a(l * 8, [[1, 8]]), dram_ap(norm_mix, l * D, [[1, 128], [128, 8]]))
                ld(gffn.a(l * 8, [[1, 8]]), dram_ap(norm_ffn, l * D, [[1, 128], [128, 8]]))
                ld(lgT.a(l * 2, [[1, 2]]), dram_ap(lb_logits, l * 256, [[1, 128], [128, 2]]))
                ld(hnormT.a(l * 2, [[1, 2]]), dram_ap(hgrn_norm, l * 256, [[1, 128], [128, 2]]))
                ld(mnormT.a(l * 2, [[1, 2]]), dram_ap(mlstm_norm, l * 256, [[1, 128], [128, 2]]))
                for j in range(4):
                    ld(cwT.a(l * 16 + j * 4, [[1, 4]]), dram_ap(conv_w, (l * 4 + j) * 512, [[1, 128], [128, 4]]))
                ld(cbT.a(l * 4, [[1, 4]]), dram_ap(conv_b, l * 512, [[1, 128], [128, 4]]))
                ld(ibT.a(l, [[1, 1]], 0, 4), dram_ap(i_bias, l * 4, [[1, 4], [1, 1]]))
                ld(fbT.a(l, [[1, 1]], 0, 4), dram_ap(f_bias, l * 4, [[1, 4], [1, 1]]))
            ld(gfin.a(), dram_ap(norm_final, 0, [[1, 128], [128, 8]]))
            ld(skrow.a(0, [[1, 16]], 0, 1), dram_ap(sinks, 0, [[16, 1], [1, 16]]))
        kb.batch_end(ch_par)
        t = scr
        l0, l1 = lgT.a(0, [[1, 2]]), lgT.a(2, [[1, 2]])
        mx, e0, e1, s_, r_ = (t.a(i * 2, [[1, 2]]) for i in range(5))
        kb.tt(mx, l0, l1, ALU.max)
        kb.tt(e0, l0, mx, ALU.subtract)
        kb.tt(e1, l1, mx, ALU.subtract)
        kb.act(e0, e0, AF.Exp)
        kb.act(e1, e1, AF.Exp)
        kb.tt(s_, e0, e1, ALU.add)
        kb.recip(r_, s_)
        kb.tt(e0, e0, r_, ALU.mult)
        kb.tt(e1, e1, r_, ALU.mult)
        kb.tt(lbT.a(0, [[1, 2]]), e0, e0, ALU.subtract)
        kb.tt(s_, e0, e1, ALU.add)
        kb.tt(lbT.a(2, [[1, 2]]), s_, e0, ALU.subtract)
        kb.ts(omlT.a(), lbT.a(), -1.0, 1.0, ALU.mult, ALU.add)
        kb.ts(nomlT.a(), omlT.a(), -1.0, None, ALU.mult)
        ps = PB[0]
        kb.mm(ps.a(0, [[1, 16]]), ones_f.a(0, [[1, 128]], 0, 1), skrow.a(0, [[1, 16]], 0, 1))
        for l in range(L):
            for p in range(2):
                kb.act(esink.a(l * 4, [[1, 4]], p * 64, 64), ps.a(l * 8 + p * 4, [[1, 4]], p * 64, 64), AF.Exp)

    wl_win, wl_fin, wl_fout = [], [], []
    state = dict(win_i=0, fin_i=0, fout_i=0, win_e=0, fin_e=0, fout_e=0)

    def wsrc(t, l, rows, col0, ncol, row0=0, nk=8, pn=128, ld=None):
        return dram_ap(t, (l * rows + row0) * ld + col0, [[ld, pn], [128 * ld, nk], [1, ncol]])

    def win_blocks(l):
        W = lambda c0, nc_: wsrc(w_in, l, D, c0, nc_, ld=DIN)
        b = []
        b.append([((0, 512), W(0, 512))])
        b.append([((0, 256), W(768, 256)), ((256, 256), W(1792, 256))])
        b.append([((0, 512), W(1024, 512))])
        f4 = []
        for p in range(2):
            for j in range(4):
                f4.append(((j * 128 + p * 64, 64), W(2056 + (j + 4 * p) * 64, 64)))
        b.append([((0, 128), W(2568, 128)), ((128, 8), W(2048, 8))])
        b.append(f4)
        b.append([((0, 256), W(512, 256)), ((256, 256), W(1536, 256))])
        b.append([((0, 256), W(2568, 256))])
        return b

    def emit_win(i):
        l, blk = wl_win[i]
        slot = i % 2
        T = WIN[slot]
        for (c0, ncol), src in win_blocks(l)[blk]:
            dst = T.a(c0, [[512, 8], [1, ncol]])
            kb.dma("pool", dst, src, ch_win[slot])

    def emit_wout(l):
        T = WOUT
        kb.dma("pool", T.a(0, [[1024, 4], [1, 1024]]), wsrc(w_out, l, D, 0, 1024, nk=4, ld=D), ch_wout)
        for p in range(2):
            src = dram_ap(w_out, (l * D + 512 + p * 256) * D, [[D, 64], [64 * D, 4], [1, 1024]])
            kb.dma("pool", T.a(4 * 1024, [[1024, 4], [1, 1024]], p * 64, 64), src, ch_wout)

    def emit_fin(i):
        l, b = wl_fin[i]
        slot = i % NFIN
        T = FIN[slot]
        kb.dma("pool", T.a(0, [[256, 8], [1, 256]]), wsrc(w_ffn_in, l, D, b * 256, 256, ld=2 * DFF), ch_fin[slot])
        kb.dma("pool", T.a(2048, [[256, 8], [1, 256]]), wsrc(w_ffn_in, l, D, DFF + b * 256, 256, ld=2 * DFF), ch_fin[slot])

    def emit_fout(i):
        l, dc = wl_fout[i]
        slot = i % NFOUT
        T = FOUT[slot]
        kb.dma("pool", T.a(0, [[128, 22], [1, 128]]), wsrc(w_ffn_out, l, DFF, dc * 128, 128, nk=22, ld=D), ch_fout[slot])

    def get_win():
        i = state["win_i"]
        state["win_i"] += 1
        while state["win_e"] < min(len(wl_win), i + 2):
            emit_win(state["win_e"])
            state["win_e"] += 1
        return WIN[i % 2]

    def get_fin():
        i = state["fin_i"]
        state["fin_i"] += 1
        while state["fin_e"] < min(len(wl_fin), i + NFIN):
            emit_fin(state["fin_e"])
            state["fin_e"] += 1
        return FIN[i % NFIN]

    def get_fout():
        i = state["fout_i"]
        state["fout_i"] += 1
        while state["fout_e"] < min(len(wl_fout), i + NFOUT):
            emit_fout(state["fout_e"])
            state["fout_e"] += 1
        return FOUT[i % NFOUT]

    groups1 = [Grp("meta", 16, 16, [(0, 16, [(0, 0, 16)])])]
    if NSAMP:
        groups1.append(Grp("sample", NSAMP * 4, 4, [(b * 4, 4, [(b, 0, 4)]) for b in range(NSAMP)]))
    groups2 = []
    for s in range(NSEQ):
        for gi in range(NGRP):
            groups2.append(Grp("prompt", 512, 64, [(i * 128, 128, [(2 * i, 0, 64), (2 * i + 1, 64, 128)]) for i in range(4)],
                               seq=s, gi=gi, last=(gi == NGRP - 1)))
    for g in groups1 + groups2:
        for l in range(L):
            for blk in range(7):
                wl_win.append((l, blk))
            if g.kind == "meta" and l == L - 1:
                continue
            for b in range(11):
                wl_fin.append((l, b))
            for dc in range(8):
                wl_fout.append((l, dc))

    dbg_out = {}

    def dump(name, a, shape):
        if not debug:
            return
        t = nc.dram_tensor("dbg_" + name, list(shape), a.ap.dtype if hasattr(a.ap, "dtype") else F32, kind="ExternalOutput")
        c = kb.chan("dbg_" + name)
        kb.dma("sp", t.ap(), a, c)
        dbg_out[name] = c

    def run_phase(groups, NM):
        with ExitStack() as es2:
            old_es = kb.es
            kb.es = es2
            kb.tag = f"_n{NM}"
            try:
                _run_phase(groups, NM)
            finally:
                kb.es = old_es
                kb.tag = ""

    def _run_phase(groups, NM):
        NT = max(len(g.tiles) for g in groups)
        hT = sb("hT", 8 * NM)
        hn = sb("hn", 8 * NM, BF16)
        XPW = max((g.N + 3 * (len(g.tiles) if g.kind == "sample" else 1)) for g in groups)
        arena = Arena(kb, "arena", max(32 * NM + 16 * XPW, 44 * NM, 25600))
        actT = arena.view(0, 22 * NM, BF16)
        xs = [arena.view(4096 * i, 1024, F32) for i in range(2)]
        sqb = [sb(f"sqb{i}", NM, BF16) for i in range(2)]
        sqh = sb("sqh", NM, BF16)
        sqh2 = sb("sqh2", NM, BF16)
        lnv = sb("lnv", NM)
        rstd = sb("rstd", NM)
        qs = [arena.view(4 * NM * j, NM, F32) for j in range(2)]
        th = [arena.view(8 * NM + 4 * NM * j, NM, F32) for j in range(2)]
        ga = [sb(f"ga{j}", NM, BF16) for j in range(2)]
        go = [sb(f"go{j}", NM, BF16) for j in range(2)]
        xp = arena.view(16 * NM, 4 * XPW, F32)
        cvo = arena.view(2048, 512, F32)
        co = arena.view(16 * NM + 16 * XPW, 4 * NM, F32)
        QT = sb("QT", 4 * NM, BF16)
        KT = sb("KT", NM, BF16)
        igr = sb("igr", NM)
        fgr = sb("fgr", NM)
        Vall = sb("Vall", NT * 640, BF16)
        kvout = sb("kvout", 256)
        T = [sb(f"T{i}", NM + 1) for i in range(6)]
        R = [sb(f"R{i}", NM + 1) for i in range(2)]
        Bm = sb("Bm", NM + 1)
        mz = sb("mz", NM + 1)
        Qh = [[sb(f"Qh{x}{j}", NM, BF16) for j in range(2)] for x in range(2)]
        Kt = [[sb(f"Kt{x}{j}", NM, BF16) for j in range(2)] for x in range(2)]
        Kh = [[sb(f"Kh{x}{j}", NM, BF16) for j in range(2)] for x in range(2)]
        dec = [[sb(f"dec{x}{j}", 16) for j in range(2)] for x in range(2)]
        smid = [sb(f"smid{j}", 16) for j in range(2)]
        em = [arena.view(16 * NM + 4 * NM * j, NM, F32) for j in range(2)]
        oall = [[arena.view(4 * NM * (2 * x + j), NM, F32) for j in range(2)] for x in range(2)]
        dall = [arena.view(24 * NM + 4 * NM * j, NM, F32) for j in range(2)]
        att_sb = [sb(f"att{i}", 256, BF16) for i in range(8)]
        khtok = [sb(f"khtok{i}", 128, BF16) for i in range(8)]
        sbf = [sb(f"sbf{i}", 128, BF16) for i in range(8)]
        pesb = [arena.view(18432 + 1024 * i, 512, BF16) for i in range(4)]
        tdc = arena.view(16384, 512, F32)
        cst = arena.view(0, 512, F32)
        has_sample = any(g.kind == "sample" for g in groups)
        if has_sample:
            NS = NSAMP
            SHs = sb("SHs", NS * 2 * 64)
            SMs = sb("SMs", NS * 2 * 128)
            Xc = arena.view(8192, NS * 2 * 64, F32)
            Xn = sb("Xn", 64)
            nl = sb("nl", NS * 2)
            m0 = sb("m0", NS)
            Xv = sb("Xv", 512)
            Kc = sb("Kc", NS * 128)
            Vc = sb("Vc", NS * 128)
            KTc = sb("KTc", NS * 128, BF16)
            Vcb = sb("Vcb", NS * 128, BF16)
            mout = sb("mout", NS)
            cvs = sb("cvs", 4 * NS * 3)
            cso = sb("cso", 512)

        print(f"[sbuf] phase NM={NM}: bytes remaining per partition = {kb.nc.sbuf_bytes_remaining}") if needed is None else None
        acc_rr = [0]

        acc_pool = [[0, 1, 2, 3]]

        def accbank():
            acc_rr[0] = (acc_rr[0] + 1) % len(acc_pool[0])
            return PB[acc_pool[0][acc_rr[0]]]

        WIDE = [0, 1, 2, 3, 5, 6, 7]

        rms_pending = []

        def rms_stats(g, c, flush=False):
            N = g.N
            ps = PB[4]
            if c is not None:
                s = sqb[c % 2]
                kb.act(s.a(0, [[1, N]]), hT.a(c * NM, [[1, N]], key=c), AF.Square)
                rms_pending.append((c, s))
            while rms_pending and (flush or len(rms_pending) > 1):
                c_, s_ = rms_pending.pop(0)
                kb.mm(ps.a(0, [[1, N]]), ones_b.a(), s_.a(0, [[1, N]]), start=(c_ == 0), stop=(c_ == 7))

        def rmsnorm(g, gam, goff, dst_bf=True, have_stats=False):
            N = g.N
            ps = PB[4]
            if not have_stats:
                for c in range(8):
                    rms_stats(g, c)
            rms_stats(g, None, flush=True)
            kb.act(lnv.a(0, [[1, N]]), ps.a(0, [[1, N]]), AF.Ln, bias=epsT.a(), scale=1.0 / D)
            kb.act(rstd.a(0, [[1, N]]), lnv.a(0, [[1, N]]), AF.Exp, scale=-0.5)
            for c in range(8):
                dst = hn.a(c * NM, [[1, N]], key=c) if dst_bf else hT.a(c * NM, [[1, N]], key=c)
                kb.stt(dst, hT.a(c * NM, [[1, N]], key=c), gam.a(goff + c, [[1, 1]]), rstd.a(0, [[1, N]]), ALU.mult, ALU.mult)

        def fm(W, col, M, ps, N):
            for k_ in range(8):
                kb.mm(ps.a(0, [[1, N]], 0, M), W.a(k_ * 512 + col, [[1, M]]), hn.a(k_ * NM, [[1, N]], key=k_),
                      start=(k_ == 0), stop=(k_ == 7))

        def tm(W, col, ncol, ps, t0, n):
            for k_ in range(8):
                kb.mm(ps.a(0, [[1, ncol]], 0, n), hn.a(k_ * NM + t0, [[1, n]], key=k_), W.a(k_ * 512 + col, [[1, ncol]]),
                      start=(k_ == 0), stop=(k_ == 7))

        def load_x(g):
            N = g.N
            if g.kind == "prompt":
                srcs = [(t0, n, dram_ap(x_prompt, (g.seq * SEQ + g.gi * 512 + t0) * D, [[D, n], [1, D]])) for (t0, n, _) in g.tiles]
            elif g.kind == "meta":
                srcs = [(0, 16, dram_ap(meta_tokens, 0, [[D, 16], [1, D]]))]
            else:
                srcs = [(0, N, dram_ap(x_sample, 0, [[D, N], [1, D]]))]
            for i, (t0, n, src) in enumerate(srcs):
                s = i % 2
                kb.dma("sp", xs[s].a(0, [[1, 1024]], 0, n), src, ch_x[s])
                for half in range(2):
                    ps = accbank()
                    for c4 in range(4):
                        c = half * 4 + c4
                        kb.tr(ps.a(c4 * 128, [[1, n]]), xs[s].a(c * 128, [[1, 128]], 0, n), ident_f.a(0, [[1, n]], 0, n))
                    for c4 in range(4):
                        c = half * 4 + c4
                        kb.cp(hT.a(c * NM + t0, [[1, n]], key=c), ps.a(c4 * 128, [[1, n]]), eng=("dve" if c4 % 2 else "act_cp"))

        def store_y(g):
            N = g.N
            if g.kind == "prompt":
                dsts = [(t0, n, dram_ap(y_prompt, (g.seq * SEQ + g.gi * 512 + t0) * D, [[D, n], [1, D]])) for (t0, n, _) in g.tiles]
            else:
                dsts = [(0, N, dram_ap(y_sample, 0, [[D, N], [1, D]]))]
            for i, (t0, n, dst) in enumerate(dsts):
                s = i % 2
                for half in range(2):
                    ps = accbank()
                    for c4 in range(4):
                        c = half * 4 + c4
                        kb.tr(ps.a(c4 * 128, [[1, 128]], 0, n), hT.a(c * NM + t0, [[1, n]], key=c), ident_f.a())
                    kb.cp(xs[s].a(half * 512, [[1, 512]], 0, n), ps.a(0, [[1, 512]], 0, n), eng=("dve" if half else "act_cp"))
                kb.dma("sp", dst, xs[s].a(0, [[1, 1024]], 0, n), ch_y[s])

        def stageA(g, l, S):
            N, Lc, nch = g.N, g.Lc, g.nch
            rmsnorm(g, gmix, l * 8, have_stats=(l > 0))
            W = get_win()
            for j in range(2):
                ps = accbank()
                fm(W, j * 128, 128, ps, N)
                kb.act(qs[j].a(0, [[1, N]]), ps.a(0, [[1, N]]), AF.Silu)
            for j in range(2):
                ps = accbank()
                fm(W, 256 + j * 128, 128, ps, N)
                kb.act(th[j].a(0, [[1, N]]), ps.a(0, [[1, N]]), AF.Tanh, scale=0.5)
            W = get_win()
            for j in range(2):
                ps = accbank()
                fm(W, j * 128, 128, ps, N)
                kb.act(ga[j].a(0, [[1, N]]), ps.a(0, [[1, N]]), AF.Silu)
            for j in range(2):
                ps = accbank()
                fm(W, 256 + j * 128, 128, ps, N)
                kb.act(T[0].a(0, [[1, N]]), ps.a(0, [[1, N]]), AF.Tanh, scale=0.5)
                kb.ts(go[j].a(0, [[1, N]]), T[0].a(0, [[1, N]]), 0.5, 0.5, ALU.mult, ALU.add)
            W = get_win()
            nsg = len(g.tiles) if g.kind == "sample" else 1
            Ls = N // nsg
            SW = Ls + 3
            XW = nsg * SW
            for ch in range(4):
                if g.kind == "sample":
                    pass
                else:
                    kb.cp(xp.a(ch * XW, [[1, 3]], key=ch), S["cvh"][l].a(ch * 3, [[1, 3]]))
                ps = accbank()
                fm(W, ch * 128, 128, ps, N)
                kb.cp(xp.a(ch * XW + 3, [[SW, nsg], [1, Ls]], key=ch), ps.a(0, [[Ls, nsg], [1, Ls]]), eng="act_cp")
                cw = lambda j_: cwT.a(l * 16 + j_ * 4 + ch, [[1, 1]])
                o_ = co.a(ch * NM, [[Ls, nsg], [1, Ls]], key=ch)
                kb.ts(o_, xp.a(ch * XW, [[SW, nsg], [1, Ls]], key=ch), cw(0), cbT.a(l * 4 + ch, [[1, 1]]), ALU.mult, ALU.add, eng=CONV_ENG)
                for j_ in range(1, 4):
                    kb.stt(o_, xp.a(ch * XW + j_, [[SW, nsg], [1, Ls]], key=ch), cw(j_), o_, ALU.mult, ALU.add, eng=CONV_ENG)
                kb.act(co.a(ch * NM, [[1, N]], key=ch), co.a(ch * NM, [[1, N]], key=ch), AF.Silu)
                if g.kind != "sample":
                    kb.cp(S["cvh"][l].a(ch * 3, [[1, 3]]), xp.a(ch * XW + N, [[1, 3]], key=ch))
                else:
                    kb.cp(cvs.a(ch * nsg * 3, [[3, nsg], [1, 3]]), xp.a(ch * XW + 4, [[SW, nsg], [1, 3]], key=ch))
            W = get_win()
            ps = accbank()
            fm(W, 0, 128, ps, N)
            kb.cp(KT.a(0, [[1, N]]), ps.a(0, [[1, N]]))
            ps = accbank()
            fm(W, 128, 4, ps, N)
            kb.ts(igr.a(0, [[1, N]], 0, 4), ps.a(0, [[1, N]], 0, 4), ibT.a(l, [[1, 1]], 0, 4), None, ALU.add)
            ps = accbank()
            fm(W, 132, 4, ps, N)
            kb.ts(fgr.a(0, [[1, N]], 0, 4), ps.a(0, [[1, N]], 0, 4), fbT.a(l, [[1, 1]], 0, 4), None, ALU.add)

        def stageA_tail(g, l):
            N = g.N
            W = get_win()
            for j in range(4):
                ps = accbank()
                fm(W, j * 128, 128, ps, N)
                kb.cp(QT.a(j * NM, [[1, N]], key=j), ps.a(0, [[1, N]]), eng=("dve" if j % 2 else "act_cp"))
                yield
            W = get_win()
            for ti, (t0, n, _) in enumerate(g.tiles):
                ps = accbank()
                tm(W, 0, 512, ps, t0, n)
                kb.cp(Vall.a(ti * 640, [[1, 512]], 0, n, key=ti), ps.a(0, [[1, 512]], 0, n), eng=("act_cp" if ti % 2 else "dve"))
                yield
            W = get_win()
            for ti, (t0, n, _) in enumerate(g.tiles):
                ps = accbank()
                tm(W, 0, 256, ps, t0, n)
                kb.cp(Vall.a(ti * 640 + 512, [[1, 128]], 0, n, key=ti), ps.a(128, [[1, 128]], 0, n), eng=("dve" if ti % 2 else "act_cp"))
                if g.kind == "sample":
                    kb.cp(Kc.a(ti * 128, [[1, 128]], 0, n), ps.a(0, [[1, 128]], 0, n))
                    kb.cp(Vc.a(ti * 128, [[1, 128]], 0, n), ps.a(128, [[1, 128]], 0, n))
                elif g.kind == "prompt" and g.last and ti == len(g.tiles) - 1:
                    kb.cp(kvout.a(0, [[1, 256]], 0, n), ps.a(0, [[1, 256]], 0, n))
                yield

        def prepH(g, l):
            N, Lc, nch = g.N, g.Lc, g.nch
            mi = Lc // 2 - 1
            tsig, tkk, tf, Bz, tbp, teb = T
            v = lambda t_, o=0: t_.a(o, [[1, N]])
            v3 = lambda t_, o=0: t_.a(o, [[Lc, nch], [1, Lc]])
            bc = lambda t_, o: t_.a(o, [[Lc, nch], [0, Lc]])
            cv_ = lambda t_, o: t_.a(o, [[Lc, nch]])
            for j in range(2):
                pj = lambda P_: P_.a(l * 2 + j, [[1, 1]])
                kb.ts(v(tsig), v(th[j]), 0.5, 0.5, ALU.mult, ALU.add)
                kb.ts(v(tkk), v(tsig), pj(nomlT), pj(omlT), ALU.mult, ALU.add)
                kb.ts(v(tf), v(tsig), pj(omlT), pj(lbT), ALU.mult, ALU.add)
                kb.ts(v(tf), v(tf), 1e-30, None, ALU.max)
                yield
                kb.act(v(tf), v(tf), AF.Ln)
                kb.ms(Bz.a(0, [[1, 1]]), 0.0)
                kb.scan(v(Bz, 1), v(tf), zerosT.a(0, [[1, N]]), 0.0, ALU.add, ALU.add)
                kb.tt(v3(tbp), v3(Bz, 1), bc(Bz, 1 + mi), ALU.subtract)
                yield
                kb.act(v(teb), v(tbp), AF.Exp)
                kb.act(v(tbp), v(tbp), AF.Exp, scale=-1.0)
                kb.tt(Qh[0][j].a(0, [[1, N]]), v(qs[j]), v(teb), ALU.mult)
                kb.tt(Kt[0][j].a(0, [[1, N]]), v(tkk), v(tbp), ALU.mult)
                yield
                kb.tt(Kh[0][j].a(0, [[Lc, nch], [1, Lc]]), Kt[0][j].a(0, [[Lc, nch], [1, Lc]]), bc(teb, Lc - 1), ALU.mult)
                kb.tt(tsig.a(0, [[1, nch]]), cv_(Bz, Lc), cv_(Bz, 0), ALU.subtract)
                kb.act(dec[0][j].a(0, [[1, nch]]), tsig.a(0, [[1, nch]]), AF.Exp)
                kb.tt(tsig.a(0, [[1, nch]]), cv_(Bz, 1 + mi), cv_(Bz, 0), ALU.subtract)
                kb.act(smid[j].a(0, [[1, nch]]), tsig.a(0, [[1, nch]]), AF.Exp)
                yield

        def prepM(g, l, S):
            N, Lc, nch = g.N, g.Lc, g.nch
            r4 = lambda t_, o=0: t_.a(o, [[1, N]], 0, 4)
            r3 = lambda t_, o=0: t_.a(o, [[Lc, nch], [1, Lc]], 0, 4)
            rb = lambda t_, o: t_.a(o, [[Lc, nch], [0, Lc]], 0, 4)
            rbl, rml = R
            rlf, ra2, ru = fgr, rml, rbl
            kb.act(r4(fgr), r4(fgr), AF.Exp, scale=-1.0)
            kb.act(r4(fgr), r4(fgr), AF.Ln, bias=1.0)
            kb.ts(r4(rlf), r4(fgr), -1.0, None, ALU.mult)
            kb.ms(Bm.a(0, [[1, 1]], 0, 4), 0.0)
            kb.scan(r4(Bm, 1), r4(rlf), zerosT.a(0, [[1, N]], 0, 4), 0.0, ALU.add, ALU.add)
            if g.kind == "sample":
                for b, (t0, n, _) in enumerate(g.tiles):
                    kb.scan(mz.a(1 + t0, [[1, n]], 0, 4), rlf.a(t0, [[1, n]], 0, 4), igr.a(t0, [[1, n]], 0, 4),
                            m0.a(b, [[1, 1]], 0, 4), ALU.add, ALU.max)
                mprev_b = m0.a(0, [[1, nch], [0, Lc]], 0, 4)
            else:
                kb.cp(mz.a(0, [[1, 1]], 0, 4), S["mst"][l].a(0, [[1, 1]], 0, 4))
                kb.scan(r4(mz, 1), r4(rlf), r4(igr), mz.a(0, [[1, 1]], 0, 4), ALU.add, ALU.max)
                kb.cp(S["mst"][l].a(0, [[1, 1]], 0, 4), mz.a(N, [[1, 1]], 0, 4))
                mprev_b = rb(mz, 0)
            yield
            kb.tt(r3(rbl), r3(Bm, 1), rb(Bm, 0), ALU.subtract)
            kb.tt(r3(rml), r3(mz, 1), mprev_b, ALU.subtract)
            kb.tt(r4(ra2), r4(rbl), r4(rml), ALU.subtract)
            kb.tt(r4(ru), r4(igr), r4(rbl), ALU.subtract)
            kb.tt(r3(ru), r3(ru), mprev_b, ALU.subtract)
            tea, teu = lnv, rstd
            v = lambda t_, o=0: t_.a(o, [[1, N]])
            yield
            for j in range(2):
                lhs = sel.a(j * 128, [[1, 128]], 0, 4)
                ps = accbank()
                kb.mm(ps.a(0, [[1, N]]), lhs, r4(ra2))
                kb.act(v(tea), ps.a(0, [[1, N]]), AF.Exp)
                ps = accbank()
                kb.mm(ps.a(0, [[1, N]]), lhs, r4(ru))
                kb.act(v(teu), ps.a(0, [[1, N]]), AF.Exp)
                ps = accbank()
                kb.mm(ps.a(0, [[1, N]]), lhs, r4(mz, 1))
                kb.act(v(em[j]), ps.a(0, [[1, N]]), AF.Exp, scale=-1.0)
                yield
                kb.tt(Qh[1][j].a(0, [[1, N]]), co.a(j * NM, [[1, N]], key=j), v(tea), ALU.mult)
                kb.stt(Kt[1][j].a(0, [[1, N]]), co.a((2 + j) * NM, [[1, N]], key=2 + j), 0.125, v(teu), ALU.mult, ALU.mult)
                kb.tt(Kh[1][j].a(0, [[Lc, nch], [1, Lc]]), Kt[1][j].a(0, [[Lc, nch], [1, Lc]]),
                      tea.a(Lc - 1, [[Lc, nch], [0, Lc]]), ALU.mult)
                kb.cp(dec[1][j].a(0, [[1, nch]]), tea.a(Lc - 1, [[Lc, nch]]))
                yield

        la_rr = [0]

        def la_tile(x, j, g, ti, Sst, Wd, slot=None):
            t0, n, chunks = g.tiles[ti]
            one = slot is not None
            q_ = slot if one else 2 * x + j
            Q, K_, KH = Qh[x][j], Kt[x][j], Kh[x][j]
            vcol = (0 if x == 0 else 256) + j * 128
            asb, kt_, sb_ = att_sb[q_], khtok[q_], sbf[q_]
            pa = PB[q_]
            kb.mm(pa.a(0, [[1, n]], 0, n), K_.a(t0, [[1, n]], 0, 64), Q.a(t0, [[1, n]], 0, 64), sgc=True)
            kb.tr(pa.a(512, [[1, 128]], 0, n, bf=True), KH.a(t0, [[1, n]]), ident_b.a())
            yield
            kb.mm(pa.a(128, [[1, n]], 0, n), K_.a(t0, [[1, n]], 64, 64), Q.a(t0, [[1, n]], 64, 64), sgc=True)
            yield
            kb.tt(asb.a(0, [[128, 2], [1, n]], 0, n), pa.a(0, [[128, 2], [1, n]], 0, n),
                  maskA.a(0, [[0, 2], [1, n]], 0, n), ALU.mult)
            kb.cp(kt_.a(0, [[1, 128]], 0, n), pa.a(512, [[1, 128]], 0, n, bf=True), eng="act_cp")
            yield
            po = pa if one else PB[4 + q_]
            oo, do = (384, 448) if one else (0, 128)
            for p in range(2):
                kb.mm(po.a(oo, [[1, n]], p * 64, 64), Vall.a(ti * 640 + vcol + p * 64, [[1, 64]], 0, n, key=ti),
                      asb.a(p * 128, [[1, n]], 0, n), start=True, stop=False, sgc=True)
                if x == 1:
                    kb.mm(po.a(do, [[1, n]], p * 64, 64), ones_b.a(0, [[1, 64]], 0, n),
                          asb.a(p * 128, [[1, n]], 0, n), start=False, stop=False, sgc=True)

            def p_matmuls(cs, ce):
                for p in range(2):
                    kb.mm(pa.a(0, [[1, 64]], p * 64, 64), kt_.a(p * 64, [[1, 64]], cs, ce - cs),
                          Vall.a(ti * 640 + vcol + p * 64, [[1, 64]], cs, ce - cs, key=ti), start=True, stop=(x == 0), sgc=True)
                    if x == 1:
                        kb.mm(pa.a(64, [[1, 64]], p * 64, 64), kt_.a(p * 64, [[1, 64]], cs, ce - cs),
                              ones_b.a(0, [[1, 64]], cs, ce - cs), start=False, stop=True, sgc=True)

            for ci, (c, cs, ce) in enumerate(chunks):
                last = ci == len(chunks) - 1
                Sf = Sst(0, 128, 0, Wd)
                if x == 0:
                    kb.ts(sb_.a(0, [[1, Wd]]), Sf, smid[j].a(c, [[1, 1]]), None, ALU.mult)
                else:
                    kb.act(sb_.a(0, [[1, Wd]]), Sf, AF.Copy)
                if not one:
                    p_matmuls(cs, ce)
                yield
                for p in range(2):
                    kb.mm(po.a(oo + cs, [[1, ce - cs]], p * 64, 64), sb_.a(0, [[1, 64]], p * 64, 64),
                          Q.a(t0 + cs, [[1, ce - cs]], p * 64, 64), start=False, stop=last, sgc=True)
                    if x == 1:
                        kb.mm(po.a(do + cs, [[1, ce - cs]], p * 64, 64), sb_.a(64, [[1, 64]], p * 64, 64),
                              Q.a(t0 + cs, [[1, ce - cs]], p * 64, 64), start=False, stop=last, sgc=True)
                    if p == 0:
                        yield
                if one:
                    p_matmuls(cs, ce)
                    yield
                kb.stt(Sf, Sf, dec[x][j].a(c, [[1, 1]]), pa.a(0, [[1, Wd]]), ALU.mult, ALU.add)
                yield
            kb.cp(oall[x][j].a(t0, [[1, n]]), po.a(oo, [[1, n]]), eng="act_cp")
            if x == 1:
                kb.cp(dall[j].a(t0, [[1, n]]), po.a(do, [[1, n]]))

        def run_interleaved(gens):
            gens = list(gens)
            while gens:
                for g_ in list(gens):
                    try:
                        next(g_)
                    except StopIteration:
                        gens.remove(g_)

        def head_norm(g, l, src, nrm, gate, dstc, q_):
            N = g.N
            v = lambda t_: t_.a(0, [[1, N]])
            sq_ = (sqb[0], sqb[1], sqh, sqh2)[q_]
            rs_ = T[q_]
            kb.act(sq_.a(0, [[1, N]]), v(src), AF.Square)
            yield
            ps = accbank()
            kb.mm(ps.a(0, [[1, N]]), bones_b.a(), sq_.a(0, [[1, N]]))
            kb.act(v(rs_), ps.a(0, [[1, N]]), AF.Ln, bias=epsT.a(), scale=1.0 / 64)
            yield
            kb.act(v(rs_), v(rs_), AF.Exp, scale=-0.5)
            yield
            kb.stt(v(src), v(src), nrm, v(rs_), ALU.mult, ALU.mult)
            yield
            kb.tt(hn.a(dstc * NM, [[1, N]], key=dstc), v(src), gate.a(0, [[1, N]]), ALU.mult)

        def postH_gen(g, l, j):
            yield from head_norm(g, l, oall[0][j], hnormT.a(l * 2 + j, [[1, 1]]), ga[j], j, j)

        def postM_gen(g, l, j):
            N = g.N
            v = lambda t_: t_.a(0, [[1, N]])
            t4 = T[4 + j]
            kb.stt(v(t4), v(dall[j]), -1.0, v(dall[j]), ALU.mult, ALU.max)
            kb.tt(v(t4), v(t4), v(em[j]), ALU.max)
            yield
            kb.act(v(t4), v(t4), AF.Ln)
            yield
            kb.act(v(t4), v(t4), AF.Exp, scale=-1.0)
            yield
            kb.tt(v(oall[1][j]), v(oall[1][j]), v(t4), ALU.mult)
            yield
            yield from head_norm(g, l, oall[1][j], mnormT.a(l * 2 + j, [[1, 1]]), go[j], 2 + j, 2 + j)

        def post_all(g, l):
            run_interleaved([postM_gen(g, l, 0), postM_gen(g, l, 1), postH_gen(g, l, 0), postH_gen(g, l, 1)])

        def swa_norm(g, l, ti, po, pd):
            t0, n, _ = g.tiles[ti]
            kb.tt(tdc.a(0, [[128, 4], [1, n]]), pd.a(0, [[128, 4], [1, n]]), esink.a(l * 4, [[1, 4], [0, n]]), ALU.add)
            kb.act(tdc.a(0, [[128, 4], [1, n]]), tdc.a(0, [[128, 4], [1, n]]), AF.Ln)
            kb.act(tdc.a(0, [[128, 4], [1, n]]), tdc.a(0, [[128, 4], [1, n]]), AF.Exp, scale=-1.0)
            dst = hn.a(4 * NM + t0, [[NM, 4], [1, n]], key=(4, 5, 6, 7))
            kb.tt(dst, po.a(0, [[128, 4], [1, n]]), tdc.a(0, [[128, 4], [1, n]]), ALU.mult)

        def swa_tile(g, l, ti, kbs, bset=0, part="S"):
            t0, n, _ = g.tiles[ti]
            po, pd = PB[4 + 2 * bset], PB[5 + 2 * bset]
            if part == "N":
                return swa_norm(g, l, ti, po, pd)
            for bi, (Kf, Vf, nk, mk) in enumerate(kbs):
                for p in range(2):
                    sl = bi * 2 + p
                    ps = accbank()
                    for j in range(4):
                        kb.mm(ps.a(j * 128, [[1, n]], 0, nk), Kf(p), QT.a(j * NM + t0, [[1, n]], p * 64, 64, key=j), sgc=True)
                    kb.act(pesb[sl].a(0, [[128, 4], [1, n]], 0, nk), ps.a(0, [[128, 4], [1, n]], 0, nk), AF.Exp, scale=0.125)
                    kb.tt(pesb[sl].a(0, [[128, 4], [1, n]], 0, nk), pesb[sl].a(0, [[128, 4], [1, n]], 0, nk), mk(nk, n), ALU.mult)
            nb = len(kbs)
            for j in range(4):
                for p in range(2):
                    for bi, (Kf, Vf, nk, mk) in enumerate(kbs):
                        sl = bi * 2 + p
                        kb.mm(po.a(j * 128, [[1, n]], p * 64, 64), Vf(p), pesb[sl].a(j * 128, [[1, n]], 0, nk),
                              start=(j == 0 and bi == 0), stop=(j == 3 and bi == nb - 1), sgc=True)
            for j in range(4):
                for p in range(2):
                    for bi, (Kf, Vf, nk, mk) in enumerate(kbs):
                        sl = bi * 2 + p
                        kb.mm(pd.a(j * 128, [[1, n]], p * 64, 64), ones_b.a(0, [[1, 64]], 0, nk),
                              pesb[sl].a(j * 128, [[1, n]], 0, nk), start=(j == 0 and bi == 0), stop=(j == 3 and bi == nb - 1), sgc=True)

        def wout_stage(g, l):
            N = g.N
            for dc in range(8):
                ps = accbank()
                for k_ in range(8):
                    rhs = hn.a(k_ * NM, [[1, N]], key=k_)
                    kb.mm(ps.a(0, [[1, N]]), WOUT.a(k_ * 1024 + dc * 128, [[1, 128]]), rhs, start=(k_ == 0), stop=(k_ == 7))
                kb.tt(hT.a(dc * NM, [[1, N]], key=dc), hT.a(dc * NM, [[1, N]], key=dc), ps.a(0, [[1, N]]), ALU.add)
                rms_stats(g, dc)

        def ffn_stage(g, l):
            N = g.N
            rmsnorm(g, gffn, l * 8, have_stats=True)
            for b in range(11):
                W = get_fin()
                for cc in range(2):
                    c = 2 * b + cc
                    bp = ((0, 1), (2, 3), (5, 6))[c % 3]
                    pg, pu = PB[bp[0]], PB[bp[1]]
                    for k_ in range(8):
                        kb.mm(pg.a(0, [[1, N]]), W.a(k_ * 256 + cc * 128, [[1, 128]]), hn.a(k_ * NM, [[1, N]], key=k_),
                              start=(k_ == 0), stop=(k_ == 7))
                    for k_ in range(8):
                        kb.mm(pu.a(0, [[1, N]]), W.a(2048 + k_ * 256 + cc * 128, [[1, 128]]), hn.a(k_ * NM, [[1, N]], key=k_),
                              start=(k_ == 0), stop=(k_ == 7))
                    sg = (lnv, rstd)[cc]
                    kb.act(sg.a(0, [[1, N]]), pg.a(0, [[1, N]]), AF.Silu)
                    kb.tt(actT.a(c * NM, [[1, N]], key=c), sg.a(0, [[1, N]]), pu.a(0, [[1, N]]), ALU.mult)
            for dc in range(8):
                W = get_fout()
                ps = PB[WIDE[dc % 7]]
                for c in range(22):
                    kb.mm(ps.a(0, [[1, N]]), W.a(c * 128, [[1, 128]]), actT.a(c * NM, [[1, N]], key=c), start=(c == 0), stop=(c == 21))
                kb.tt(hT.a(dc * NM, [[1, N]], key=dc), hT.a(dc * NM, [[1, N]], key=dc), ps.a(0, [[1, N]]), ALU.add)
                rms_stats(g, dc)

        def seq_init():
            for l in range(L):
                for j in range(2):
                    kb.cp(STW["SH"][l][j].a(), STM["SH"][l][j].a())
                    kb.cp(STW["SM"][l][j].a(), STM["SM"][l][j].a(), eng="act_cp")
                kb.cp(STW["mst"][l].a(0, [[1, 1]], 0, 4), STM["mst"][l].a(0, [[1, 1]], 0, 4))
                kb.cp(STW["cvh"][l].a(), STM["cvh"][l].a())

        def load_sample_states(g, l):
            q = "sp"
            NS = NSAMP
            kb.batch_begin(q, ch_ld)
            with nc.allow_non_contiguous_dma(reason="state layouts"):
                for p in range(2):
                    kb.batch_add(kb.dma(q, SHs.a(0, [[64, NS * 2], [1, 64]], p * 64, 64, key=tuple(range(NS * 2))),
                                        dram_ap(st_hgrn, l * NS * 16384 + p * 4096, [[64, 64], [8192, NS * 2], [1, 64]]), ch_ld))
                    kb.batch_add(kb.dma(q, Xc.a(0, [[64, NS * 2], [1, 64]], p * 64, 64),
                                        dram_ap(st_c, l * NS * 16384 + p * 4096, [[64, 64], [8192, NS * 2], [1, 64]]), ch_ld))
                kb.batch_add(kb.dma(q, Xn.a(0, [[1, 64]], 0, NS * 4), dram_ap(st_n, l * NS * 256, [[64, NS * 4], [1, 64]]), ch_ld))
                kb.batch_add(kb.dma(q, m0.a(0, [[1, NS]], 0, 4), dram_ap(st_m, l * NS * 4, [[1, 4], [4, NS]]), ch_ld))
                kb.batch_add(kb.dma(q, Xv.a(0, [[1, 512]], 0, NS * 3), dram_ap(st_conv, l * NS * 1536, [[512, NS * 3], [1, 512]]), ch_ld))
                kb.batch_add(kb.dma(q, Kc.a(0, [[128, NS], [1, 128]]), dram_ap(c_k, l * NS * 16384, [[128, 128], [16384, NS], [1, 128]]), ch_ld))
                kb.batch_add(kb.dma(q, Vc.a(0, [[128, NS], [1, 128]]), dram_ap(c_v, l * NS * 16384, [[128, 128], [16384, NS], [1, 128]]), ch_ld))
            kb.batch_end(ch_ld)
            for grp8 in range(NS * 2 // 8):
                ps = accbank()
                for p in range(2):
                    for q8 in range(8):
                        bj = grp8 * 8 + q8
                        kb.trx(ps.a(q8 * 64, [[1, 64]], p * 64, 64), Xc.a(bj * 64, [[1, 64]], p * 64, 64),
                               ident_f.a(p * 64, [[1, 64]], p * 64, 64), p)
                kb.cp(SMs.a(grp8 * 8 * 128, [[128, 8], [1, 64]], key=tuple(range(grp8 * 8, grp8 * 8 + 8))), ps.a(0, [[64, 8], [1, 64]]))
            ps = accbank()
            for p in range(2):
                kb.trx(ps.a(0, [[1, NS * 4]], p * 64, 64), Xn.a(0, [[1, 64]], 0, NS * 4), ident_f.a(0, [[1, NS * 4]], 0, NS * 4), p)
                kb.cp(nl.a(0, [[1, NS * 2]], p * 64, 64), ps.a(p, [[2, NS * 2]], p * 64, 64))
            kb.cp(SMs.a(64, [[128, NS * 2], [1, 64]], key=tuple(range(NS * 2))), nl.a(0, [[1, NS * 2], [0, 64]]))
            ps = accbank()
            for ch in range(4):
                kb.tr(ps.a(ch * 128, [[1, NS * 3]]), Xv.a(ch * 128, [[1, 128]], 0, NS * 3), ident_f.a(0, [[1, NS * 3]], 0, NS * 3))
            for ch in range(4):
                kb.cp(xp.a(ch * NS * 7, [[7, NS], [1, 3]], key=ch), ps.a(ch * 128, [[3, NS], [1, 3]]))
            for b in range(NS):
                ps = accbank()
                kb.tr(ps.a(0, [[1, 128]]), Kc.a(b * 128, [[1, 128]]), ident_f.a())
                kb.cp(KTc.a(b * 128, [[1, 128]]), ps.a(0, [[1, 128]]), eng=("act_cp" if b % 2 else "dve"))
            kb.cp(Vcb.a(), Vc.a())

        def store_sample_states(g, l):
            q = "sp"
            NS = NSAMP
            for grp8 in range(NS * 2 // 8):
                ps = accbank()
                for p in range(2):
                    for q8 in range(8):
                        bj = grp8 * 8 + q8
                        kb.trx(ps.a(q8 * 64, [[1, 64]], p * 64, 64), SMs.a(bj * 128, [[1, 64]], p * 64, 64, key=bj),
                               ident_f.a(p * 64, [[1, 64]], p * 64, 64), p)
                kb.cp(Xc.a(grp8 * 8 * 64, [[1, 512]]), ps.a(0, [[1, 512]]))
            kb.cp(nl.a(0, [[1, NS * 2]]), SMs.a(64, [[128, NS * 2]], key=tuple(range(NS * 2))))
            kb.cp(mout.a(0, [[1, NS]], 0, 4), mz.a(4, [[4, NS]], 0, 4))
            ps = accbank()
            for ch in range(4):
                kb.tr(ps.a(ch * 128, [[1, 128]], 0, NS * 3), cvs.a(ch * NS * 3, [[1, NS * 3]]), ident_f.a())
            kb.cp(cso.a(0, [[1, 512]], 0, NS * 3), ps.a(0, [[1, 512]], 0, NS * 3))
            kb.batch_begin(q, ch_st)
            with nc.allow_non_contiguous_dma(reason="state layouts"):
                for p in range(2):
                    kb.batch_add(kb.dma(q, dram_ap(o_hgrn_s, l * NS * 16384 + p * 4096, [[64, 64], [8192, NS * 2], [1, 64]]),
                                        SHs.a(0, [[64, NS * 2], [1, 64]], p * 64, 64, key=tuple(range(NS * 2))), ch_st))
                    kb.batch_add(kb.dma(q, dram_ap(o_c_s, l * NS * 16384 + p * 4096, [[64, 64], [8192, NS * 2], [1, 64]]),
                                        Xc.a(0, [[64, NS * 2], [1, 64]], p * 64, 64), ch_st))
                    kb.batch_add(kb.dma(q, dram_ap(o_n_s, l * NS * 256 + p * 64, [[1, 64], [128, NS * 2]]),
                                        nl.a(0, [[1, NS * 2]], p * 64, 64), ch_st))
                kb.batch_add(kb.dma(q, dram_ap(o_m_s, l * NS * 4, [[1, 4], [4, NS]]), mout.a(0, [[1, NS]], 0, 4), ch_st))
                kb.batch_add(kb.dma(q, dram_ap(o_conv_s, l * NS * 1536, [[512, NS * 3], [1, 512]]), cso.a(0, [[1, 512]], 0, NS * 3), ch_st))
                for b in range(NS):
                    kb.batch_add(kb.dma(q, dram_ap(o_k_s, (l * NS + b) * 16384 + 124 * 128, [[128, 4], [1, 128]]),
                                        Kc.a(b * 128, [[1, 128]], 0, 4), ch_st))
                    kb.batch_add(kb.dma(q, dram_ap(o_v_s, (l * NS + b) * 16384 + 124 * 128, [[128, 4], [1, 128]]),
                                        Vc.a(b * 128, [[1, 128]], 0, 4), ch_st))
            kb.batch_end(ch_st)
            kb.eng[q].dma_start(out=dram_ap(o_k_s, l * NS * 16384, [[16384, NS], [1, 124 * 128]]),
                                in_=dram_ap(c_k, l * NS * 16384 + 4 * 128, [[16384, NS], [1, 124 * 128]])).then_inc(ch_dd.sem, 16)
            ch_dd.val += 16
            kb.eng[q].dma_start(out=dram_ap(o_v_s, l * NS * 16384, [[16384, NS], [1, 124 * 128]]),
                                in_=dram_ap(c_v, l * NS * 16384 + 4 * 128, [[16384, NS], [1, 124 * 128]])).then_inc(ch_dd.sem, 16)
            ch_dd.val += 16

        def store_prompt_states(g, l):
            q = "sp"
            N = g.N
            s = g.seq
            S = STW
            ps = accbank()
            for j in range(2):
                for p in range(2):
                    kb.trx(ps.a(j * 64, [[1, 64]], p * 64, 64), S["SM"][l][j].a(0, [[1, 64]], p * 64, 64),
                           ident_f.a(p * 64, [[1, 64]], p * 64, 64), p)
            kb.cp(cst.a(0, [[1, 128]]), ps.a(0, [[1, 128]]))
            ps2 = accbank()
            for ch in range(4):
                kb.tr(ps2.a(ch * 128, [[1, 128]], 0, 3), S["cvh"][l].a(ch * 3, [[1, 3]]), ident_f.a())
            kb.cp(cvo.a(0, [[1, 512]], 0, 3), ps2.a(0, [[1, 512]], 0, 3))
            kb.batch_begin(q, ch_st)
            with nc.allow_non_contiguous_dma(reason="state layouts"):
                for j in range(2):
                    for p in range(2):
                        h = 2 * j + p
                        base = ((l * NSEQ + s) * 4 + h)
                        kb.batch_add(kb.dma(q, dram_ap(o_hgrn_p, base * 4096, [[64, 64], [1, 64]]),
                                            S["SH"][l][j].a(0, [[1, 64]], p * 64, 64), ch_st))
                        kb.batch_add(kb.dma(q, dram_ap(o_c_p, base * 4096, [[64, 64], [1, 64]]),
                                            cst.a(j * 64, [[1, 64]], p * 64, 64), ch_st))
                        kb.batch_add(kb.dma(q, dram_ap(o_n_p, base * 64, [[1, 64], [1, 1]]),
                                            S["SM"][l][j].a(64, [[1, 1]], p * 64, 64), ch_st))
                kb.batch_add(kb.dma(q, dram_ap(o_m_p, (l * NSEQ + s) * 4, [[1, 4], [1, 1]]), S["mst"][l].a(0, [[1, 1]], 0, 4), ch_st))
                kb.batch_add(kb.dma(q, dram_ap(o_conv_p, (l * NSEQ + s) * 1536, [[512, 3], [1, 512]]),
                                    cvo.a(0, [[1, 512]], 0, 3), ch_st))
                kb.batch_add(kb.dma(q, dram_ap(o_k_p, (l * NSEQ + s) * 16384, [[128, 128], [1, 128]]), kvout.a(0, [[1, 128]]), ch_st))
                kb.batch_add(kb.dma(q, dram_ap(o_v_p, (l * NSEQ + s) * 16384, [[128, 128], [1, 128]]), kvout.a(128, [[1, 128]]), ch_st))
            kb.batch_end(ch_st)

        def run_group(g):
            N = g.N
            S = STM if g.kind == "meta" else STW
            if g.kind == "prompt" and g.gi == 0:
                seq_init()
            gl = f"{g.kind[0]}{g.seq}{g.gi}"
            kb.lab = gl + ":loadx"
            load_x(g)
            for l in range(L):
                if g.kind == "sample":
                    kb.lab = gl + f":L{l}:ldst"
                    load_sample_states(g, l)
                kb.lab = gl + f":L{l}:A"
                acc_pool[0] = WIDE
                stageA(g, l, S)
                dead_tail = (g.kind == "meta" and l == L - 1)
                if not dead_tail:
                    emit_wout(l)
                kb.lab = gl + f":L{l}:prep"
                run_interleaved([stageA_tail(g, l), prepH(g, l), prepM(g, l, S)])
                kb.lab = gl + f":L{l}:tiles"
                acc_pool[0] = [0, 1, 2, 3]
                acc_rr[0] = 0
                def chain_gen(x, j, par=None):
                    for ti in range(len(g.tiles)):
                        if par is not None and ti % 2 != par:
                            continue
                        if g.kind == "sample":
                            if x == 0:
                                st = (lambda b_, j_: (lambda p0, pn, off, w: SHs.a((b_ * 2 + j_) * 64 + off, [[1, w]], p0, pn, key=b_ * 2 + j_)))(ti, j)
                            else:
                                st = (lambda b_, j_: (lambda p0, pn, off, w: SMs.a((b_ * 2 + j_) * 128 + off, [[1, w]], p0, pn, key=b_ * 2 + j_)))(ti, j)
                        else:
                            if x == 0:
                                st = (lambda j_: (lambda p0, pn, off, w: S["SH"][l][j_].a(off, [[1, w]], p0, pn)))(j)
                            else:
                                st = (lambda j_: (lambda p0, pn, off, w: S["SM"][l][j_].a(off, [[1, w]], p0, pn)))(j)
                        yield from la_tile(x, j, g, ti, st, 64 if x == 0 else 128,
                                           slot=None if par is None else par * 4 + 2 * x + j)
                        yield

                if g.kind == "sample" and len(g.tiles) > 1:
                    run_interleaved([chain_gen(x_, j_, par_) for par_ in range(2) for x_ in (1, 0) for j_ in range(2)])
                else:
                    run_interleaved([chain_gen(1, 0), chain_gen(1, 1), chain_gen(0, 0), chain_gen(0, 1)])
                for ti, (t0, n, chunks) in enumerate([] if dead_tail else g.tiles):
                    cur = (lambda t0_, n_, ti_: ((lambda p: KT.a(t0_, [[1, n_]], p * 64, 64)),
                                                 (lambda p: Vall.a(ti_ * 640 + 512 + p * 64, [[1, 64]], 0, n_, key=ti_)),
                                                 n_, (lambda nk, nq: maskC.a(0, [[0, 4], [1, nq]], 0, nk))))(t0, n, ti)
                    if g.kind == "meta":
                        kbs = [cur]
                    elif g.kind == "sample":
                        b = ti
                        prev = ((lambda p, b=b: KTc.a(b * 128, [[1, 128]], p * 64, 64)),
                                (lambda p, b=b: Vcb.a(b * 128 + p * 64, [[1, 64]])),
                                128, (lambda nk, nq: maskP.a(0, [[0, 4], [1, nq]], 0, nk)))
                        kbs = [prev, cur]
                    else:
                        if ti > 0:
                            pt0 = g.tiles[ti - 1][0]
                            prev = ((lambda p, pt0=pt0: KT.a(pt0, [[1, 128]], p * 64, 64)),
                                    (lambda p, ti=ti: Vall.a((ti - 1) * 640 + 512 + p * 64, [[1, 64]], key=ti - 1)),
                                    128, (lambda nk, nq: maskP.a(0, [[0, 4], [1, nq]], 0, nk)))
                        elif g.gi > 0:
                            prev = ((lambda p: STW["sK"][l].a(0, [[1, 128]], p * 64, 64)),
                                    (lambda p: STW["sV"][l].a(p * 64, [[1, 64]])),
                                    128, (lambda nk, nq: maskP.a(0, [[0, 4], [1, nq]], 0, nk)))
                        else:
                            prev = ((lambda p: STM["sK"][l].a(0, [[1, 16]], p * 64, 64)),
                                    (lambda p: STM["sV"][l].a(p * 64, [[1, 64]], 0, 16)),
                                    16, (lambda nk, nq: maskM.a(0, [[0, 4], [1, nq]], 0, nk)))
                        kbs = [prev, cur]
                    swa_tile(g, l, ti, kbs, bset=ti % 2, part="S")
                    if ti > 0:
                        swa_tile(g, l, ti - 1, None, bset=(ti - 1) % 2, part="N")
                if not dead_tail:
                    swa_tile(g, l, len(g.tiles) - 1, None, bset=(len(g.tiles) - 1) % 2, part="N")
                if g.kind == "meta":
                    kb.cp(STM["sK"][l].a(0, [[1, 16]]), KT.a(0, [[1, 16]]))
                    kb.cp(STM["sV"][l].a(0, [[1, 128]], 0, 16), Vall.a(512, [[1, 128]], 0, 16, key=0))
                elif g.kind == "prompt" and not g.last:
                    lt = len(g.tiles) - 1
                    kb.cp(STW["sK"][l].a(), KT.a(g.tiles[lt][0], [[1, 128]]))
                    kb.cp(STW["sV"][l].a(), Vall.a(lt * 640 + 512, [[1, 128]], key=lt))
                kb.lab = gl + f":L{l}:post"
                if not dead_tail:
                    post_all(g, l)
                if g.kind == "sample":
                    store_sample_states(g, l)
                if g.kind == "prompt" and g.last:
                    store_prompt_states(g, l)
                if dead_tail:
                    continue
                kb.lab = gl + f":L{l}:wout"
                wout_stage(g, l)
                kb.lab = gl + f":L{l}:ffn"
                ffn_stage(g, l)
            kb.lab = gl + ":final"
            if g.kind == "meta":
                rms_stats(g, None, flush=True)
            if g.kind != "meta":
                rmsnorm(g, gfin, 0, dst_bf=False, have_stats=True)
                store_y(g)

        for g in groups:
            run_group(g)

    _cp = kb.cp

    def cp2(out, in_, eng="dve"):
        if eng == "act_cp":
            return kb.act(out, in_, AF.Copy)
        return _cp(out, in_, eng)

    kb.cp = cp2

    build_consts()
    load_params()
    run_phase(groups1, 64 if NSAMP else 16)
    kb.barrier()
    if groups2:
        run_phase(groups2, 512)
    for c in kb.chans:
        if c.val:
            kb._wait("sp", c, c.val)
    kb.es.close()
    return kb


OUT_NAMES = ["y_prompt", "y_sample", "hgrn_p", "mlstm_c_p", "mlstm_n_p", "mlstm_m_p", "mlstm_conv_p", "swa_k_p", "swa_v_p",
             "hgrn_s", "mlstm_c_s", "mlstm_n_s", "mlstm_m_s", "mlstm_conv_s", "swa_k_s", "swa_v_s"]
PROMPT_OUT = set(OUT_NAMES[:1] + OUT_NAMES[2:9])
BATCH_AXIS = {"y_prompt": 0, "y_sample": 0}
SAMPLE_IN = ["state_hgrn", "state_mlstm_c", "state_mlstm_n", "state_mlstm_m", "state_mlstm_conv", "cache_swa_k", "cache_swa_v"]


def make_in_maps(inputs, ncores, nseq, nsamp):
    maps = []
    for c in range(ncores):
        m = {}
        for k_, v in inputs.items():
            v = np.asarray(v)
            if k_ == "x_prompt":
                m[k_] = np.ascontiguousarray(v[c * nseq:(c + 1) * nseq])
            elif k_ == "x_sample":
                m[k_] = np.ascontiguousarray(v[c * nsamp:(c + 1) * nsamp])
            elif k_ in SAMPLE_IN:
                m[k_] = np.ascontiguousarray(v[:, c * nsamp:(c + 1) * nsamp])
            else:
                m[k_] = np.ascontiguousarray(v)
        maps.append(m)
    return maps


def run(inputs, ncores=NCORES, NSEQ=2, NGRP=4, NSAMP=16, debug=False, trace=False):
    k1 = build(None, NSEQ, NGRP, NSAMP, debug)
    needed = k1.rec
    k2 = build(needed, NSEQ, NGRP, NSAMP, debug)
    in_maps = make_in_maps(inputs, ncores, NSEQ, NSAMP)
    res = run_bass_kernel_spmd(k2.nc, in_maps, core_ids=list(range(ncores)), trace=trace)
    outs = []
    for name in OUT_NAMES:
        ax = 0 if name in BATCH_AXIS else 1
        outs.append(np.concatenate([np.asarray(r[name]) for r in res.results], axis=ax).astype(np.float32))
    return tuple(outs), res


def kernel(**inputs):
    outs, _ = run(inputs)
    return outs
```

```python
import numpy as np
from contextlib import ExitStack
import concourse.bass as bass
import concourse.mybir as mybir
from concourse.bass_utils import run_bass_kernel_spmd

F32 = mybir.dt.float32
BF16 = mybir.dt.bfloat16
ALU = mybir.AluOpType
AF = mybir.ActivationFunctionType

D = 1024
DIN = 2824
DFF = 2816
L = 2
NCORES = 8
EPS = 1e-6


class Res:
    __slots__ = ("lw", "rd", "name")

    def __init__(self, name):
        self.lw = None
        self.rd = {}
        self.name = name


class A:
    __slots__ = ("ap", "res", "p0", "bank")

    def __init__(self, ap, res, p0=0):
        self.ap = ap
        self.res = res
        self.p0 = p0
        self.bank = None


class Chan:
    def __init__(self, sem, name):
        self.sem = sem
        self.val = 0
        self.name = name


class TT:
    def __init__(self, kb, name, F, dt, psum=False):
        self.kb = kb
        name = name + kb.tag
        self.name = name
        self.F = F
        self.dt = dt
        if psum:
            self.t = kb.es.enter_context(kb.nc.psum_tensor(name, [128, F], dt))
        else:
            self.t = kb.es.enter_context(kb.nc.sbuf_tensor(name, [128, F], dt))
        self.res = {}

    def r(self, key=0):
        if key not in self.res:
            self.res[key] = Res(f"{self.name}:{key}")
        return self.res[key]

    def a(self, off=0, dims=None, p0=0, pn=128, key=0):
        if dims is None:
            dims = [[1, self.F - off]]
        ap = bass.AP(self.t, p0 * self.F + off, [[self.F, pn]] + [list(d) for d in dims])
        if isinstance(key, tuple):
            return A(ap, tuple(self.r(k_) for k_ in key), p0)
        return A(ap, self.r(key), p0)


class Bank:
    def __init__(self, kb, name):
        self.t = kb.es.enter_context(kb.nc.psum_tensor(name, [128, 512], F32))
        self.tb = self.t.bitcast(BF16)
        self.res = Res(name)
        self.res_pe = None

    def a(self, off=0, dims=None, p0=0, pn=128, key=0, bf=False):
        F = 1024 if bf else 512
        if dims is None:
            dims = [[1, F - off]]
        ap = bass.AP(self.tb if bf else self.t, p0 * F + off, [[F, pn]] + [list(d) for d in dims])
        a = A(ap, self.res, p0)
        a.bank = self
        return a


class Arena:
    PAGE = 1024

    def __init__(self, kb, name, nbytes):
        nbytes = -(-nbytes // self.PAGE) * self.PAGE
        self.nbytes = nbytes
        self.t32 = kb.es.enter_context(kb.nc.sbuf_tensor(name + kb.tag, [128, nbytes // 4], F32))
        self.tb = self.t32.bitcast(BF16)
        self.pages = [Res(f"{name}:pg{i}") for i in range(nbytes // self.PAGE)]

    def view(self, boff, F, dt):
        return AV(self, boff, F, dt)


class AV:
    def __init__(self, ar, boff, F, dt):
        self.ar, self.boff, self.F, self.dt = ar, boff, F, dt
        self.es = 4 if dt == F32 else 2
        assert boff % self.es == 0
        self.h = ar.t32 if dt == F32 else ar.tb
        self.ps = ar.nbytes // self.es

    def a(self, off=0, dims=None, p0=0, pn=128, key=0):
        if dims is None:
            dims = [[1, self.F - off]]
        base = self.boff // self.es + off
        ap = bass.AP(self.h, p0 * self.ps + base, [[self.ps, pn]] + [list(d) for d in dims])
        span = sum((c - 1) * abs(st) for st, c in dims)
        b0 = base * self.es
        b1 = (base + span + 1) * self.es - 1
        pg = self.ar.pages[b0 // Arena.PAGE: b1 // Arena.PAGE + 1]
        return A(ap, tuple(pg), p0)


class KB:
    ENG = ("pe", "act", "dve", "pool")

    def __init__(self, needed=None):
        self.nc = bass.Bass("TRN2", target_bir_lowering=False)
        self.es = ExitStack()
        nc = self.nc
        self.eng = {"pe": nc.tensor, "act": nc.scalar, "dve": nc.vector, "pool": nc.gpsimd, "sp": nc.sync}
        self.sem = {e: self.es.enter_context(nc.semaphore("s_" + e)) for e in self.ENG}
        self.idx = {e: 0 for e in self.ENG}
        self.needed = needed
        if needed is not None:
            self.rank = {e: {i: r for r, i in enumerate(sorted(needed[e]))} for e in self.ENG}
        self.rec = {e: set() for e in self.ENG}
        self.waited = {e: {} for e in ("pe", "act", "dve", "pool", "sp")}
        self.chans = []
        self.tag = ""
        self.lab = ""
        self.labels = {e: [] for e in self.ENG}
        self.dbg = []
        self.ninst = 0

    def chan(self, name):
        c = Chan(self.es.enter_context(self.nc.semaphore("c_" + name)), name)
        self.chans.append(c)
        return c

    def _wait(self, eng, src, i):
        if isinstance(src, Chan):
            val, sem = i, src.sem
        else:
            self.rec[src].add(i)
            val = (i + 1) if self.needed is None else self.rank[src][i] + 1
            sem = self.sem[src]
        w = self.waited[eng]
        if w.get(src, 0) >= val:
            return
        w[src] = val
        self.eng[eng].wait_ge(sem, val)

    def _deps(self, eng, reads, writes, skip=None):
        for r in reads:
            if r.lw is not None:
                src, i = r.lw
                if (src == eng and eng == "pe") or src is skip:
                    continue
                self._wait(eng, src, i)
        for w in writes:
            if w.lw is not None:
                src, i = w.lw
                if (src != eng or eng != "pe") and src is not skip:
                    self._wait(eng, src, i)
            for src, i in w.rd.items():
                if (src != eng or eng != "pe") and src is not skip:
                    self._wait(eng, src, i)

    @staticmethod
    def _flat(xs):
        out = []
        for x in xs:
            if isinstance(x, A):
                x = x.res
            if isinstance(x, Res):
                out.append(x)
            elif isinstance(x, (tuple, list)):
                out.extend(x)
        return out

    def emit(self, eng, fn, reads, writes):
        writes = list(writes) + [x for x in reads if isinstance(x, A) and x.bank is not None]
        reads = self._flat(reads)
        writes = self._flat(writes)
        self._deps(eng, reads, writes)
        ins = fn()
        i = self.idx[eng]
        self.idx[eng] = i + 1
        self.labels[eng].append(self.lab)
        self.ninst += 1
        if self.needed is None or i in self.rank[eng]:
            ins.then_inc(self.sem[eng], 1)
        for r in reads:
            r.rd[eng] = i
        for w in writes:
            w.lw = (eng, i)
            w.rd = {}
        return (eng, i)

    def dma(self, q, out, in_, ch, **kw):
        reads = self._flat([in_]) if isinstance(in_, A) else []
        writes = self._flat([out]) if isinstance(out, A) else []
        self._deps(q, reads, writes, skip=ch)
        o = out.ap if isinstance(out, A) else out
        i_ = in_.ap if isinstance(in_, A) else in_
        self.eng[q].dma_start(out=o, in_=i_, **kw).then_inc(ch.sem, 16)
        ch.val += 16
        self.ninst += 1
        for r in reads:
            r.rd[ch] = ch.val
        for w in writes:
            w.lw = (ch, ch.val)
            w.rd = {}
        return reads + writes

    def batch_begin(self, q, ch):
        if ch.val:
            self._wait(q, ch, ch.val)
        self._batch = []

    def batch_add(self, touched):
        self._batch += touched

    def batch_end(self, ch):
        for r in self._batch:
            if r.lw is not None and r.lw[0] is ch:
                r.lw = (ch, ch.val)
            if ch in r.rd:
                r.rd[ch] = ch.val

    def barrier(self):
        last = {e: self.idx[e] - 1 for e in self.ENG}
        for e in ("pe", "act", "dve", "pool", "sp"):
            for s in self.ENG:
                if s != e and last[s] >= 0:
                    self._wait(e, s, last[s])
            for c in self.chans:
                if c.val:
                    self._wait(e, c, c.val)

    def _rowsync(self, out, kbase):
        b = out.bank
        if b is not None and b.res_pe is not None and b.res_pe[0] != kbase:
            self._wait("pe", "pe", b.res_pe[1])

    def mm(self, out, lhsT, rhs, start=True, stop=True, sgc=False):
        self._rowsync(out, lhsT.p0)
        t = self.emit("pe", lambda: self.nc.tensor.matmul(out.ap, lhsT=lhsT.ap, rhs=rhs.ap, start=start, stop=stop,
                                                          skip_group_check=sgc), [lhsT, rhs], [out])
        if out.bank is not None:
            out.bank.res_pe = (lhsT.p0, t[1])
        return t

    def tr(self, out, in_, ident):
        self._rowsync(out, in_.p0)
        t = self.emit("pe", lambda: self.nc.tensor.transpose(out.ap, in_.ap, ident.ap), [in_, ident], [out])
        if out.bank is not None:
            out.bank.res_pe = (in_.p0, t[1])
        return t

    def trx(self, out, in_, ident, p):
        if p == 0:
            return self.tr(out, in_, ident)
        return self.mm(out, in_, ident)

    def act(self, out, in_, func, bias=None, scale=1.0):
        kw = {}
        if bias is not None:
            kw["bias"] = bias.ap if isinstance(bias, A) else bias
        kw["scale"] = scale.ap if isinstance(scale, A) else scale
        return self.emit("act", lambda: self.nc.scalar.activation(out=out.ap, in_=in_.ap, func=func, **kw),
                         [in_, bias, scale], [out])

    def tt(self, out, in0, in1, op, eng="dve"):
        return self.emit(eng, lambda: self.eng[eng].tensor_tensor(out=out.ap, in0=in0.ap, in1=in1.ap, op=op),
                         [in0, in1], [out])

    def ts(self, out, in0, s1, s2, op0, op1=None, eng="dve"):
        a1 = s1.ap if isinstance(s1, A) else s1
        a2 = s2.ap if isinstance(s2, A) else s2
        if op1 is None:
            f = lambda: self.eng[eng].tensor_scalar(out=out.ap, in0=in0.ap, scalar1=a1, scalar2=None, op0=op0)
        else:
            f = lambda: self.eng[eng].tensor_scalar(out=out.ap, in0=in0.ap, scalar1=a1, scalar2=a2, op0=op0, op1=op1)
        return self.emit(eng, f, [in0, s1, s2], [out])

    def stt(self, out, in0, sc, in1, op0, op1, eng="dve"):
        a = sc.ap if isinstance(sc, A) else sc
        return self.emit(eng, lambda: self.eng[eng].scalar_tensor_tensor(out=out.ap, in0=in0.ap, scalar=a, in1=in1.ap,
                                                                          op0=op0, op1=op1), [in0, sc, in1], [out])

    def cp(self, out, in_, eng="dve"):
        return self.emit(eng, lambda: self.eng[eng].tensor_copy(out=out.ap, in_=in_.ap), [in_], [out])

    def ms(self, out, val, eng="dve"):
        return self.emit(eng, lambda: self.eng[eng].memset(out.ap, val), [], [out])

    def scan(self, out, d0, d1, init, op0, op1):
        a = init.ap if isinstance(init, A) else init
        return self.emit("dve", lambda: self.nc.vector.tensor_tensor_scan(out=out.ap, data0=d0.ap, data1=d1.ap, initial=a,
                                                                          op0=op0, op1=op1), [d0, d1, init], [out])

    def recip(self, out, in_):
        return self.emit("dve", lambda: self.nc.vector.reciprocal(out=out.ap, in_=in_.ap), [in_], [out])

    def asel(self, out, in_, pattern, op, fill, base, cm):
        return self.emit("pool", lambda: self.nc.gpsimd.affine_select(out=out.ap, in_=in_.ap, pattern=pattern, compare_op=op,
                                                                       fill=fill, base=base, channel_multiplier=cm),
                         [in_], [out])


class Grp:
    def __init__(self, kind, N, Lc, tiles, seq=0, gi=0, last=False):
        self.kind, self.N, self.Lc, self.tiles, self.seq, self.gi, self.last = kind, N, Lc, tiles, seq, gi, last
        self.nch = N // Lc


def dram_ap(t, off, dims):
    return bass.AP(t, off, [list(d) for d in dims])


CONV_ENG = "dve"
S1_ENG = "pool"


def build(needed=None, NSEQ=2, NGRP=4, NSAMP=16, debug=False):
    kb = KB(needed)
    nc = kb.nc
    SEQ = NGRP * 512
    NMAX = 512

    def din(name, shape):
        return nc.dram_tensor(name, list(shape), F32, kind="ExternalInput")

    def dout(name, shape):
        return nc.dram_tensor(name, list(shape), F32, kind="ExternalOutput")

    x_prompt = din("x_prompt", [NSEQ, SEQ, D])
    x_sample = din("x_sample", [NSAMP, 4, D])
    st_hgrn = din("state_hgrn", [L, NSAMP, 4, 64, 64])
    st_c = din("state_mlstm_c", [L, NSAMP, 4, 64, 64])
    st_n = din("state_mlstm_n", [L, NSAMP, 4, 64])
    st_m = din("state_mlstm_m", [L, NSAMP, 4])
    st_conv = din("state_mlstm_conv", [L, NSAMP, 3, 512])
    c_k = din("cache_swa_k", [L, NSAMP, 128, 2, 64])
    c_v = din("cache_swa_v", [L, NSAMP, 128, 2, 64])
    meta_tokens = din("meta_tokens", [16, D])
    norm_mix = din("norm_mix", [L, D])
    norm_ffn = din("norm_ffn", [L, D])
    norm_final = din("norm_final", [D])
    w_in = din("w_in", [L, D, DIN])
    lb_logits = din("hgrn_lb_logits", [L, 256])
    hgrn_norm = din("hgrn_norm", [L, 256])
    conv_w = din("mlstm_conv_w", [L, 4, 512])
    conv_b = din("mlstm_conv_b", [L, 512])
    i_bias = din("mlstm_i_bias", [L, 4])
    f_bias = din("mlstm_f_bias", [L, 4])
    mlstm_norm = din("mlstm_norm", [L, 256])
    sinks = din("attn_sinks", [L, 8])
    w_out = din("w_out", [L, D, D])
    w_ffn_in = din("w_ffn_in", [L, D, 2 * DFF])
    w_ffn_out = din("w_ffn_out", [L, DFF, D])

    y_prompt = dout("y_prompt", [NSEQ, SEQ, D])
    y_sample = dout("y_sample", [NSAMP, 4, D])
    o_hgrn_p = dout("hgrn_p", [L, NSEQ, 4, 64, 64])
    o_c_p = dout("mlstm_c_p", [L, NSEQ, 4, 64, 64])
    o_n_p = dout("mlstm_n_p", [L, NSEQ, 4, 64])
    o_m_p = dout("mlstm_m_p", [L, NSEQ, 4])
    o_conv_p = dout("mlstm_conv_p", [L, NSEQ, 3, 512])
    o_k_p = dout("swa_k_p", [L, NSEQ, 128, 2, 64])
    o_v_p = dout("swa_v_p", [L, NSEQ, 128, 2, 64])
    o_hgrn_s = dout("hgrn_s", [L, NSAMP, 4, 64, 64])
    o_c_s = dout("mlstm_c_s", [L, NSAMP, 4, 64, 64])
    o_n_s = dout("mlstm_n_s", [L, NSAMP, 4, 64])
    o_m_s = dout("mlstm_m_s", [L, NSAMP, 4])
    o_conv_s = dout("mlstm_conv_s", [L, NSAMP, 3, 512])
    o_k_s = dout("swa_k_s", [L, NSAMP, 128, 2, 64])
    o_v_s = dout("swa_v_s", [L, NSAMP, 128, 2, 64])

    def sb(name, F, dt=F32):
        return TT(kb, name, F, dt)

    ident_f = sb("ident_f", 128)
    ident_b = sb("ident_b", 128, BF16)
    ones_b = sb("ones_b", 128, BF16)
    ones_f = sb("ones_f", 128)
    bones_b = sb("bones_b", 128, BF16)
    maskA = sb("maskA", 128, BF16)
    maskC = sb("maskC", 128, BF16)
    maskP = sb("maskP", 128, BF16)
    maskM = sb("maskM", 128, BF16)
    sel = sb("sel", 256)
    epsT = sb("epsT", 1)
    zerosT = sb("zerosT", NMAX)
    scr = sb("scr", 256)
    gmix = sb("gmix", L * 8)
    gffn = sb("gffn", L * 8)
    gfin = sb("gfin", 8)
    lbT = sb("lbT", L * 2)
    omlT = sb("omlT", L * 2)
    nomlT = sb("nomlT", L * 2)
    lgT = sb("lgT", L * 2)
    hnormT = sb("hnormT", L * 2)
    mnormT = sb("mnormT", L * 2)
    cwT = sb("cwT", L * 16)
    cbT = sb("cbT", L * 4)
    ibT = sb("ibT", L)
    fbT = sb("fbT", L)
    skrow = sb("skrow", 16)
    esink = sb("esink", L * 4)

    WIN = [sb(f"win{i}", 8 * 512, BF16) for i in range(2)]
    WOUT = sb("woutb", 8 * 1024, BF16)
    NFIN, NFOUT = 3, 3
    FIN = [sb(f"fin{i}", 2 * 8 * 256, BF16) for i in range(NFIN)]
    FOUT = [sb(f"fout{i}", 22 * 128, BF16) for i in range(NFOUT)]
    ch_win = [kb.chan(f"win{i}") for i in range(2)]
    ch_wout = kb.chan("wout")
    ch_fin = [kb.chan(f"fin{i}") for i in range(NFIN)]
    ch_fout = [kb.chan(f"fout{i}") for i in range(NFOUT)]
    ch_par = kb.chan("par")
    ch_x = [kb.chan("x0"), kb.chan("x1")]
    ch_y = [kb.chan("y0"), kb.chan("y1")]
    ch_st = kb.chan("stout")
    ch_ld = kb.chan("stld")
    ch_dd = kb.chan("d2d")

    def stset(tag):
        return dict(
            SH=[[sb(f"SH{tag}{l}{j}", 64) for j in range(2)] for l in range(L)],
            SM=[[sb(f"SM{tag}{l}{j}", 128) for j in range(2)] for l in range(L)],
            mst=[sb(f"mst{tag}{l}", 1) for l in range(L)],
            cvh=[sb(f"cvh{tag}{l}", 12) for l in range(L)],
            sK=[sb(f"sK{tag}{l}", 128, BF16) for l in range(L)],
            sV=[sb(f"sV{tag}{l}", 128, BF16) for l in range(L)],
        )

    STM = stset("M")
    STW = stset("W")

    PB = [Bank(kb, f"pb{i}") for i in range(8)]

    def build_consts():
        kb.ms(ident_f.a(), 1.0, eng="pool")
        kb.asel(ident_f.a(), ident_f.a(), [[-1, 128]], ALU.is_equal, 0.0, 0, 1)
        kb.cp(ident_b.a(), ident_f.a(), eng="pool")
        kb.ms(ones_b.a(), 1.0, eng="pool")
        kb.ms(ones_f.a(), 1.0, eng="pool")
        kb.ms(bones_b.a(), 1.0, eng="pool")
        kb.ms(bones_b.a(64, [[1, 64]], 0, 64), 0.0, eng="pool")
        kb.ms(bones_b.a(0, [[1, 64]], 64, 64), 0.0, eng="pool")
        t = scr
        kb.ms(t.a(0, [[1, 128]]), 1.0, eng="pool")
        kb.asel(t.a(0, [[1, 128]]), t.a(0, [[1, 128]]), [[1, 128]], ALU.is_ge, 0.0, 0, -1)
        kb.cp(maskC.a(), t.a(0, [[1, 128]]), eng="pool")
        kb.cp(maskA.a(), t.a(0, [[1, 128]]), eng="pool")
        kb.ms(maskA.a(64, [[1, 64]], 0, 64), 0.0, eng="pool")
        kb.ms(t.a(0, [[1, 128]]), 1.0, eng="pool")
        kb.asel(t.a(0, [[1, 128]]), t.a(0, [[1, 128]]), [[-1, 128]], ALU.is_gt, 0.0, 0, 1)
        kb.cp(maskP.a(), t.a(0, [[1, 128]]), eng="pool")
        kb.ms(t.a(0, [[1, 128]]), 1.0, eng="pool")
        kb.asel(t.a(0, [[1, 128]]), t.a(0, [[1, 128]]), [[-1, 128]], ALU.is_gt, 0.0, 112, 1)
        kb.cp(maskM.a(), t.a(0, [[1, 128]]), eng="pool")
        kb.ms(sel.a(), 1.0, eng="pool")
        kb.asel(sel.a(), sel.a(), [[1, 256]], ALU.is_ge, 0.0, 0, -64)
        kb.asel(sel.a(), sel.a(), [[-1, 256]], ALU.is_ge, 0.0, 63, 64)
        kb.ms(epsT.a(), EPS, eng="pool")
        kb.ms(zerosT.a(), 0.0, eng="pool")
        for S in (STM,):
            for l in range(L):
                for j in range(2):
                    kb.ms(S["SH"][l][j].a(), 0.0, eng="pool")
                    kb.ms(S["SM"][l][j].a(), 0.0, eng="pool")
                kb.ms(S["mst"][l].a(), 0.0, eng="pool")
                kb.ms(S["cvh"][l].a(), 0.0, eng="pool")

    def load_params():
        q = "sp"
        kb.batch_begin(q, ch_par)
        with nc.allow_non_contiguous_dma(reason="tiny param layouts"):
            def ld(dst, src):
                kb.batch_add(kb.dma(q, dst, src, ch_par))
            for l in range(L):
                ld(gmix.a(l * 8, [[1, 8]]), dram_ap(norm_mix, l * D, [[1, 128], [128, 8]]))
                ld(gffn.a(l * 8, [[1, 8]]), dram_ap(norm_ffn, l * D, [[1, 128], [128, 8]]))
                ld(lgT.a(l * 2, [[1, 2]]), dram_ap(lb_logits, l * 256, [[1, 128], [128, 2]]))
                ld(hnormT.a(l * 2, [[1, 2]]), dram_ap(hgrn_norm, l * 256, [[1, 128], [128, 2]]))
                ld(mnormT.a(l * 2, [[1, 2]]), dram_ap(mlstm_norm, l * 256, [[1, 128], [128, 2]]))
                for j in range(4):
                    ld(cwT.a(l * 16 + j * 4, [[1, 4]]), dram_ap(conv_w, (l * 4 + j) * 512, [[1, 128], [128, 4]]))
                ld(cbT.a(l * 4, [[1, 4]]), dram_ap(conv_b, l * 512, [[1, 128], [128, 4]]))
                ld(ibT.a(l, [[1, 1]], 0, 4), dram_ap(i_bias, l * 4, [[1, 4], [1, 1]]))
                ld(fbT.a(l, [[1, 1]], 0, 4), dram_ap(f_bias, l * 4, [[1, 4], [1, 1]]))
            ld(gfin.a(), dram_ap(norm_final, 0, [[1, 128], [128, 8]]))
            ld(skrow.a(0, [[1, 16]], 0, 1), dram_ap(sinks, 0, [[16, 1], [1, 16]]))
        kb.batch_end(ch_par)
        t = scr
        l0, l1 = lgT.a(0, [[1, 2]]), lgT.a(2, [[1, 2]])
        mx, e0, e1, s_, r_ = (t.a(i * 2, [[1, 2]]) for i in range(5))
        kb.tt(mx, l0, l1, ALU.max)
        kb.tt(e0, l0, mx, ALU.subtract)
        kb.tt(e1, l1, mx, ALU.subtract)
        kb.act(e0, e0, AF.Exp)
        kb.act(e1, e1, AF.Exp)
        kb.tt(s_, e0, e1, ALU.add)
        kb.recip(r_, s_)
        kb.tt(e0, e0, r_, ALU.mult)
        kb.tt(e1, e1, r_, ALU.mult)
        kb.tt(lbT.a(0, [[1, 2]]), e0, e0, ALU.subtract)
        kb.tt(s_, e0, e1, ALU.add)
        kb.tt(lbT.a(2, [[1, 2]]), s_, e0, ALU.subtract)
        kb.ts(omlT.a(), lbT.a(), -1.0, 1.0, ALU.mult, ALU.add)
        kb.ts(nomlT.a(), omlT.a(), -1.0, None, ALU.mult)
        ps = PB[0]
        kb.mm(ps.a(0, [[1, 16]]), ones_f.a(0, [[1, 128]], 0, 1), skrow.a(0, [[1, 16]], 0, 1))
        for l in range(L):
            for p in range(2):
                kb.act(esink.a(l * 4, [[1, 4]], p * 64, 64), ps.a(l * 8 + p * 4, [[1, 4]], p * 64, 64), AF.Exp)

    wl_win, wl_fin, wl_fout = [], [], []
    state = dict(win_i=0, fin_i=0, fout_i=0, win_e=0, fin_e=0, fout_e=0)

    def wsrc(t, l, rows, col0, ncol, row0=0, nk=8, pn=128, ld=None):
        return dram_ap(t, (l * rows + row0) * ld + col0, [[ld, pn], [128 * ld, nk], [1, ncol]])

    def win_blocks(l):
        W = lambda c0, nc_: wsrc(w_in, l, D, c0, nc_, ld=DIN)
        b = []
        b.append([((0, 512), W(0, 512))])
        b.append([((0, 256), W(768, 256)), ((256, 256), W(1792, 256))])
        b.append([((0, 512), W(1024, 512))])
        f4 = []
        for p in range(2):
            for j in range(4):
                f4.append(((j * 128 + p * 64, 64), W(2056 + (j + 4 * p) * 64, 64)))
        b.append([((0, 128), W(2568, 128)), ((128, 8), W(2048, 8))])
        b.append(f4)
        b.append([((0, 256), W(512, 256)), ((256, 256), W(1536, 256))])
        b.append([((0, 256), W(2568, 256))])
        return b

    def emit_win(i):
        l, blk = wl_win[i]
        slot = i % 2
        T = WIN[slot]
        for (c0, ncol), src in win_blocks(l)[blk]:
            dst = T.a(c0, [[512, 8], [1, ncol]])
            kb.dma("pool", dst, src, ch_win[slot])

    def emit_wout(l):
        T = WOUT
        kb.dma("pool", T.a(0, [[1024, 4], [1, 1024]]), wsrc(w_out, l, D, 0, 1024, nk=4, ld=D), ch_wout)
        for p in range(2):
            src = dram_ap(w_out, (l * D + 512 + p * 256) * D, [[D, 64], [64 * D, 4], [1, 1024]])
            kb.dma("pool", T.a(4 * 1024, [[1024, 4], [1, 1024]], p * 64, 64), src, ch_wout)

    def emit_fin(i):
        l, b = wl_fin[i]
        slot = i % NFIN
        T = FIN[slot]
        kb.dma("pool", T.a(0, [[256, 8], [1, 256]]), wsrc(w_ffn_in, l, D, b * 256, 256, ld=2 * DFF), ch_fin[slot])
        kb.dma("pool", T.a(2048, [[256, 8], [1, 256]]), wsrc(w_ffn_in, l, D, DFF + b * 256, 256, ld=2 * DFF), ch_fin[slot])

    def emit_fout(i):
        l, dc = wl_fout[i]
        slot = i % NFOUT
        T = FOUT[slot]
        kb.dma("pool", T.a(0, [[128, 22], [1, 128]]), wsrc(w_ffn_out, l, DFF, dc * 128, 128, nk=22, ld=D), ch_fout[slot])

    def get_win():
        i = state["win_i"]
        state["win_i"] += 1
        while state["win_e"] < min(len(wl_win), i + 2):
            emit_win(state["win_e"])
            state["win_e"] += 1
        return WIN[i % 2]

    def get_fin():
        i = state["fin_i"]
        state["fin_i"] += 1
        while state["fin_e"] < min(len(wl_fin), i + NFIN):
            emit_fin(state["fin_e"])
            state["fin_e"] += 1
        return FIN[i % NFIN]

    def get_fout():
        i = state["fout_i"]
        state["fout_i"] += 1
        while state["fout_e"] < min(len(wl_fout), i + NFOUT):
            emit_fout(state["fout_e"])
            state["fout_e"] += 1
        return FOUT[i % NFOUT]

    groups1 = [Grp("meta", 16, 16, [(0, 16, [(0, 0, 16)])])]
    if NSAMP:
        groups1.append(Grp("sample", NSAMP * 4, 4, [(b * 4, 4, [(b, 0, 4)]) for b in range(NSAMP)]))
    groups2 = []
    for s in range(NSEQ):
        for gi in range(NGRP):
            groups2.append(Grp("prompt", 512, 64, [(i * 128, 128, [(2 * i, 0, 64), (2 * i + 1, 64, 128)]) for i in range(4)],
                               seq=s, gi=gi, last=(gi == NGRP - 1)))
    for g in groups1 + groups2:
        for l in range(L):
            for blk in range(7):
                wl_win.append((l, blk))
            if g.kind == "meta" and l == L - 1:
                continue
            for b in range(11):
                wl_fin.append((l, b))
            for dc in range(8):
                wl_fout.append((l, dc))

    dbg_out = {}

    def dump(name, a, shape):
        if not debug:
            return
        t = nc.dram_tensor("dbg_" + name, list(shape), a.ap.dtype if hasattr(a.ap, "dtype") else F32, kind="ExternalOutput")
        c = kb.chan("dbg_" + name)
        kb.dma("sp", t.ap(), a, c)
        dbg_out[name] = c

    def run_phase(groups, NM):
        with ExitStack() as es2:
            old_es = kb.es
            kb.es = es2
            kb.tag = f"_n{NM}"
            try:
                _run_phase(groups, NM)
            finally:
                kb.es = old_es
                kb.tag = ""

    def _run_phase(groups, NM):
        NT = max(len(g.tiles) for g in groups)
        hT = sb("hT", 8 * NM)
        hn = sb("hn", 8 * NM, BF16)
        XPW = max((g.N + 3 * (len(g.tiles) if g.kind == "sample" else 1)) for g in groups)
        arena = Arena(kb, "arena", max(32 * NM + 16 * XPW, 44 * NM, 25600))
        actT = arena.view(0, 22 * NM, BF16)
        xs = [arena.view(4096 * i, 1024, F32) for i in range(2)]
        sqb = [sb(f"sqb{i}", NM, BF16) for i in range(2)]
        sqh = sb("sqh", NM, BF16)
        sqh2 = sb("sqh2", NM, BF16)
        lnv = sb("lnv", NM)
        rstd = sb("rstd", NM)
        qs = [arena.view(4 * NM * j, NM, F32) for j in range(2)]
        th = [arena.view(8 * NM + 4 * NM * j, NM, F32) for j in range(2)]
        ga = [sb(f"ga{j}", NM, BF16) for j in range(2)]
        go = [sb(f"go{j}", NM, BF16) for j in range(2)]
        xp = arena.view(16 * NM, 4 * XPW, F32)
        cvo = arena.view(2048, 512, F32)
        co = arena.view(16 * NM + 16 * XPW, 4 * NM, F32)
        QT = sb("QT", 4 * NM, BF16)
        KT = sb("KT", NM, BF16)
        igr = sb("igr", NM)
        fgr = sb("fgr", NM)
        Vall = sb("Vall", NT * 640, BF16)
        kvout = sb("kvout", 256)
        T = [sb(f"T{i}", NM + 1) for i in range(6)]
        R = [sb(f"R{i}", NM + 1) for i in range(2)]
        Bm = sb("Bm", NM + 1)
        mz = sb("mz", NM + 1)
        Qh = [[sb(f"Qh{x}{j}", NM, BF16) for j in range(2)] for x in range(2)]
        Kt = [[sb(f"Kt{x}{j}", NM, BF16) for j in range(2)] for x in range(2)]
        Kh = [[sb(f"Kh{x}{j}", NM, BF16) for j in range(2)] for x in range(2)]
        dec = [[sb(f"dec{x}{j}", 16) for j in range(2)] for x in range(2)]
        smid = [sb(f"smid{j}", 16) for j in range(2)]
        em = [arena.view(16 * NM + 4 * NM * j, NM, F32) for j in range(2)]
        oall = [[arena.view(4 * NM * (2 * x + j), NM, F32) for j in range(2)] for x in range(2)]
        dall = [arena.view(24 * NM + 4 * NM * j, NM, F32) for j in range(2)]
        att_sb = [sb(f"att{i}", 256, BF16) for i in range(4)]
        khtok = [sb(f"khtok{i}", 128, BF16) for i in range(4)]
        sbf = [sb(f"sbf{i}", 128, BF16) for i in range(4)]
        s1 = [sb(f"s1{i}", 128) for i in range(4)]
        pesb = [arena.view(18432 + 1024 * i, 512, BF16) for i in range(4)]
        tdc = arena.view(16384, 512, F32)
        cst = arena.view(0, 512, F32)
        has_sample = any(g.kind == "sample" for g in groups)
        if has_sample:
            NS = NSAMP
            SHs = sb("SHs", NS * 2 * 64)
            SMs = sb("SMs", NS * 2 * 128)
            Xc = arena.view(8192, NS * 2 * 64, F32)
            Xn = sb("Xn", 64)
            nl = sb("nl", NS * 2)
            m0 = sb("m0", NS)
            Xv = sb("Xv", 512)
            Kc = sb("Kc", NS * 128)
            Vc = sb("Vc", NS * 128)
            KTc = sb("KTc", NS * 128, BF16)
            Vcb = sb("Vcb", NS * 128, BF16)
            mout = sb("mout", NS)
            cvs = sb("cvs", 4 * NS * 3)
            cso = sb("cso", 512)

        acc_rr = [0]

        acc_pool = [[0, 1, 2, 3]]

        def accbank():
            acc_rr[0] = (acc_rr[0] + 1) % len(acc_pool[0])
            return PB[acc_pool[0][acc_rr[0]]]

        WIDE = [0, 1, 2, 3, 5, 6, 7]

        rms_pending = []

        def rms_stats(g, c, flush=False):
            N = g.N
            ps = PB[4]
            if c is not None:
                s = sqb[c % 2]
                kb.act(s.a(0, [[1, N]]), hT.a(c * NM, [[1, N]], key=c), AF.Square)
                rms_pending.append((c, s))
            while rms_pending and (flush or len(rms_pending) > 1):
                c_, s_ = rms_pending.pop(0)
                kb.mm(ps.a(0, [[1, N]]), ones_b.a(), s_.a(0, [[1, N]]), start=(c_ == 0), stop=(c_ == 7))

        def rmsnorm(g, gam, goff, dst_bf=True, have_stats=False):
            N = g.N
            ps = PB[4]
            if not have_stats:
                for c in range(8):
                    rms_stats(g, c)
            rms_stats(g, None, flush=True)
            kb.act(lnv.a(0, [[1, N]]), ps.a(0, [[1, N]]), AF.Ln, bias=epsT.a(), scale=1.0 / D)
            kb.act(rstd.a(0, [[1, N]]), lnv.a(0, [[1, N]]), AF.Exp, scale=-0.5)
            for c in range(8):
                dst = hn.a(c * NM, [[1, N]], key=c) if dst_bf else hT.a(c * NM, [[1, N]], key=c)
                kb.stt(dst, hT.a(c * NM, [[1, N]], key=c), gam.a(goff + c, [[1, 1]]), rstd.a(0, [[1, N]]), ALU.mult, ALU.mult)

        def fm(W, col, M, ps, N):
            for k_ in range(8):
                kb.mm(ps.a(0, [[1, N]], 0, M), W.a(k_ * 512 + col, [[1, M]]), hn.a(k_ * NM, [[1, N]], key=k_),
                      start=(k_ == 0), stop=(k_ == 7))

        def tm(W, col, ncol, ps, t0, n):
            for k_ in range(8):
                kb.mm(ps.a(0, [[1, ncol]], 0, n), hn.a(k_ * NM + t0, [[1, n]], key=k_), W.a(k_ * 512 + col, [[1, ncol]]),
                      start=(k_ == 0), stop=(k_ == 7))

        def load_x(g):
            N = g.N
            if g.kind == "prompt":
                srcs = [(t0, n, dram_ap(x_prompt, (g.seq * SEQ + g.gi * 512 + t0) * D, [[D, n], [1, D]])) for (t0, n, _) in g.tiles]
            elif g.kind == "meta":
                srcs = [(0, 16, dram_ap(meta_tokens, 0, [[D, 16], [1, D]]))]
            else:
                srcs = [(0, N, dram_ap(x_sample, 0, [[D, N], [1, D]]))]
            for i, (t0, n, src) in enumerate(srcs):
                s = i % 2
                kb.dma("sp", xs[s].a(0, [[1, 1024]], 0, n), src, ch_x[s])
                for half in range(2):
                    ps = accbank()
                    for c4 in range(4):
                        c = half * 4 + c4
                        kb.tr(ps.a(c4 * 128, [[1, n]]), xs[s].a(c * 128, [[1, 128]], 0, n), ident_f.a(0, [[1, n]], 0, n))
                    for c4 in range(4):
                        c = half * 4 + c4
                        kb.cp(hT.a(c * NM + t0, [[1, n]], key=c), ps.a(c4 * 128, [[1, n]]), eng=("dve" if c4 % 2 else "act_cp"))

        def store_y(g):
            N = g.N
            if g.kind == "prompt":
                dsts = [(t0, n, dram_ap(y_prompt, (g.seq * SEQ + g.gi * 512 + t0) * D, [[D, n], [1, D]])) for (t0, n, _) in g.tiles]
            else:
                dsts = [(0, N, dram_ap(y_sample, 0, [[D, N], [1, D]]))]
            for i, (t0, n, dst) in enumerate(dsts):
                s = i % 2
                for half in range(2):
                    ps = accbank()
                    for c4 in range(4):
                        c = half * 4 + c4
                        kb.tr(ps.a(c4 * 128, [[1, 128]], 0, n), hT.a(c * NM + t0, [[1, n]], key=c), ident_f.a())
                    kb.cp(xs[s].a(half * 512, [[1, 512]], 0, n), ps.a(0, [[1, 512]], 0, n), eng=("dve" if half else "act_cp"))
                kb.dma("sp", dst, xs[s].a(0, [[1, 1024]], 0, n), ch_y[s])

        def stageA(g, l, S):
            N, Lc, nch = g.N, g.Lc, g.nch
            rmsnorm(g, gmix, l * 8, have_stats=(l > 0))
            W = get_win()
            for j in range(2):
                ps = accbank()
                fm(W, j * 128, 128, ps, N)
                kb.act(qs[j].a(0, [[1, N]]), ps.a(0, [[1, N]]), AF.Silu)
            for j in range(2):
                ps = accbank()
                fm(W, 256 + j * 128, 128, ps, N)
                kb.act(th[j].a(0, [[1, N]]), ps.a(0, [[1, N]]), AF.Tanh, scale=0.5)
            W = get_win()
            for j in range(2):
                ps = accbank()
                fm(W, j * 128, 128, ps, N)
                kb.act(ga[j].a(0, [[1, N]]), ps.a(0, [[1, N]]), AF.Silu)
            for j in range(2):
                ps = accbank()
                fm(W, 256 + j * 128, 128, ps, N)
                kb.act(T[0].a(0, [[1, N]]), ps.a(0, [[1, N]]), AF.Tanh, scale=0.5)
                kb.ts(go[j].a(0, [[1, N]]), T[0].a(0, [[1, N]]), 0.5, 0.5, ALU.mult, ALU.add)
            W = get_win()
            nsg = len(g.tiles) if g.kind == "sample" else 1
            Ls = N // nsg
            SW = Ls + 3
            XW = nsg * SW
            for ch in range(4):
                if g.kind == "sample":
                    pass
                else:
                    kb.cp(xp.a(ch * XW, [[1, 3]], key=ch), S["cvh"][l].a(ch * 3, [[1, 3]]))
                ps = accbank()
                fm(W, ch * 128, 128, ps, N)
                kb.cp(xp.a(ch * XW + 3, [[SW, nsg], [1, Ls]], key=ch), ps.a(0, [[Ls, nsg], [1, Ls]]), eng="act_cp")
                cw = lambda j_: cwT.a(l * 16 + j_ * 4 + ch, [[1, 1]])
                o_ = co.a(ch * NM, [[Ls, nsg], [1, Ls]], key=ch)
                kb.ts(o_, xp.a(ch * XW, [[SW, nsg], [1, Ls]], key=ch), cw(0), cbT.a(l * 4 + ch, [[1, 1]]), ALU.mult, ALU.add, eng=CONV_ENG)
                for j_ in range(1, 4):
                    kb.stt(o_, xp.a(ch * XW + j_, [[SW, nsg], [1, Ls]], key=ch), cw(j_), o_, ALU.mult, ALU.add, eng=CONV_ENG)
                kb.act(co.a(ch * NM, [[1, N]], key=ch), co.a(ch * NM, [[1, N]], key=ch), AF.Silu)
                if g.kind != "sample":
                    kb.cp(S["cvh"][l].a(ch * 3, [[1, 3]]), xp.a(ch * XW + N, [[1, 3]], key=ch))
                else:
                    kb.cp(cvs.a(ch * nsg * 3, [[3, nsg], [1, 3]]), xp.a(ch * XW + 4, [[SW, nsg], [1, 3]], key=ch))
            W = get_win()
            ps = accbank()
            fm(W, 0, 128, ps, N)
            kb.cp(KT.a(0, [[1, N]]), ps.a(0, [[1, N]]))
            ps = accbank()
            fm(W, 128, 4, ps, N)
            kb.ts(igr.a(0, [[1, N]], 0, 4), ps.a(0, [[1, N]], 0, 4), ibT.a(l, [[1, 1]], 0, 4), None, ALU.add)
            ps = accbank()
            fm(W, 132, 4, ps, N)
            kb.ts(fgr.a(0, [[1, N]], 0, 4), ps.a(0, [[1, N]], 0, 4), fbT.a(l, [[1, 1]], 0, 4), None, ALU.add)

        def stageA_tail(g, l):
            N = g.N
            W = get_win()
            for j in range(4):
                ps = accbank()
                fm(W, j * 128, 128, ps, N)
                kb.cp(QT.a(j * NM, [[1, N]], key=j), ps.a(0, [[1, N]]), eng=("dve" if j % 2 else "act_cp"))
                yield
            W = get_win()
            for ti, (t0, n, _) in enumerate(g.tiles):
                ps = accbank()
                tm(W, 0, 512, ps, t0, n)
                kb.cp(Vall.a(ti * 640, [[1, 512]], 0, n, key=ti), ps.a(0, [[1, 512]], 0, n), eng=("act_cp" if ti % 2 else "dve"))
                yield
            W = get_win()
            for ti, (t0, n, _) in enumerate(g.tiles):
                ps = accbank()
                tm(W, 0, 256, ps, t0, n)
                kb.cp(Vall.a(ti * 640 + 512, [[1, 128]], 0, n, key=ti), ps.a(128, [[1, 128]], 0, n), eng=("dve" if ti % 2 else "act_cp"))
                if g.kind == "sample":
                    kb.cp(Kc.a(ti * 128, [[1, 128]], 0, n), ps.a(0, [[1, 128]], 0, n))
                    kb.cp(Vc.a(ti * 128, [[1, 128]], 0, n), ps.a(128, [[1, 128]], 0, n))
                elif g.kind == "prompt" and g.last and ti == len(g.tiles) - 1:
                    kb.cp(kvout.a(0, [[1, 256]], 0, n), ps.a(0, [[1, 256]], 0, n))
                yield

        def prepH(g, l):
            N, Lc, nch = g.N, g.Lc, g.nch
            mi = Lc // 2 - 1
            tsig, tkk, tf, Bz, tbp, teb = T
            v = lambda t_, o=0: t_.a(o, [[1, N]])
            v3 = lambda t_, o=0: t_.a(o, [[Lc, nch], [1, Lc]])
            bc = lambda t_, o: t_.a(o, [[Lc, nch], [0, Lc]])
            cv_ = lambda t_, o: t_.a(o, [[Lc, nch]])
            for j in range(2):
                pj = lambda P_: P_.a(l * 2 + j, [[1, 1]])
                kb.ts(v(tsig), v(th[j]), 0.5, 0.5, ALU.mult, ALU.add)
                kb.ts(v(tkk), v(tsig), pj(nomlT), pj(omlT), ALU.mult, ALU.add)
                kb.ts(v(tf), v(tsig), pj(omlT), pj(lbT), ALU.mult, ALU.add)
                kb.ts(v(tf), v(tf), 1e-30, None, ALU.max)
                yield
                kb.act(v(tf), v(tf), AF.Ln)
                kb.ms(Bz.a(0, [[1, 1]]), 0.0)
                kb.scan(v(Bz, 1), v(tf), zerosT.a(0, [[1, N]]), 0.0, ALU.add, ALU.add)
                kb.tt(v3(tbp), v3(Bz, 1), bc(Bz, 1 + mi), ALU.subtract)
                yield
                kb.act(v(teb), v(tbp), AF.Exp)
                kb.act(v(tbp), v(tbp), AF.Exp, scale=-1.0)
                kb.tt(Qh[0][j].a(0, [[1, N]]), v(qs[j]), v(teb), ALU.mult)
                kb.tt(Kt[0][j].a(0, [[1, N]]), v(tkk), v(tbp), ALU.mult)
                yield
                kb.tt(Kh[0][j].a(0, [[Lc, nch], [1, Lc]]), Kt[0][j].a(0, [[Lc, nch], [1, Lc]]), bc(teb, Lc - 1), ALU.mult)
                kb.tt(tsig.a(0, [[1, nch]]), cv_(Bz, Lc), cv_(Bz, 0), ALU.subtract)
                kb.act(dec[0][j].a(0, [[1, nch]]), tsig.a(0, [[1, nch]]), AF.Exp)
                kb.tt(tsig.a(0, [[1, nch]]), cv_(Bz, 1 + mi), cv_(Bz, 0), ALU.subtract)
                kb.act(smid[j].a(0, [[1, nch]]), tsig.a(0, [[1, nch]]), AF.Exp)
                yield

        def prepM(g, l, S):
            N, Lc, nch = g.N, g.Lc, g.nch
            r4 = lambda t_, o=0: t_.a(o, [[1, N]], 0, 4)
            r3 = lambda t_, o=0: t_.a(o, [[Lc, nch], [1, Lc]], 0, 4)
            rb = lambda t_, o: t_.a(o, [[Lc, nch], [0, Lc]], 0, 4)
            rbl, rml = R
            rlf, ra2, ru = fgr, rml, rbl
            kb.act(r4(fgr), r4(fgr), AF.Exp, scale=-1.0)
            kb.act(r4(fgr), r4(fgr), AF.Ln, bias=1.0)
            kb.ts(r4(rlf), r4(fgr), -1.0, None, ALU.mult)
            kb.ms(Bm.a(0, [[1, 1]], 0, 4), 0.0)
            kb.scan(r4(Bm, 1), r4(rlf), zerosT.a(0, [[1, N]], 0, 4), 0.0, ALU.add, ALU.add)
            if g.kind == "sample":
                for b, (t0, n, _) in enumerate(g.tiles):
                    kb.scan(mz.a(1 + t0, [[1, n]], 0, 4), rlf.a(t0, [[1, n]], 0, 4), igr.a(t0, [[1, n]], 0, 4),
                            m0.a(b, [[1, 1]], 0, 4), ALU.add, ALU.max)
                mprev_b = m0.a(0, [[1, nch], [0, Lc]], 0, 4)
            else:
                kb.cp(mz.a(0, [[1, 1]], 0, 4), S["mst"][l].a(0, [[1, 1]], 0, 4))
                kb.scan(r4(mz, 1), r4(rlf), r4(igr), mz.a(0, [[1, 1]], 0, 4), ALU.add, ALU.max)
                kb.cp(S["mst"][l].a(0, [[1, 1]], 0, 4), mz.a(N, [[1, 1]], 0, 4))
                mprev_b = rb(mz, 0)
            yield
            kb.tt(r3(rbl), r3(Bm, 1), rb(Bm, 0), ALU.subtract)
            kb.tt(r3(rml), r3(mz, 1), mprev_b, ALU.subtract)
            kb.tt(r4(ra2), r4(rbl), r4(rml), ALU.subtract)
            kb.tt(r4(ru), r4(igr), r4(rbl), ALU.subtract)
            kb.tt(r3(ru), r3(ru), mprev_b, ALU.subtract)
            tea, teu = lnv, rstd
            v = lambda t_, o=0: t_.a(o, [[1, N]])
            yield
            for j in range(2):
                lhs = sel.a(j * 128, [[1, 128]], 0, 4)
                ps = accbank()
                kb.mm(ps.a(0, [[1, N]]), lhs, r4(ra2))
                kb.act(v(tea), ps.a(0, [[1, N]]), AF.Exp)
                ps = accbank()
                kb.mm(ps.a(0, [[1, N]]), lhs, r4(ru))
                kb.act(v(teu), ps.a(0, [[1, N]]), AF.Exp)
                ps = accbank()
                kb.mm(ps.a(0, [[1, N]]), lhs, r4(mz, 1))
                kb.act(v(em[j]), ps.a(0, [[1, N]]), AF.Exp, scale=-1.0)
                yield
                kb.tt(Qh[1][j].a(0, [[1, N]]), co.a(j * NM, [[1, N]], key=j), v(tea), ALU.mult)
                kb.stt(Kt[1][j].a(0, [[1, N]]), co.a((2 + j) * NM, [[1, N]], key=2 + j), 0.125, v(teu), ALU.mult, ALU.mult)
                kb.tt(Kh[1][j].a(0, [[Lc, nch], [1, Lc]]), Kt[1][j].a(0, [[Lc, nch], [1, Lc]]),
                      tea.a(Lc - 1, [[Lc, nch], [0, Lc]]), ALU.mult)
                kb.cp(dec[1][j].a(0, [[1, nch]]), tea.a(Lc - 1, [[Lc, nch]]))
                yield

        la_rr = [0]

        def la_tile(x, j, g, ti, Sst, Wd):
            t0, n, chunks = g.tiles[ti]
            q_ = 2 * x + j
            Q, K_, KH = Qh[x][j], Kt[x][j], Kh[x][j]
            vcol = (0 if x == 0 else 256) + j * 128
            asb, kt_, sb_, s1_ = att_sb[q_], khtok[q_], sbf[q_], s1[q_]
            pa = PB[q_]
            kb.mm(pa.a(0, [[1, n]], 0, n), K_.a(t0, [[1, n]], 0, 64), Q.a(t0, [[1, n]], 0, 64), sgc=True)
            kb.tr(pa.a(512, [[1, 128]], 0, n, bf=True), KH.a(t0, [[1, n]]), ident_b.a())
            yield
            kb.mm(pa.a(128, [[1, n]], 0, n), K_.a(t0, [[1, n]], 64, 64), Q.a(t0, [[1, n]], 64, 64), sgc=True)
            yield
            kb.tt(asb.a(0, [[128, 2], [1, n]], 0, n), pa.a(0, [[128, 2], [1, n]], 0, n),
                  maskA.a(0, [[0, 2], [1, n]], 0, n), ALU.mult)
            kb.cp(kt_.a(0, [[1, 128]], 0, n), pa.a(512, [[1, 128]], 0, n, bf=True), eng="act_cp")
            yield
            po = PB[4 + q_]
            for p in range(2):
                kb.mm(po.a(0, [[1, n]], p * 64, 64), Vall.a(ti * 640 + vcol + p * 64, [[1, 64]], 0, n, key=ti),
                      asb.a(p * 128, [[1, n]], 0, n), start=True, stop=False, sgc=True)
                if x == 1:
                    kb.mm(po.a(128, [[1, n]], p * 64, 64), ones_b.a(0, [[1, 64]], 0, n),
                          asb.a(p * 128, [[1, n]], 0, n), start=False, stop=False, sgc=True)
            for ci, (c, cs, ce) in enumerate(chunks):
                last = ci == len(chunks) - 1
                Sf = Sst(0, 128, 0, Wd)
                if x == 0:
                    kb.ts(sb_.a(0, [[1, Wd]]), Sf, smid[j].a(c, [[1, 1]]), None, ALU.mult)
                else:
                    kb.act(sb_.a(0, [[1, Wd]]), Sf, AF.Copy)
                pP = pa
                for p in range(2):
                    kb.mm(pP.a(0, [[1, 64]], p * 64, 64), kt_.a(p * 64, [[1, 64]], cs, ce - cs),
                          Vall.a(ti * 640 + vcol + p * 64, [[1, 64]], cs, ce - cs, key=ti), start=True, stop=(x == 0), sgc=True)
                    if x == 1:
                        kb.mm(pP.a(64, [[1, 64]], p * 64, 64), kt_.a(p * 64, [[1, 64]], cs, ce - cs),
                              ones_b.a(0, [[1, 64]], cs, ce - cs), start=False, stop=True, sgc=True)
                yield
                for p in range(2):
                    kb.mm(po.a(cs, [[1, ce - cs]], p * 64, 64), sb_.a(0, [[1, 64]], p * 64, 64),
                          Q.a(t0 + cs, [[1, ce - cs]], p * 64, 64), start=False, stop=last, sgc=True)
                    if x == 1:
                        kb.mm(po.a(128 + cs, [[1, ce - cs]], p * 64, 64), sb_.a(64, [[1, 64]], p * 64, 64),
                              Q.a(t0 + cs, [[1, ce - cs]], p * 64, 64), start=False, stop=last, sgc=True)
                    if p == 0:
                        yield
                kb.stt(Sf, Sf, dec[x][j].a(c, [[1, 1]]), pP.a(0, [[1, Wd]]), ALU.mult, ALU.add)
                yield
            kb.cp(oall[x][j].a(t0, [[1, n]]), po.a(0, [[1, n]]), eng="act_cp")
            if x == 1:
                kb.cp(dall[j].a(t0, [[1, n]]), po.a(128, [[1, n]]))

        def run_interleaved(gens):
            gens = list(gens)
            while gens:
                for g_ in list(gens):
                    try:
                        next(g_)
                    except StopIteration:
                        gens.remove(g_)

        def head_norm(g, l, src, nrm, gate, dstc, q_):
            N = g.N
            v = lambda t_: t_.a(0, [[1, N]])
            sq_ = (sqb[0], sqb[1], sqh, sqh2)[q_]
            rs_ = T[q_]
            kb.act(sq_.a(0, [[1, N]]), v(src), AF.Square)
            yield
            ps = accbank()
            kb.mm(ps.a(0, [[1, N]]), bones_b.a(), sq_.a(0, [[1, N]]))
            kb.act(v(rs_), ps.a(0, [[1, N]]), AF.Ln, bias=epsT.a(), scale=1.0 / 64)
            yield
            kb.act(v(rs_), v(rs_), AF.Exp, scale=-0.5)
            yield
            kb.stt(v(src), v(src), nrm, v(rs_), ALU.mult, ALU.mult)
            yield
            kb.tt(hn.a(dstc * NM, [[1, N]], key=dstc), v(src), gate.a(0, [[1, N]]), ALU.mult)

        def postH_gen(g, l, j):
            yield from head_norm(g, l, oall[0][j], hnormT.a(l * 2 + j, [[1, 1]]), ga[j], j, j)

        def postM_gen(g, l, j):
            N = g.N
            v = lambda t_: t_.a(0, [[1, N]])
            t4 = T[4 + j]
            kb.stt(v(t4), v(dall[j]), -1.0, v(dall[j]), ALU.mult, ALU.max)
            kb.tt(v(t4), v(t4), v(em[j]), ALU.max)
            yield
            kb.act(v(t4), v(t4), AF.Ln)
            yield
            kb.act(v(t4), v(t4), AF.Exp, scale=-1.0)
            yield
            kb.tt(v(oall[1][j]), v(oall[1][j]), v(t4), ALU.mult)
            yield
            yield from head_norm(g, l, oall[1][j], mnormT.a(l * 2 + j, [[1, 1]]), go[j], 2 + j, 2 + j)

        def post_all(g, l):
            run_interleaved([postM_gen(g, l, 0), postM_gen(g, l, 1), postH_gen(g, l, 0), postH_gen(g, l, 1)])

        def swa_norm(g, l, ti, po, pd):
            t0, n, _ = g.tiles[ti]
            kb.tt(tdc.a(0, [[128, 4], [1, n]]), pd.a(0, [[128, 4], [1, n]]), esink.a(l * 4, [[1, 4], [0, n]]), ALU.add)
            kb.act(tdc.a(0, [[128, 4], [1, n]]), tdc.a(0, [[128, 4], [1, n]]), AF.Ln)
            kb.act(tdc.a(0, [[128, 4], [1, n]]), tdc.a(0, [[128, 4], [1, n]]), AF.Exp, scale=-1.0)
            dst = hn.a(4 * NM + t0, [[NM, 4], [1, n]], key=(4, 5, 6, 7))
            kb.tt(dst, po.a(0, [[128, 4], [1, n]]), tdc.a(0, [[128, 4], [1, n]]), ALU.mult)

        def swa_tile(g, l, ti, kbs, bset=0, part="S"):
            t0, n, _ = g.tiles[ti]
            po, pd = PB[4 + 2 * bset], PB[5 + 2 * bset]
            if part == "N":
                return swa_norm(g, l, ti, po, pd)
            for bi, (Kf, Vf, nk, mk) in enumerate(kbs):
                for p in range(2):
                    sl = bi * 2 + p
                    ps = accbank()
                    for j in range(4):
                        kb.mm(ps.a(j * 128, [[1, n]], 0, nk), Kf(p), QT.a(j * NM + t0, [[1, n]], p * 64, 64, key=j), sgc=True)
                    kb.act(pesb[sl].a(0, [[128, 4], [1, n]], 0, nk), ps.a(0, [[128, 4], [1, n]], 0, nk), AF.Exp, scale=0.125)
                    kb.tt(pesb[sl].a(0, [[128, 4], [1, n]], 0, nk), pesb[sl].a(0, [[128, 4], [1, n]], 0, nk), mk(nk, n), ALU.mult)
            nb = len(kbs)
            for j in range(4):
                for p in range(2):
                    for bi, (Kf, Vf, nk, mk) in enumerate(kbs):
                        sl = bi * 2 + p
                        kb.mm(po.a(j * 128, [[1, n]], p * 64, 64), Vf(p), pesb[sl].a(j * 128, [[1, n]], 0, nk),
                              start=(j == 0 and bi == 0), stop=(j == 3 and bi == nb - 1), sgc=True)
            for j in range(4):
                for p in range(2):
                    for bi, (Kf, Vf, nk, mk) in enumerate(kbs):
                        sl = bi * 2 + p
                        kb.mm(pd.a(j * 128, [[1, n]], p * 64, 64), ones_b.a(0, [[1, 64]], 0, nk),
                              pesb[sl].a(j * 128, [[1, n]], 0, nk), start=(j == 0 and bi == 0), stop=(j == 3 and bi == nb - 1), sgc=True)

        def wout_stage(g, l):
            N = g.N
            for dc in range(8):
                ps = accbank()
                for k_ in range(8):
                    rhs = hn.a(k_ * NM, [[1, N]], key=k_)
                    kb.mm(ps.a(0, [[1, N]]), WOUT.a(k_ * 1024 + dc * 128, [[1, 128]]), rhs, start=(k_ == 0), stop=(k_ == 7))
                kb.tt(hT.a(dc * NM, [[1, N]], key=dc), hT.a(dc * NM, [[1, N]], key=dc), ps.a(0, [[1, N]]), ALU.add)
                rms_stats(g, dc)

        def ffn_stage(g, l):
            N = g.N
            rmsnorm(g, gffn, l * 8, have_stats=True)
            for b in range(11):
                W = get_fin()
                for cc in range(2):
                    c = 2 * b + cc
                    bp = ((0, 1), (2, 3), (5, 6))[c % 3]
                    pg, pu = PB[bp[0]], PB[bp[1]]
                    for k_ in range(8):
                        kb.mm(pg.a(0, [[1, N]]), W.a(k_ * 256 + cc * 128, [[1, 128]]), hn.a(k_ * NM, [[1, N]], key=k_),
                              start=(k_ == 0), stop=(k_ == 7))
                    for k_ in range(8):
                        kb.mm(pu.a(0, [[1, N]]), W.a(2048 + k_ * 256 + cc * 128, [[1, 128]]), hn.a(k_ * NM, [[1, N]], key=k_),
                              start=(k_ == 0), stop=(k_ == 7))
                    sg = (lnv, rstd)[cc]
                    kb.act(sg.a(0, [[1, N]]), pg.a(0, [[1, N]]), AF.Silu)
                    kb.tt(actT.a(c * NM, [[1, N]], key=c), sg.a(0, [[1, N]]), pu.a(0, [[1, N]]), ALU.mult)
            for dc in range(8):
                W = get_fout()
                ps = PB[WIDE[dc % 7]]
                for c in range(22):
                    kb.mm(ps.a(0, [[1, N]]), W.a(c * 128, [[1, 128]]), actT.a(c * NM, [[1, N]], key=c), start=(c == 0), stop=(c == 21))
                kb.tt(hT.a(dc * NM, [[1, N]], key=dc), hT.a(dc * NM, [[1, N]], key=dc), ps.a(0, [[1, N]]), ALU.add)
                rms_stats(g, dc)

        def seq_init():
            for l in range(L):
                for j in range(2):
                    kb.cp(STW["SH"][l][j].a(), STM["SH"][l][j].a())
                    kb.cp(STW["SM"][l][j].a(), STM["SM"][l][j].a(), eng="act_cp")
                kb.cp(STW["mst"][l].a(0, [[1, 1]], 0, 4), STM["mst"][l].a(0, [[1, 1]], 0, 4))
                kb.cp(STW["cvh"][l].a(), STM["cvh"][l].a())

        def load_sample_states(g, l):
            q = "sp"
            NS = NSAMP
            kb.batch_begin(q, ch_ld)
            with nc.allow_non_contiguous_dma(reason="state layouts"):
                for p in range(2):
                    kb.batch_add(kb.dma(q, SHs.a(0, [[64, NS * 2], [1, 64]], p * 64, 64, key=tuple(range(NS * 2))),
                                        dram_ap(st_hgrn, l * NS * 16384 + p * 4096, [[64, 64], [8192, NS * 2], [1, 64]]), ch_ld))
                    kb.batch_add(kb.dma(q, Xc.a(0, [[64, NS * 2], [1, 64]], p * 64, 64),
                                        dram_ap(st_c, l * NS * 16384 + p * 4096, [[64, 64], [8192, NS * 2], [1, 64]]), ch_ld))
                kb.batch_add(kb.dma(q, Xn.a(0, [[1, 64]], 0, NS * 4), dram_ap(st_n, l * NS * 256, [[64, NS * 4], [1, 64]]), ch_ld))
                kb.batch_add(kb.dma(q, m0.a(0, [[1, NS]], 0, 4), dram_ap(st_m, l * NS * 4, [[1, 4], [4, NS]]), ch_ld))
                kb.batch_add(kb.dma(q, Xv.a(0, [[1, 512]], 0, NS * 3), dram_ap(st_conv, l * NS * 1536, [[512, NS * 3], [1, 512]]), ch_ld))
                kb.batch_add(kb.dma(q, Kc.a(0, [[128, NS], [1, 128]]), dram_ap(c_k, l * NS * 16384, [[128, 128], [16384, NS], [1, 128]]), ch_ld))
                kb.batch_add(kb.dma(q, Vc.a(0, [[128, NS], [1, 128]]), dram_ap(c_v, l * NS * 16384, [[128, 128], [16384, NS], [1, 128]]), ch_ld))
            kb.batch_end(ch_ld)
            for grp8 in range(NS * 2 // 8):
                ps = accbank()
                for p in range(2):
                    for q8 in range(8):
                        bj = grp8 * 8 + q8
                        kb.trx(ps.a(q8 * 64, [[1, 64]], p * 64, 64), Xc.a(bj * 64, [[1, 64]], p * 64, 64),
                               ident_f.a(p * 64, [[1, 64]], p * 64, 64), p)
                kb.cp(SMs.a(grp8 * 8 * 128, [[128, 8], [1, 64]], key=tuple(range(grp8 * 8, grp8 * 8 + 8))), ps.a(0, [[64, 8], [1, 64]]))
            ps = accbank()
            for p in range(2):
                kb.trx(ps.a(0, [[1, NS * 4]], p * 64, 64), Xn.a(0, [[1, 64]], 0, NS * 4), ident_f.a(0, [[1, NS * 4]], 0, NS * 4), p)
                kb.cp(nl.a(0, [[1, NS * 2]], p * 64, 64), ps.a(p, [[2, NS * 2]], p * 64, 64))
            kb.cp(SMs.a(64, [[128, NS * 2], [1, 64]], key=tuple(range(NS * 2))), nl.a(0, [[1, NS * 2], [0, 64]]))
            ps = accbank()
            for ch in range(4):
                kb.tr(ps.a(ch * 128, [[1, NS * 3]]), Xv.a(ch * 128, [[1, 128]], 0, NS * 3), ident_f.a(0, [[1, NS * 3]], 0, NS * 3))
            for ch in range(4):
                kb.cp(xp.a(ch * NS * 7, [[7, NS], [1, 3]], key=ch), ps.a(ch * 128, [[3, NS], [1, 3]]))
            for b in range(NS):
                ps = accbank()
                kb.tr(ps.a(0, [[1, 128]]), Kc.a(b * 128, [[1, 128]]), ident_f.a())
                kb.cp(KTc.a(b * 128, [[1, 128]]), ps.a(0, [[1, 128]]), eng=("act_cp" if b % 2 else "dve"))
            kb.cp(Vcb.a(), Vc.a())

        def store_sample_states(g, l):
            q = "sp"
            NS = NSAMP
            for grp8 in range(NS * 2 // 8):
                ps = accbank()
                for p in range(2):
                    for q8 in range(8):
                        bj = grp8 * 8 + q8
                        kb.trx(ps.a(q8 * 64, [[1, 64]], p * 64, 64), SMs.a(bj * 128, [[1, 64]], p * 64, 64, key=bj),
                               ident_f.a(p * 64, [[1, 64]], p * 64, 64), p)
                kb.cp(Xc.a(grp8 * 8 * 64, [[1, 512]]), ps.a(0, [[1, 512]]))
            kb.cp(nl.a(0, [[1, NS * 2]]), SMs.a(64, [[128, NS * 2]], key=tuple(range(NS * 2))))
            kb.cp(mout.a(0, [[1, NS]], 0, 4), mz.a(4, [[4, NS]], 0, 4))
            ps = accbank()
            for ch in range(4):
                kb.tr(ps.a(ch * 128, [[1, 128]], 0, NS * 3), cvs.a(ch * NS * 3, [[1, NS * 3]]), ident_f.a())
            kb.cp(cso.a(0, [[1, 512]], 0, NS * 3), ps.a(0, [[1, 512]], 0, NS * 3))
            kb.batch_begin(q, ch_st)
            with nc.allow_non_contiguous_dma(reason="state layouts"):
                for p in range(2):
                    kb.batch_add(kb.dma(q, dram_ap(o_hgrn_s, l * NS * 16384 + p * 4096, [[64, 64], [8192, NS * 2], [1, 64]]),
                                        SHs.a(0, [[64, NS * 2], [1, 64]], p * 64, 64, key=tuple(range(NS * 2))), ch_st))
                    kb.batch_add(kb.dma(q, dram_ap(o_c_s, l * NS * 16384 + p * 4096, [[64, 64], [8192, NS * 2], [1, 64]]),
                                        Xc.a(0, [[64, NS * 2], [1, 64]], p * 64, 64), ch_st))
                    kb.batch_add(kb.dma(q, dram_ap(o_n_s, l * NS * 256 + p * 64, [[1, 64], [128, NS * 2]]),
                                        nl.a(0, [[1, NS * 2]], p * 64, 64), ch_st))
                kb.batch_add(kb.dma(q, dram_ap(o_m_s, l * NS * 4, [[1, 4], [4, NS]]), mout.a(0, [[1, NS]], 0, 4), ch_st))
                kb.batch_add(kb.dma(q, dram_ap(o_conv_s, l * NS * 1536, [[512, NS * 3], [1, 512]]), cso.a(0, [[1, 512]], 0, NS * 3), ch_st))
                for b in range(NS):
                    kb.batch_add(kb.dma(q, dram_ap(o_k_s, (l * NS + b) * 16384 + 124 * 128, [[128, 4], [1, 128]]),
                                        Kc.a(b * 128, [[1, 128]], 0, 4), ch_st))
                    kb.batch_add(kb.dma(q, dram_ap(o_v_s, (l * NS + b) * 16384 + 124 * 128, [[128, 4], [1, 128]]),
                                        Vc.a(b * 128, [[1, 128]], 0, 4), ch_st))
            kb.batch_end(ch_st)
            kb.eng[q].dma_start(out=dram_ap(o_k_s, l * NS * 16384, [[16384, NS], [1, 124 * 128]]),
                                in_=dram_ap(c_k, l * NS * 16384 + 4 * 128, [[16384, NS], [1, 124 * 128]])).then_inc(ch_dd.sem, 16)
            ch_dd.val += 16
            kb.eng[q].dma_start(out=dram_ap(o_v_s, l * NS * 16384, [[16384, NS], [1, 124 * 128]]),
                                in_=dram_ap(c_v, l * NS * 16384 + 4 * 128, [[16384, NS], [1, 124 * 128]])).then_inc(ch_dd.sem, 16)
            ch_dd.val += 16

        def store_prompt_states(g, l):
            q = "sp"
            N = g.N
            s = g.seq
            S = STW
            ps = accbank()
            for j in range(2):
                for p in range(2):
                    kb.trx(ps.a(j * 64, [[1, 64]], p * 64, 64), S["SM"][l][j].a(0, [[1, 64]], p * 64, 64),
                           ident_f.a(p * 64, [[1, 64]], p * 64, 64), p)
            kb.cp(cst.a(0, [[1, 128]]), ps.a(0, [[1, 128]]))
            ps2 = accbank()
            for ch in range(4):
                kb.tr(ps2.a(ch * 128, [[1, 128]], 0, 3), S["cvh"][l].a(ch * 3, [[1, 3]]), ident_f.a())
            kb.cp(cvo.a(0, [[1, 512]], 0, 3), ps2.a(0, [[1, 512]], 0, 3))
            kb.batch_begin(q, ch_st)
            with nc.allow_non_contiguous_dma(reason="state layouts"):
                for j in range(2):
                    for p in range(2):
                        h = 2 * j + p
                        base = ((l * NSEQ + s) * 4 + h)
                        kb.batch_add(kb.dma(q, dram_ap(o_hgrn_p, base * 4096, [[64, 64], [1, 64]]),
                                            S["SH"][l][j].a(0, [[1, 64]], p * 64, 64), ch_st))
                        kb.batch_add(kb.dma(q, dram_ap(o_c_p, base * 4096, [[64, 64], [1, 64]]),
                                            cst.a(j * 64, [[1, 64]], p * 64, 64), ch_st))
                        kb.batch_add(kb.dma(q, dram_ap(o_n_p, base * 64, [[1, 64], [1, 1]]),
                                            S["SM"][l][j].a(64, [[1, 1]], p * 64, 64), ch_st))
                kb.batch_add(kb.dma(q, dram_ap(o_m_p, (l * NSEQ + s) * 4, [[1, 4], [1, 1]]), S["mst"][l].a(0, [[1, 1]], 0, 4), ch_st))
                kb.batch_add(kb.dma(q, dram_ap(o_conv_p, (l * NSEQ + s) * 1536, [[512, 3], [1, 512]]),
                                    cvo.a(0, [[1, 512]], 0, 3), ch_st))
                kb.batch_add(kb.dma(q, dram_ap(o_k_p, (l * NSEQ + s) * 16384, [[128, 128], [1, 128]]), kvout.a(0, [[1, 128]]), ch_st))
                kb.batch_add(kb.dma(q, dram_ap(o_v_p, (l * NSEQ + s) * 16384, [[128, 128], [1, 128]]), kvout.a(128, [[1, 128]]), ch_st))
            kb.batch_end(ch_st)

        def run_group(g):
            N = g.N
            S = STM if g.kind == "meta" else STW
            if g.kind == "prompt" and g.gi == 0:
                seq_init()
            gl = f"{g.kind[0]}{g.seq}{g.gi}"
            kb.lab = gl + ":loadx"
            load_x(g)
            for l in range(L):
                if g.kind == "sample":
                    kb.lab = gl + f":L{l}:ldst"
                    load_sample_states(g, l)
                kb.lab = gl + f":L{l}:A"
                acc_pool[0] = WIDE
                stageA(g, l, S)
                dead_tail = (g.kind == "meta" and l == L - 1)
                if not dead_tail:
                    emit_wout(l)
                kb.lab = gl + f":L{l}:prep"
                run_interleaved([stageA_tail(g, l), prepH(g, l), prepM(g, l, S)])
                kb.lab = gl + f":L{l}:tiles"
                acc_pool[0] = [0, 1, 2, 3]
                acc_rr[0] = 0
                def chain_gen(x, j):
                    for ti in range(len(g.tiles)):
                        if g.kind == "sample":
                            if x == 0:
                                st = (lambda b_, j_: (lambda p0, pn, off, w: SHs.a((b_ * 2 + j_) * 64 + off, [[1, w]], p0, pn, key=b_ * 2 + j_)))(ti, j)
                            else:
                                st = (lambda b_, j_: (lambda p0, pn, off, w: SMs.a((b_ * 2 + j_) * 128 + off, [[1, w]], p0, pn, key=b_ * 2 + j_)))(ti, j)
                        else:
                            if x == 0:
                                st = (lambda j_: (lambda p0, pn, off, w: S["SH"][l][j_].a(off, [[1, w]], p0, pn)))(j)
                            else:
                                st = (lambda j_: (lambda p0, pn, off, w: S["SM"][l][j_].a(off, [[1, w]], p0, pn)))(j)
                        yield from la_tile(x, j, g, ti, st, 64 if x == 0 else 128)
                        yield

                run_interleaved([chain_gen(1, 0), chain_gen(1, 1), chain_gen(0, 0), chain_gen(0, 1)])
                for ti, (t0, n, chunks) in enumerate([] if dead_tail else g.tiles):
                    cur = (lambda t0_, n_, ti_: ((lambda p: KT.a(t0_, [[1, n_]], p * 64, 64)),
                                                 (lambda p: Vall.a(ti_ * 640 + 512 + p * 64, [[1, 64]], 0, n_, key=ti_)),
                                                 n_, (lambda nk, nq: maskC.a(0, [[0, 4], [1, nq]], 0, nk))))(t0, n, ti)
                    if g.kind == "meta":
                        kbs = [cur]
                    elif g.kind == "sample":
                        b = ti
                        prev = ((lambda p, b=b: KTc.a(b * 128, [[1, 128]], p * 64, 64)),
                                (lambda p, b=b: Vcb.a(b * 128 + p * 64, [[1, 64]])),
                                128, (lambda nk, nq: maskP.a(0, [[0, 4], [1, nq]], 0, nk)))
                        kbs = [prev, cur]
                    else:
                        if ti > 0:
                            pt0 = g.tiles[ti - 1][0]
                            prev = ((lambda p, pt0=pt0: KT.a(pt0, [[1, 128]], p * 64, 64)),
                                    (lambda p, ti=ti: Vall.a((ti - 1) * 640 + 512 + p * 64, [[1, 64]], key=ti - 1)),
                                    128, (lambda nk, nq: maskP.a(0, [[0, 4], [1, nq]], 0, nk)))
                        elif g.gi > 0:
                            prev = ((lambda p: STW["sK"][l].a(0, [[1, 128]], p * 64, 64)),
                                    (lambda p: STW["sV"][l].a(p * 64, [[1, 64]])),
                                    128, (lambda nk, nq: maskP.a(0, [[0, 4], [1, nq]], 0, nk)))
                        else:
                            prev = ((lambda p: STM["sK"][l].a(0, [[1, 16]], p * 64, 64)),
                                    (lambda p: STM["sV"][l].a(p * 64, [[1, 64]], 0, 16)),
                                    16, (lambda nk, nq: maskM.a(0, [[0, 4], [1, nq]], 0, nk)))
                        kbs = [prev, cur]
                    swa_tile(g, l, ti, kbs, bset=ti % 2, part="S")
                    if ti > 0:
                        swa_tile(g, l, ti - 1, None, bset=(ti - 1) % 2, part="N")
                if not dead_tail:
                    swa_tile(g, l, len(g.tiles) - 1, None, bset=(len(g.tiles) - 1) % 2, part="N")
                if g.kind == "meta":
                    kb.cp(STM["sK"][l].a(0, [[1, 16]]), KT.a(0, [[1, 16]]))
                    kb.cp(STM["sV"][l].a(0, [[1, 128]], 0, 16), Vall.a(512, [[1, 128]], 0, 16, key=0))
                elif g.kind == "prompt" and not g.last:
                    lt = len(g.tiles) - 1
                    kb.cp(STW["sK"][l].a(), KT.a(g.tiles[lt][0], [[1, 128]]))
                    kb.cp(STW["sV"][l].a(), Vall.a(lt * 640 + 512, [[1, 128]], key=lt))
                kb.lab = gl + f":L{l}:post"
                if not dead_tail:
                    post_all(g, l)
                if g.kind == "sample":
                    store_sample_states(g, l)
                if g.kind == "prompt" and g.last:
                    store_prompt_states(g, l)
                if dead_tail:
                    continue
                kb.lab = gl + f":L{l}:wout"
                wout_stage(g, l)
                kb.lab = gl + f":L{l}:ffn"
                ffn_stage(g, l)
            kb.lab = gl + ":final"
            if g.kind == "meta":
                rms_stats(g, None, flush=True)
            if g.kind != "meta":
                rmsnorm(g, gfin, 0, dst_bf=False, have_stats=True)
                store_y(g)

        for g in groups:
            run_group(g)

    _cp = kb.cp

    def cp2(out, in_, eng="dve"):
        if eng == "act_cp":
            return kb.act(out, in_, AF.Copy)
        return _cp(out, in_, eng)

    kb.cp = cp2

    build_consts()
    load_params()
    run_phase(groups1, 64 if NSAMP else 16)
    kb.barrier()
    if groups2:
        run_phase(groups2, 512)
    for c in kb.chans:
        if c.val:
            kb._wait("sp", c, c.val)
    kb.es.close()
    return kb


OUT_NAMES = ["y_prompt", "y_sample", "hgrn_p", "mlstm_c_p", "mlstm_n_p", "mlstm_m_p", "mlstm_conv_p", "swa_k_p", "swa_v_p",
             "hgrn_s", "mlstm_c_s", "mlstm_n_s", "mlstm_m_s", "mlstm_conv_s", "swa_k_s", "swa_v_s"]
PROMPT_OUT = set(OUT_NAMES[:1] + OUT_NAMES[2:9])
BATCH_AXIS = {"y_prompt": 0, "y_sample": 0}
SAMPLE_IN = ["state_hgrn", "state_mlstm_c", "state_mlstm_n", "state_mlstm_m", "state_mlstm_conv", "cache_swa_k", "cache_swa_v"]


def make_in_maps(inputs, ncores, nseq, nsamp):
    maps = []
    for c in range(ncores):
        m = {}
        for k_, v in inputs.items():
            v = np.asarray(v)
            if k_ == "x_prompt":
                m[k_] = np.ascontiguousarray(v[c * nseq:(c + 1) * nseq])
            elif k_ == "x_sample":
                m[k_] = np.ascontiguousarray(v[c * nsamp:(c + 1) * nsamp])
            elif k_ in SAMPLE_IN:
                m[k_] = np.ascontiguousarray(v[:, c * nsamp:(c + 1) * nsamp])
            else:
                m[k_] = np.ascontiguousarray(v)
        maps.append(m)
    return maps


def run(inputs, ncores=NCORES, NSEQ=2, NGRP=4, NSAMP=16, debug=False, trace=False):
    k1 = build(None, NSEQ, NGRP, NSAMP, debug)
    needed = k1.rec
    k2 = build(needed, NSEQ, NGRP, NSAMP, debug)
    in_maps = make_in_maps(inputs, ncores, NSEQ, NSAMP)
    res = run_bass_kernel_spmd(k2.nc, in_maps, core_ids=list(range(ncores)), trace=trace)
    outs = []
    for name in OUT_NAMES:
        ax = 0 if name in BATCH_AXIS else 1
        outs.append(np.concatenate([np.asarray(r[name]) for r in res.results], axis=ax).astype(np.float32))
    return tuple(outs), res


def kernel(**inputs):
    outs, _ = run(inputs)
    return outs
```
